# Optimizing a Trainium2 kernel written in Bass

```python
import math
import jax, jax.numpy as jnp
from jax import lax
import numpy as np

D_MODEL = 1024
BATCH = 8
SEQ = 2048
DEPTH = 1

CHUNK = 64
Q_BLOCK = 128
EPS = 1e-6
ATTN_HEADS = 16
HEAD_DIM = 64
Q_RANK = 256
KV_RANK = 128
IDX_HEADS = 16
IDX_DIM = 64
TOPK_MAX = 256
NUM_BUCKETS = 32
MAX_DISTANCE = 128
D_INNER = 2 * D_MODEL
SSD_HEADDIM = 64
SSD_HEADS = D_INNER // SSD_HEADDIM
SSD_GROUPS = 8
D_STATE = 128
CONV_W = 4
CONV_DIM = D_INNER + 2 * SSD_GROUPS * D_STATE
D_FF = -(-8 * D_MODEL // (3 * 256)) * 256
IN_SIZES = [Q_RANK, KV_RANK, IDX_DIM, IDX_HEADS, D_INNER, CONV_DIM, SSD_HEADS, D_MODEL, D_MODEL]
IN_COLS = int(sum(IN_SIZES))
IN_SPLITS = [int(v) for v in np.cumsum(IN_SIZES)[:-1]]

kernel_name = "hybrid_dsa_ssd_gated_block"


def rmsnorm(x, w):
    x32 = x.astype(jnp.float32)
    y = x32 * lax.rsqrt(jnp.mean(x32 * x32, axis=-1, keepdims=True) + EPS)
    return (y * w.astype(jnp.float32)).astype(x.dtype)


def t5_bucket(rel):
    half = NUM_BUCKETS // 2
    max_exact = half // 2
    side = jnp.where(rel > 0, half, 0)
    n = jnp.abs(rel)
    large = max_exact + (jnp.log(jnp.maximum(n, max_exact).astype(jnp.float32) / max_exact)
                         / math.log(MAX_DISTANCE / max_exact) * (half - max_exact)).astype(jnp.int32)
    large = jnp.minimum(large, half - 1)
    return side + jnp.where(n < max_exact, n, large)


def sparse_mla_attention(q_lat, kv_lat, k_idx_raw, w_idx_raw, positions,
                         q_norm, kv_norm, w_uq, w_uv, w_qidx, kidx_norm, rel_bias):
    Bsz, L, _ = q_lat.shape
    nb = L // Q_BLOCK
    k_sel_n = min(TOPK_MAX, L // 4)
    qn = rmsnorm(q_lat, q_norm)
    q = (qn @ w_uq).reshape(Bsz, L, ATTN_HEADS, KV_RANK)
    q_idx = (qn @ w_qidx).reshape(Bsz, L, IDX_HEADS, IDX_DIM)
    kv = rmsnorm(kv_lat, kv_norm)
    k_idx = rmsnorm(k_idx_raw, kidx_norm)
    w_idx = w_idx_raw * (IDX_HEADS ** -0.5 * IDX_DIM ** -0.5)
    key_chunk = positions // CHUNK
    scale = KV_RANK ** -0.5

    def to_blocks(a):
        return jnp.moveaxis(a.reshape((Bsz, nb, Q_BLOCK) + a.shape[2:]), 1, 0)

    def block_fn(args):
        q_b, qi_b, w_b, pos_b = args
        q_chunk = pos_b // CHUNK
        rel = jax.nn.relu(jnp.einsum('bqhd,bsd->bqhs', qi_b, k_idx).astype(jnp.float32))
        iscore = jnp.einsum('bqh,bqhs->bqs', w_b.astype(jnp.float32), rel)
        admissible = key_chunk[:, None, :] <= q_chunk[:, :, None]
        iscore = jnp.where(admissible, iscore, -jnp.inf)
        _, idx = lax.top_k(iscore, k_sel_n)
        kv_sel = jax.vmap(lambda kv_b, i_b: kv_b[i_b])(kv, idx)
        pos_sel = jax.vmap(lambda p_b, i_b: p_b[i_b])(positions, idx)
        valid = (pos_sel // CHUNK) <= q_chunk[..., None]
        bias = rel_bias[t5_bucket(pos_sel - pos_b[..., None])]
        logits = jnp.einsum('bqhc,bqkc->bhqk', q_b, kv_sel).astype(jnp.float32) * scale
        logits = logits + jnp.transpose(bias, (0, 3, 1, 2)).astype(jnp.float32)
        logits = jnp.where(valid[:, None], logits, -jnp.inf)
        probs = jax.nn.softmax(logits, axis=-1).astype(kv_sel.dtype)
        return jnp.einsum('bhqk,bqkc->bqhc', probs, kv_sel)

    o = lax.map(block_fn, (to_blocks(q), to_blocks(q_idx), to_blocks(w_idx), to_blocks(positions)))
    o = jnp.moveaxis(o, 0, 1).reshape(Bsz, L, ATTN_HEADS, KV_RANK)
    o = jnp.einsum('blhc,hcd->blhd', o, w_uv)
    return o.reshape(Bsz, L, ATTN_HEADS * HEAD_DIM)


def ssd_mixer(z, xbc, dt_raw, conv_w, conv_b, dt_bias, a_log, d_skip, ssd_norm):
    Bsz, L, _ = xbc.shape
    nc = L // CHUNK
    R = SSD_HEADS // SSD_GROUPS
    xbc = lax.conv_general_dilated(xbc, conv_w[:, None, :], window_strides=(1,),
                                   padding=[(CONV_W - 1, 0)],
                                   dimension_numbers=('NWC', 'WIO', 'NWC'),
                                   feature_group_count=CONV_DIM) + conv_b
    xbc = jax.nn.silu(xbc)
    xs, Bm, Cm = jnp.split(xbc, [D_INNER, D_INNER + SSD_GROUPS * D_STATE], axis=-1)
    dt = jax.nn.softplus((dt_raw + dt_bias).astype(jnp.float32))
    A = -jnp.exp(a_log.astype(jnp.float32))
    xh = xs.reshape(Bsz, L, SSD_HEADS, SSD_HEADDIM)
    X = (xh * dt[..., None]).reshape(Bsz, nc, CHUNK, SSD_GROUPS, R, SSD_HEADDIM)
    Bc = Bm.reshape(Bsz, nc, CHUNK, SSD_GROUPS, D_STATE)
    Cc = Cm.reshape(Bsz, nc, CHUNK, SSD_GROUPS, D_STATE)
    dA = (dt * A).reshape(Bsz, nc, CHUNK, SSD_GROUPS, R).transpose(0, 1, 3, 4, 2)
    a_cum = jnp.cumsum(dA, axis=-1)
    causal = jnp.tril(jnp.ones((CHUNK, CHUNK), dtype=bool))
    seg = jnp.where(causal, a_cum[..., :, None] - a_cum[..., None, :], -jnp.inf)
    Lmat = jnp.exp(seg)
    CB = jnp.einsum('bclgn,bcsgn->bcgls', Cc, Bc)
    y_diag = jnp.einsum('bcgls,bcgrls,bcsgrp->bclgrp', CB, Lmat, X)
    decay = jnp.exp(a_cum[..., -1:] - a_cum)
    states = jnp.einsum('bclgn,bcgrl,bclgrp->bcgrpn', Bc, decay, X)
    chunk_decay = jnp.exp(a_cum[..., -1])

    def step(h, inp):
        s_c, d_c = inp
        return h * d_c[..., None, None] + s_c, h

    h0 = jnp.zeros_like(states[:, 0])
    _, prev = lax.scan(step, h0, (jnp.moveaxis(states, 1, 0), jnp.moveaxis(chunk_decay, 1, 0)))
    prev = jnp.moveaxis(prev, 0, 1)
    y_off = jnp.einsum('bclgn,bcgrpn,bcgrl->bclgrp', Cc, prev, jnp.exp(a_cum))
    y = (y_diag + y_off).reshape(Bsz, L, SSD_HEADS, SSD_HEADDIM) + d_skip[:, None] * xh
    y = y.reshape(Bsz, L, D_INNER)
    return rmsnorm(y * jax.nn.silu(z), ssd_norm)


def setup_inputs(seed: int = 0) -> dict:
    key = jax.random.key(seed)
    ks = jax.random.split(key, 32)
    nrm = lambda k, shape, s: jax.random.normal(k, shape, jnp.float32) * s
    gain = lambda k, n: 1.0 + 0.05 * jax.random.normal(k, (n,), jnp.float32)
    offset = jax.random.randint(ks[2], (BATCH, 1), 0, 16) * CHUNK
    positions = (offset + jnp.arange(SEQ, dtype=jnp.int32)[None, :]).astype(jnp.int32)
    dt0 = jnp.exp(jax.random.uniform(ks[20], (SSD_HEADS,), jnp.float32, math.log(1e-3), math.log(1e-1)))
    return {
        "x": nrm(ks[0], (BATCH, SEQ, D_MODEL), 1.0),
        "c": nrm(ks[1], (BATCH, D_MODEL), 1.0),
        "positions": positions,
        "ada_w": nrm(ks[3], (D_MODEL, 6 * D_MODEL), 0.5 * D_MODEL ** -0.5),
        "ada_b": nrm(ks[4], (6 * D_MODEL,), 0.02),
        "pre_norm_mix": gain(ks[5], D_MODEL),
        "post_norm_mix": gain(ks[6], D_MODEL),
        "pre_norm_ffn": gain(ks[7], D_MODEL),
        "post_norm_ffn": gain(ks[8], D_MODEL),
        "w_in": nrm(ks[9], (D_MODEL, IN_COLS), D_MODEL ** -0.5),
        "q_norm": gain(ks[10], Q_RANK),
        "kv_norm": gain(ks[11], KV_RANK),
        "w_uq": nrm(ks[12], (Q_RANK, ATTN_HEADS * KV_RANK), Q_RANK ** -0.5),
        "w_uv": nrm(ks[13], (ATTN_HEADS, KV_RANK, HEAD_DIM), KV_RANK ** -0.5),
        "rel_bias": nrm(ks[14], (NUM_BUCKETS, ATTN_HEADS), 0.3),
        "w_qidx": nrm(ks[15], (Q_RANK, IDX_HEADS * IDX_DIM), Q_RANK ** -0.5),
        "kidx_norm": gain(ks[16], IDX_DIM),
        "conv_w": nrm(ks[17], (CONV_W, CONV_DIM), CONV_W ** -0.5),
        "conv_b": nrm(ks[18], (CONV_DIM,), 0.02),
        "dt_bias": dt0 + jnp.log(-jnp.expm1(-dt0)),
        "a_log": jnp.log(jax.random.uniform(ks[21], (SSD_HEADS,), jnp.float32, 1.0, 16.0)),
        "d_skip": 1.0 + 0.1 * jax.random.normal(ks[22], (SSD_HEADS,), jnp.float32),
        "ssd_norm": gain(ks[23], D_INNER),
        "w_o_attn": nrm(ks[24], (ATTN_HEADS * HEAD_DIM, D_MODEL), (ATTN_HEADS * HEAD_DIM) ** -0.5),
        "w_o_ssd": nrm(ks[25], (D_INNER, D_MODEL), D_INNER ** -0.5),
        "w_out": nrm(ks[26], (D_MODEL, D_MODEL), D_MODEL ** -0.5),
        "w_ffn_in": nrm(ks[27], (D_MODEL, 2 * D_FF), D_MODEL ** -0.5),
        "w_ffn_out": nrm(ks[28], (D_FF, D_MODEL), D_FF ** -0.5),
    }


def reference(x, c, positions, ada_w, ada_b, pre_norm_mix, post_norm_mix, pre_norm_ffn,
              post_norm_ffn, w_in, q_norm, kv_norm, w_uq, w_uv, rel_bias, w_qidx, kidx_norm,
              conv_w, conv_b, dt_bias, a_log, d_skip, ssd_norm, w_o_attn, w_o_ssd, w_out,
              w_ffn_in, w_ffn_out):
    mod = (jax.nn.silu(c) @ ada_w + ada_b)[:, None, :]
    shift_m, scale_m, gate_m, shift_f, scale_f, gate_f = jnp.split(mod, 6, axis=-1)
    for _ in range(DEPTH):
        h = rmsnorm(x, pre_norm_mix) * (1.0 + scale_m) + shift_m
        proj = h @ w_in
        q_lat, kv_lat, k_idx, w_idx, z, xbc, dt_raw, g_a, g_b = jnp.split(proj, IN_SPLITS, axis=-1)
        y_a = sparse_mla_attention(q_lat, kv_lat, k_idx, w_idx, positions, q_norm, kv_norm,
                                   w_uq, w_uv, w_qidx, kidx_norm, rel_bias) @ w_o_attn
        y_b = ssd_mixer(z, xbc, dt_raw, conv_w, conv_b, dt_bias, a_log, d_skip, ssd_norm) @ w_o_ssd
        mix = (jax.nn.sigmoid(g_a) * y_a + jax.nn.sigmoid(g_b) * y_b) @ w_out
        x = x + gate_m * rmsnorm(mix, post_norm_mix)
        h2 = rmsnorm(x, pre_norm_ffn) * (1.0 + scale_f) + shift_f
        u_gate, u_up = jnp.split(h2 @ w_ffn_in, 2, axis=-1)
        f = (jax.nn.silu(u_gate) * u_up) @ w_ffn_out
        x = x + gate_f * rmsnorm(f, post_norm_ffn)
    return x
```

```python
import math
import numpy as np
import concourse.bass as bass
import concourse.mybir as mybir
from concourse.bass_utils import run_bass_kernel_spmd

F32 = mybir.dt.float32
BF = mybir.dt.bfloat16
AF = mybir.ActivationFunctionType
ALU = mybir.AluOpType
AX = mybir.AxisListType

D = 1024
EPS = 1e-6
NEG = -1.0e30
DSZ = {F32: 4, BF: 2}
C_Q, C_KV, C_KI, C_WI, C_Z, C_XS, C_B, C_C, C_DT, C_GA, C_GB = 0, 256, 384, 448, 464, 2512, 4560, 5584, 6608, 6640, 7664
DFF = 2816
NJ = DFF // 128


def _isap(v):
    return hasattr(v, "tensor") and hasattr(v, "ap") and hasattr(v, "offset")


def _box(ap):
    t = ap.tensor
    esz = DSZ.get(ap.dtype, 4)
    pat = [(int(s), int(c)) for s, c in ap.ap]
    off = int(ap.offset)
    if "DRAM" in str(ap.space).upper() or "HBM" in str(ap.space).upper():
        ext = sum((c - 1) * abs(s) for s, c in pat)
        return (t.name, 0, 1, off * esz, (off + ext + 1) * esz)
    row = 1
    for d in list(t.shape)[1:]:
        row *= int(d)
    p0 = off // row
    f0 = off % row
    pstride, pcnt = pat[0]
    if pstride == 0:
        pcnt = 1
    ext = sum((c - 1) * abs(s) for s, c in pat[1:])
    if "PSUM" in str(ap.space).upper():
        return (t.name, 0, 128, 0, 2048)
    return (t.name, p0, p0 + pcnt, f0 * esz, (f0 + ext + 1) * esz)


def _ovl(a, b):
    return a[1] < b[2] and b[1] < a[2] and a[3] < b[4] and b[3] < a[4]


def _contains(a, b):
    return a[1] <= b[1] and b[2] <= a[2] and a[3] <= b[3] and b[4] <= a[4]


class Sched:
    NDS = 40

    def __init__(self, nc):
        self.nc = nc
        self.engs = {}
        for name, e in (("pe", nc.tensor), ("act", nc.scalar), ("dve", nc.vector),
                        ("pool", nc.gpsimd), ("sp", nc.sync)):
            sem = nc.semaphore("sem_" + name).__enter__()
            self.engs[name] = dict(e=e, sem=sem, cnt=0, seen={}, key="E" + name)
        self.dsems = [nc.semaphore("dsem%d" % i).__enter__() for i in range(self.NDS)]
        self.dcnt = [0] * self.NDS
        self.dk = 0
        self.acc = {}
        self.nins = 0

    def _deps(self, eng, reads, writes):
        need = {}
        rb = [_box(a) for a in reads]
        wb = [_box(a) for a in writes]
        for b in rb:
            psum = b[0].startswith("ps") and b[0][2:].isdigit()
            for rec in self.acc.get(b[0], ()):
                if (rec[0] == "w" or (psum and rec[5] != eng)) and _ovl(rec[1], b):
                    if rec[5] == eng and eng == "pe":
                        continue
                    k = rec[2]
                    if need.get(k, (None, 0))[1] < rec[4]:
                        need[k] = (rec[3], rec[4])
        for b in wb:
            for rec in self.acc.get(b[0], ()):
                if _ovl(rec[1], b):
                    if rec[5] == eng and eng == "pe":
                        continue
                    k = rec[2]
                    if need.get(k, (None, 0))[1] < rec[4]:
                        need[k] = (rec[3], rec[4])
        return need, rb, wb

    def _record(self, eng, rb, wb, key, sem, val):
        for b in wb:
            lst = self.acc.setdefault(b[0], [])
            lst[:] = [r for r in lst if not _contains(b, r[1])]
            lst.append(("w", b, key, sem, val, eng))
        for b in rb:
            lst = self.acc.setdefault(b[0], [])
            lst[:] = [r for r in lst if not (r[0] == "r" and r[5] == eng and r[2] == key and _contains(b, r[1]))]
            lst.append(("r", b, key, sem, val, eng))

    def _wait(self, E, need):
        for k, (sem, val) in need.items():
            if E["seen"].get(k, 0) < val:
                E["e"].wait_ge(sem, val)
                E["seen"][k] = val

    def op(self, eng, fn, reads=(), writes=()):
        E = self.engs[eng]
        need, rb, wb = self._deps(eng, reads, writes)
        self._wait(E, need)
        ins = fn(E["e"])
        E["cnt"] += 1
        ins.then_inc(E["sem"], 1)
        self._record(eng, rb, wb, E["key"], E["sem"], E["cnt"])
        self.nins += 1
        return ins

    def dma(self, q, out, in_, **kw):
        E = self.engs[q]
        k = self.dk % self.NDS
        self.dk += 1
        sem = self.dsems[k]
        key = "D%d" % k
        need, rb, wb = self._deps("dma", [in_], [out])
        if self.dcnt[k] > 0:
            if need.get(key, (None, 0))[1] < self.dcnt[k]:
                need[key] = (sem, self.dcnt[k])
        self._wait(E, need)
        ins = E["e"].dma_start(out=out, in_=in_, **kw)
        self.dcnt[k] += 16
        ins.then_inc(sem, 16)
        self._record("dma", rb, wb, key, sem, self.dcnt[k])
        self.nins += 1
        return ins

    def barrier(self, only=None):
        for name, E in self.engs.items():
            if only and name not in only:
                continue
            need = {}
            for n2, E2 in self.engs.items():
                if n2 != name and E2["cnt"] > 0:
                    need[E2["key"]] = (E2["sem"], E2["cnt"])
            for k in range(self.NDS):
                if self.dcnt[k] > 0:
                    need["D%d" % k] = (self.dsems[k], self.dcnt[k])
            self._wait(E, need)
        if not only:
            self.acc = {}


def _interleave(*gens):
    gens = [g for g in gens if g is not None]
    while gens:
        for g in list(gens):
            try:
                next(g)
            except StopIteration:
                gens.remove(g)


def build(L, stop=99):
    from contextlib import ExitStack
    NT = L // 128
    KSEL = min(256, L // 4)
    TB = min(512, L)
    NB = L // TB
    nc = bass.Bass("TRN2", target_bir_lowering=False)
    S = Sched(nc)

    def dram(name, shape, dt=F32, kind="ExternalInput"):
        return nc.dram_tensor(name, list(shape), dt, kind=kind).ap()

    x = dram("x", [L, D]); cvec = dram("c", [D])
    ada_w = dram("ada_w", [D, 6 * D]); ada_b = dram("ada_b", [6 * D])
    pre_norm_mix = dram("pre_norm_mix", [D]); post_norm_mix = dram("post_norm_mix", [D])
    pre_norm_ffn = dram("pre_norm_ffn", [D]); post_norm_ffn = dram("post_norm_ffn", [D])
    w_in = dram("w_in", [D, 8688]); q_norm = dram("q_norm", [256]); kv_norm = dram("kv_norm", [128])
    w_uq = dram("w_uq", [256, 2048]); w_uv = dram("w_uv", [16, 128, 64]); rel_bias = dram("rel_bias", [32, 16])
    w_qidx = dram("w_qidx", [256, 1024]); kidx_norm = dram("kidx_norm", [64])
    conv_w = dram("conv_w", [4, 4096]); conv_b = dram("conv_b", [4096])
    dt_bias = dram("dt_bias", [32]); a_log = dram("a_log", [32]); d_skip = dram("d_skip", [32])
    ssd_norm = dram("ssd_norm", [2048]); w_o_attn = dram("w_o_attn", [D, D]); w_o_ssd = dram("w_o_ssd", [2048, D])
    w_out = dram("w_out", [D, D]); w_ffn_in = dram("w_ffn_in", [D, 2 * DFF]); w_ffn_out = dram("w_ffn_out", [DFF, D])
    cI4 = dram("cI4", [128, 512]); cJ = dram("cJ", [128, 128]); cTRI = dram("cTRI", [128, 128])
    cSTR = dram("cSTR", [128, 128]); cDM = dram("cDM", [128, 128]); cOH = dram("cOH", [32, 383])
    out = dram("out", [L, D], F32, kind="ExternalOutput")
    tv_d = dram("tv_d", [16, 383], F32, kind="Internal")
    mixa_d = dram("mixa_d", [L, D], BF, kind="Internal")
    yz_d = dram("yz_d", [L, 2048], BF, kind="Internal")
    x1_d = dram("x1_d", [L, D], F32, kind="Internal")

    glob = ExitStack()

    def sb(name, shape, dt=F32, st=None):
        return (st or glob).enter_context(nc.sbuf_tensor(name, list(shape), dt))

    banks = [glob.enter_context(nc.psum_tensor("ps%d" % i, [128, 512], F32)) for i in range(8)]
    bstate = {"k": 0, "free": list(range(8))}

    def bank():
        fr = bstate["free"]
        b = fr[bstate["k"] % len(fr)]
        bstate["k"] += 1
        return banks[b]

    def bfv(bk):
        return bk.bitcast(BF)

    def I(eng, meth, **kw):
        outs = [kw[k] for k in ("out", "accum_out", "ap") if k in kw and _isap(kw[k])]
        ins = [v for k, v in kw.items() if k not in ("out", "accum_out", "ap") and _isap(v)]
        return S.op(eng, lambda e: getattr(e, meth)(**kw), reads=ins, writes=outs)

    def mm(out_, lhsT, rhs, start=True, stop=True):
        return S.op("pe", lambda e: e.matmul(out_, lhsT, rhs, start=start, stop=stop), reads=[lhsT, rhs], writes=[out_])

    def tr(out_, in_, ident):
        return S.op("pe", lambda e: e.transpose(out_, in_, ident), reads=[in_, ident], writes=[out_])

    def act(out_, in_, func, **kw):
        return I("act", "activation", out=out_, in_=in_, func=func, **kw)

    def ts(eng, out_, in0, s1, s2, op0, op1=None):
        if op1 is None:
            return I(eng, "tensor_scalar", out=out_, in0=in0, scalar1=s1, scalar2=None, op0=op0)
        return I(eng, "tensor_scalar", out=out_, in0=in0, scalar1=s1, scalar2=s2, op0=op0, op1=op1)

    def tt(eng, out_, in0, in1, op):
        return I(eng, "tensor_tensor", out=out_, in0=in0, in1=in1, op=op)

    def stt(out_, in0, scalar, in1, op0, op1):
        return I("dve", "scalar_tensor_tensor", out=out_, in0=in0, scalar=scalar, in1=in1, op0=op0, op1=op1)

    def cp(eng, out_, in_):
        if eng == "act":
            return I("act", "copy", out=out_, in_=in_)
        return I(eng, "tensor_copy", out=out_, in_=in_)

    def memset(eng, ap, v):
        return I(eng, "memset", ap=ap, constant=v)

    def dma(q, out_, in_, **kw):
        return S.dma(q, out_, in_, **kw)

    def bc(ap, shape):
        return ap.broadcast_to(list(shape))

    _rr = {"k": 0}

    def evq():
        _rr["k"] += 1
        return "act" if _rr["k"] % 2 else "dve"

    identF = sb("identF", [128, 128]); dma("sp", identF[:], cI4[:, 0:128])
    i4b = sb("i4b", [128, 512], BF)
    identB = i4b[:, 0:128]
    jb = sb("jb", [128, 128], BF)
    triF = sb("triF", [128, 128]); dma("sp", triF[:], cTRI[:, :])
    strF = sb("strF", [128, 128]); dma("sp", strF[:], cSTR[:, :])
    dmF = sb("dmF", [128, 128]); dma("sp", dmF[:], cDM[:, :])
    onesF = sb("onesF", [128, 128]); memset("dve", onesF[:], 1.0)
    onesB = sb("onesB", [128, 128], BF); memset("dve", onesB[:], 1.0)

    lcstg = sb("lcstg", [48, 128], F32)

    def load_cols(dst, vec, n):
        kc = n // 128
        if kc <= 2:
            for k_ in range(kc):
                dma("sp", dst[:, k_:k_ + 1], vec[k_ * 128:(k_ + 1) * 128].rearrange("(p o) -> p o", o=1))
            return
        dma("sp", lcstg[0:kc, :], vec.rearrange("(k p) -> k p", p=128))
        ps = bank()
        tr(ps[:, 0:kc], lcstg[0:kc, :], identF[0:kc, 0:kc])
        cp("dve", dst, ps[:, 0:kc])

    wstg = [sb("wstg%d" % i, [128, 1024], F32) for i in range(3)]
    _ws = {"k": 0}

    def stage():
        _ws["k"] += 1
        return wstg[_ws["k"] % 3]

    def load_w(dst, W, c0, c1, r0=0):
        kcn = dst.shape[1]
        n = c1 - c0
        per = max(1, 1024 // n)
        for k0 in range(0, kcn, per):
            k1 = min(kcn, k0 + per)
            sv = stage()[:, 0:(k1 - k0) * n].rearrange("p (k n) -> p k n", n=n)
            dma("sp", sv, W[r0 + k0 * 128:r0 + k1 * 128, c0:c1].rearrange("(kc p) n -> p kc n", p=128))
            cp("pool", dst[:, k0:k1, :], sv)

    sv_ = stage(); dma("sp", sv_[:, 0:512], cI4[:, :]); cp("pool", i4b[:], sv_[:, 0:512])
    sv_ = stage(); dma("sp", sv_[:, 0:128], cJ[:, :]); cp("pool", jb[:], sv_[:, 0:128])

    modc = sb("modc", [128, 48])
    gcol = sb("gcol", [128, 8]); scol = sb("scol", [128, 8])
    gcol2 = sb("gcol2", [128, 8]); scol2 = sb("scol2", [128, 8])
    gpmB = sb("gpmB", [128, D]); gpfB = sb("gpfB", [128, D])
    with ExitStack() as st:
        ccol = sb("ccol", [128, 8], F32, st); scc = sb("scc", [128, 8], F32, st)
        abcol = sb("abcol", [128, 48], F32, st)
        tmpc = sb("tmpc", [128, 8], F32, st)
        load_cols(ccol[:], cvec, D)
        load_cols(abcol[:], ada_b, 6 * D)
        act(scc[:], ccol[:], AF.Silu)
        awt = [sb("awt%d" % i, [128, 8, 512], F32, st) for i in range(2)]
        pm = bank()
        for cb in range(12):
            a = awt[cb % 2]
            dma("sp", a[:], ada_w[:, cb * 512:(cb + 1) * 512].rearrange("(kc p) n -> p kc n", p=128))
            for jj in range(4):
                j = cb * 4 + jj
                for kc in range(8):
                    mm(pm[:, j:j + 1], a[:, kc, jj * 128:(jj + 1) * 128], scc[:, kc:kc + 1], start=(kc == 0), stop=(kc == 7))
        tt("dve", modc[:], pm[:, 0:48], abcol[:], ALU.add)
        pn = sb("pn", [128, 8], F32, st)
        load_cols(pn[:], pre_norm_mix, D)
        ts("dve", tmpc[:], modc[:, 8:16], 1.0, None, ALU.add)
        tt("dve", gcol[:], tmpc[:], pn[:], ALU.mult)
        cp("dve", scol[:], modc[:, 0:8])
        pn2 = sb("pn2", [128, 8], F32, st)
        load_cols(pn2[:], pre_norm_ffn, D)
        ts("dve", tmpc[:], modc[:, 32:40], 1.0, None, ALU.add)
        tt("dve", gcol2[:], tmpc[:], pn2[:], ALU.mult)
        cp("dve", scol2[:], modc[:, 24:32])
        for (dst, gsl, pvec, nm) in ((gpmB, slice(16, 24), post_norm_mix, "a"), (gpfB, slice(40, 48), post_norm_ffn, "b")):
            pcol = sb("pcol" + nm, [128, 8], F32, st)
            gp = sb("gp" + nm, [128, 8], F32, st)
            gb = sb("gb" + nm, [128, 128], F32, st)
            load_cols(pcol[:], pvec, D)
            tt("dve", gp[:], modc[:, gsl], pcol[:], ALU.mult)
            for kc in range(8):
                ts("dve", gb[:], onesF[:], gp[:, kc:kc + 1], None, ALU.mult)
                ps = bank()
                mm(ps[:, 0:128], gb[:], identF[:])
                cp("act", dst[:, kc * 128:(kc + 1) * 128], ps[:, 0:128])
        S.barrier()
    if stop == 0:
        return nc

    import os
    DBG = int(os.environ.get("KDBG", "9"))

    def norm_to_T(xt, dstT, tile, gc, sc, tmp_bf, stat):
        junk = tmp_bf
        act(junk[:], xt, AF.Square, accum_out=stat[:, 0:1])
        if DBG < 2:
            return
        ts("dve", stat[:, 1:2], stat[:, 0:1], 1.0 / D, EPS, ALU.mult, ALU.add)
        act(stat[:, 2:3], stat[:, 1:2], AF.Sqrt)
        I("dve", "reciprocal", out=stat[:, 3:4], in_=stat[:, 2:3])
        if DBG < 3:
            return
        ts("dve", tmp_bf[:], xt, stat[:, 3:4], None, ALU.mult)
        if DBG < 4:
            return
        ps = bank()
        pb = bfv(ps)
        for kc in range(8):
            tr(pb[:, kc * 128:(kc + 1) * 128], tmp_bf[:, kc * 128:(kc + 1) * 128], identB)
        if DBG < 5:
            return
        for kc in range(8):
            if False:
                pass
            else:
                ts("dve", dstT[:, kc, tile * 128:(tile + 1) * 128], pb[:, kc * 128:(kc + 1) * 128],
                   gc[:, kc:kc + 1], sc[:, kc:kc + 1], ALU.mult, ALU.add)

    hT = sb("hT", [128, 8, L], BF)
    stA = ExitStack()
    st = stA
    dl = []
    wq = sb("wq", [128, 8, 256], BF, st); dl.append(lambda: load_w(wq[:], w_in, C_Q, C_Q + 256))
    wkv = sb("wkv", [128, 8, 128], BF, st); dl.append(lambda: load_w(wkv[:], w_in, C_KV, C_KV + 128))
    wkk = sb("wkk", [128, 8, 128], BF, st)
    dl.append(lambda: (load_w(wkk[:, :, 0:64], w_in, C_KI, C_KI + 64), load_w(wkk[:, :, 64:128], w_in, C_KI, C_KI + 64)))
    wwi = sb("wwi", [128, 8, 16], BF, st); dl.append(lambda: load_w(wwi[:], w_in, C_WI, C_WI + 16))
    wuq = sb("wuq", [128, 2, 2048], BF, st)
    for r in range(2):
        dl.append(lambda r=r: (load_w(wuq[:, r:r + 1, 0:1024], w_uq, 0, 1024, r0=r * 128), load_w(wuq[:, r:r + 1, 1024:2048], w_uq, 1024, 2048, r0=r * 128)))
    wqi = sb("wqi", [128, 2, 1024], BF, st)
    for r in range(2):
        dl.append(lambda r=r: load_w(wqi[:, r:r + 1, :], w_qidx, 0, 1024, r0=r * 128))
    wuv = sb("wuv", [128, 16, 128], BF, st)

    def _ld_wuv():
        memset("pool", wuv[:], 0.0)
        sv_ = stage()[:, 0:1024].rearrange("p (h d) -> p h d", d=64)
        dma("sp", sv_, w_uv.rearrange("h c d -> c h d"))
        for h in range(16):
            o = (h % 2) * 64
            cp("pool", wuv[:, h, o:o + 64], sv_[:, h, :])
    dl.append(_ld_wuv)
    woa = sb("woa", [128, 8, D], BF, st)
    for k0 in range(0, 8, 2):
        dl.append(lambda k0=k0: load_w(woa[:, k0:k0 + 2, :], w_o_attn, 0, D, r0=k0 * 128))
    wga = sb("wga", [128, 8, D], BF, st)
    for k0 in range(0, 8, 2):
        dl.append(lambda k0=k0: load_w(wga[:, k0:k0 + 2, :], w_in, C_GA, C_GA + D, r0=k0 * 128))
    qncol = sb("qncol", [128, 2], F32, st); dl.append(lambda: load_cols(qncol[:], q_norm, 256))
    kvcol = sb("kvcol", [128, 1], F32, st); dl.append(lambda: load_cols(kvcol[:], kv_norm, 128))
    kicol = sb("kicol", [128, 1], F32, st)
    dl.append(lambda: (dma("sp", kicol[0:64, :], kidx_norm.rearrange("(p o) -> p o", o=1)),
                       dma("sp", kicol[64:128, :], kidx_norm.rearrange("(p o) -> p o", o=1))))

    with ExitStack() as st:
        xts = [sb("xa%d" % i, [128, D], F32, st) for i in range(2)]
        xbs = [sb("xb%d" % i, [128, D], BF, st) for i in range(2)]
        sts = [sb("xs%d" % i, [128, 4], F32, st) for i in range(2)]
        for i in range(NT):
            xt = xts[i % 2]
            dma("sp", xt[:], x[i * 128:(i + 1) * 128, :])
            if i >= 1:
                for _ in range(3):
                    if dl:
                        dl.pop(0)()
            norm_to_T(xt[:], hT, i, gcol, scol, xbs[i % 2], sts[i % 2])
        while dl:
            dl.pop(0)()
        S.barrier()
    if stop == 1:
        return nc

    with stA as st:
        BFt = sb("BFt", [128, 2, 16, 128], BF, st)
        with ExitStack() as s2:
            rb = sb("rb", [32, 16], F32, s2); dma("sp", rb[:], rel_bias[:, :])
            rb15 = sb("rb15", [32, 16], F32, s2); dma("sp", rb15[:], rel_bias[15, :].partition_broadcast(32))
            tt("dve", rb[:], rb[:], rb15[:], ALU.subtract)
            oh = sb("oh", [32, 383], F32, s2); dma("sp", oh[:], cOH[:, :])
            ps = bank()
            mm(ps[0:16, 0:383], rb[:], oh[:])
            tv = sb("tv", [16, 383], F32, s2)
            cp("dve", tv[:], ps[0:16, 0:383])
            dma("sp", tv_d[:, :], tv[:])
            for dd in range(2):
                src = bass.AP(tensor=tv_d.tensor, offset=128 * dd, ap=[[1, 128], [383, 16], [1, 128]])
                for hh_ in range(2):
                    src = bass.AP(tensor=tv_d.tensor, offset=128 * dd + hh_ * 8 * 383, ap=[[1, 128], [383, 8], [1, 128]])
                    sv_ = stage()[:, 0:1024].rearrange("p (h q) -> p h q", q=128)
                    dma("sp", sv_, src)
                    cp("pool", BFt[:, dd, hh_ * 8:(hh_ + 1) * 8, :], sv_)
            S.barrier()
        if stop == 2:
            return nc

        qnT = sb("qnT", [128, 2, L], BF, st)
        knT = sb("knT", [128, L], BF, st)
        kiT = sb("kiT", [128, L], BF, st)
        with ExitStack() as s2:
            raw = sb("lraw", [128, 2, TB], F32, s2)
            sq = sb("lsq", [128, 2, TB], BF, s2)
            rs = sb("lrs", [128, TB], F32, s2)
            for tb in range(NB):
                tsl = slice(tb * TB, (tb + 1) * TB)
                for (wt, nch, ncol, dst) in ((wq, 2, qncol, qnT), (wkv, 1, kvcol, knT), (wkk, 1, kicol, kiT)):
                    for r in range(nch):
                        ps = bank()
                        for kc in range(8):
                            mm(ps[:, 0:TB], wt[:, kc, r * 128:(r + 1) * 128], hT[:, kc, tsl], start=(kc == 0), stop=(kc == 7))
                        cp("dve", raw[:, r, :], ps[:, 0:TB])
                        act(sq[:, r, :], raw[:, r, :], AF.Square)
                    p2 = bank()
                    for r in range(nch):
                        mm(p2[:, 0:TB], onesB[:], sq[:, r, :], start=(r == 0), stop=(r == nch - 1))
                    ts("dve", rs[:], p2[:, 0:TB], 1.0 / (128 * nch), EPS, ALU.mult, ALU.add)
                    act(rs[:], rs[:], AF.Sqrt)
                    I("dve", "reciprocal", out=rs[:], in_=rs[:])
                    for r in range(nch):
                        o_ = dst[:, r, tsl] if nch == 2 else dst[:, tsl]
                        stt(o_, raw[:, r, :], ncol[:, r:r + 1], rs[:], ALU.mult, ALU.mult)
            S.barrier()
        if stop == 3:
            return nc
        kvtm = sb("kvtm", [128, NT, 128], BF, st)
        witm = sb("witm", [128, NT, 16], F32, st)
        for t in range(NT):
            ps = bank(); pb = bfv(ps)
            tr(pb[:, 0:128], knT[:, t * 128:(t + 1) * 128], identB)
            cp("dve", kvtm[:, t, :], pb[:, 0:128])
            ps = bank()
            for kc in range(8):
                mm(ps[:, 0:16], hT[:, kc, t * 128:(t + 1) * 128], wwi[:, kc, :], start=(kc == 0), stop=(kc == 7))
            act(witm[:, t, :], ps[:, 0:16], AF.Copy, scale=1.0 / 32.0)

        qT = [sb("qT%d" % i, [128, 16, 128], BF, st) for i in range(2)]
        qiT = [sb("qiT%d" % i, [128, 8, 128], BF, st) for i in range(2)]
        accs = [sb("acc0", [128, L], F32, st)] * 2
        works = [sb("work0", [128, L], F32, st)] * 2
        sel = sb("sel", [128, L], BF, st)
        selT = [sb("selT%d" % i, [128, NT, 128], BF, st) for i in range(2)]
        m8 = [sb("m8%d" % i, [128, 8], F32, st) for i in range(2)]
        thr = [sb("thr%d" % i, [128, 1], F32, st) for i in range(2)]
        rtmp = [sb("rtmp%d" % i, [128, 512], BF, st) for i in range(4)]
        dws = [sb("dw0", [128, 16, 128], BF, st)] * 2
        pTs = [sb("pT%d" % i, [128, 512], BF, st) for i in range(3)]
        oT = sb("oT", [128, 16, 128], BF, st)
        o2T = sb("o2T", [128, 8, 128], BF, st)
        lnd = sb("lnd", [128, 512], F32, st)
        sga = [sb("sga%d" % i, [128, 512], BF, st) for i in range(2)]
        mixt = [sb("mixt0", [128, D], BF, st)] * 2
        bstate["free"] = [0, 1, 2, 3, 7]
        bo = [banks[4], banks[4]]; bd = [banks[5], banks[5]]; pacc = banks[6]
        cnt = {"r": 0, "p": 0}
        QS = 128.0 ** -0.5

        def stage1(i):
            b = i % 2
            qs = slice(i * 128, (i + 1) * 128)
            for hq in range(4):
                ps = bank()
                for hh in range(4):
                    h = hq * 4 + hh
                    for r in range(2):
                        mm(ps[:, hh * 128:(hh + 1) * 128], wuq[:, r, h * 128:(h + 1) * 128], qnT[:, r, qs], start=(r == 0), stop=(r == 1))
                act(qT[b][:, hq * 4:(hq + 1) * 4, :].rearrange("p h q -> p (h q)"), ps[:, :], AF.Copy, scale=QS)
            yield
            for mq in range(2):
                ps = bank()
                for mm_ in range(4):
                    m = mq * 4 + mm_
                    for r in range(2):
                        mm(ps[:, mm_ * 128:(mm_ + 1) * 128], wqi[:, r, m * 128:(m + 1) * 128], qnT[:, r, qs], start=(r == 0), stop=(r == 1))
                cp("act", qiT[b][:, mq * 4:(mq + 1) * 4, :].rearrange("p h q -> p (h q)"), ps[:, :])
            yield
            nk = (i + 1) * 128
            acc = accs[b]
            dw = dws[b]
            for h in range(16):
                ts("pool", dw[:, h, :], identF[:], witm[:, i, h:h + 1], None, ALU.mult)
            for kb in range((nk + 511) // 512):
                n = min(512, nk - kb * 512)
                ks = slice(kb * 512, kb * 512 + n)
                pend = []

                def accmm(h, rt):
                    mm(pacc[:, 0:n], dw[:, h, :], rt[:, 0:n], start=(h == 0), stop=(h == 15))

                for h in range(16):
                    m, half = h // 2, h % 2
                    pr = slice(half * 64, half * 64 + 64)
                    ps = bank()
                    mm(ps[:, 0:n], qiT[b][pr, m, :], kiT[pr, ks])
                    rt = rtmp[cnt["r"] % 4]; cnt["r"] += 1
                    act(rt[:, 0:n], ps[:, 0:n], AF.Relu)
                    pend.append((h, rt))
                    if len(pend) > 2:
                        accmm(*pend.pop(0))
                    if h % 4 == 3:
                        yield
                while pend:
                    accmm(*pend.pop(0))
                cp("act", acc[:, ks], pacc[:, 0:n])
                yield
            tt("pool", acc[:, i * 128:(i + 1) * 128], acc[:, i * 128:(i + 1) * 128], dmF[:], ALU.add)
            if nk > KSEL:
                src = acc
                rounds = KSEL // 8
                for r in range(rounds):
                    I("dve", "max", out=m8[b][:], in_=src[:, 0:nk])
                    if r < rounds - 1:
                        I("dve", "match_replace", out=works[b][:, 0:nk], in_to_replace=m8[b][:], in_values=src[:, 0:nk], imm_value=NEG)
                        src = works[b]
                    if r % 2 == 1:
                        yield
                ts("dve", thr[b][:], m8[b][:, 7:8], -1.0e29, None, ALU.max)
            else:
                memset("dve", thr[b][:], -1.0e29)
            ts("dve", sel[:, 0:nk], acc[:, 0:nk], thr[b][:, 0:1], None, ALU.is_ge)
            for j0 in range(0, i + 1, 8):
                j1 = min(i + 1, j0 + 8)
                ps = bank(); pb = bfv(ps)
                for j in range(j0, j1):
                    tr(pb[:, (j - j0) * 128:(j - j0 + 1) * 128], sel[:, j * 128:(j + 1) * 128], identB)
                cp("act", selT[b][:, j0:j1, :].rearrange("p j q -> p (j q)"), pb[:, 0:(j1 - j0) * 128])
            yield

        def stage2(i):
            b = i % 2
            qs = slice(i * 128, (i + 1) * 128)
            its = [(hg, j) for hg in range(4) for j in range(i + 1)]
            pend = []

            def logits(hg, j):
                dd = i - j
                pl = bank()
                mm(pl[:, :], knT[:, j * 128:(j + 1) * 128], qT[b][:, hg * 4:(hg + 1) * 4, :].rearrange("p h q -> p (h q)"), start=True, stop=(dd >= 2))
                if dd < 2:
                    mm(pl[:, :], jb[:], BFt[:, dd, hg * 4:(hg + 1) * 4, :].rearrange("p h q -> p (h q)"), start=False, stop=True)
                pT = pTs[cnt["p"] % 3]; cnt["p"] += 1
                act(pT[:], pl[:, :], AF.Exp)
                pv_ = pT[:].rearrange("p (h q) -> p h q", h=4)
                tt("pool", pv_, pv_, bc(selT[b][:, j, :].unsqueeze(1), [128, 4, 128]), ALU.mult)
                return pT

            def pv(hg, j, pT):
                po = bo[0]; pd = bd[0]
                mm(po[:, :], kvtm[:, j, :], pT[:], start=(j == 0), stop=(j == i))
                mm(pd[:, :], onesB[:], pT[:], start=(j == 0), stop=(j == i))
                if j == i:
                    act(lnd[:], pd[:, :], AF.Ln)
                    act(lnd[:], lnd[:], AF.Exp, scale=-1.0)
                    tt("dve", oT[:, hg * 4:(hg + 1) * 4, :].rearrange("p h q -> p (h q)"), po[:, :], lnd[:], ALU.mult)

            for t_, (hg, j) in enumerate(its):
                pend.append((hg, j, logits(hg, j)))
                if len(pend) > 2:
                    pv(*pend.pop(0))
                if t_ % 4 == 3:
                    yield
            while pend:
                pv(*pend.pop(0))
            yield
            for mq in range(2):
                ps = bank()
                for mm_ in range(4):
                    m = mq * 4 + mm_
                    mm(ps[:, mm_ * 128:(mm_ + 1) * 128], wuv[:, 2 * m, :], oT[:, 2 * m, :], start=True, stop=False)
                    mm(ps[:, mm_ * 128:(mm_ + 1) * 128], wuv[:, 2 * m + 1, :], oT[:, 2 * m + 1, :], start=False, stop=True)
                cp("act", o2T[:, mq * 4:(mq + 1) * 4, :].rearrange("p h q -> p (h q)"), ps[:, :])
            yield
            mx = mixt[b]
            for cb in range(2):
                cs = slice(cb * 512, (cb + 1) * 512)
                pg = bank()
                for kc in range(8):
                    mm(pg[:, :], hT[:, kc, qs], wga[:, kc, cs], start=(kc == 0), stop=(kc == 7))
                act(sga[cb][:], pg[:, :], AF.Sigmoid)
                py = bank()
                for m in range(8):
                    mm(py[:, :], o2T[:, m, :], woa[:, m, cs], start=(m == 0), stop=(m == 7))
                tt("dve", mx[:, cs], py[:, :], sga[cb][:], ALU.mult)
                yield
            dma("sp", mixa_d[qs, :], mx[:])
            yield

        prev = None
        for i in range(NT + 1):
            g1 = stage1(i) if i < NT else None
            g2 = stage2(i - 1) if i >= 1 else None
            _interleave(g1, g2)
        bstate["free"] = list(range(8))
        S.barrier()
        if stop == 4:
            return nc

    ssq = sb("ssq", [128, NT, 8])
    rstds = sb("rstds", [128, NT])
    with ExitStack() as st:
        dtall = sb("dtall", [128, NT, 32], F32, st); dAall = sb("dAall", [128, NT, 32], F32, st)
        eaall = sb("eaall", [128, NT, 32], F32, st); decall = sb("decall", [128, NT, 32], F32, st)
        cdall = sb("cdall", [128, NT, 32], F32, st)
        dI = sb("dI", [128, 32, 128], BF, st)
        cwc = sb("cwc", [128, 4, 32], F32, st); cbc = sb("cbc", [128, 32], F32, st)
        for k in range(4):
            load_cols(cwc[:, k, :], conv_w[k, :], 4096)
        load_cols(cbc[:], conv_b, 4096)
        with ExitStack() as s2:
            wdt = sb("wdt", [128, 8, 32], BF, s2); load_w(wdt[:], w_in, C_DT, C_DT + 32)
            dtbB = sb("dtbB", [128, 32], F32, s2); dma("sp", dtbB[:], dt_bias.partition_broadcast(128))
            AB = sb("AB", [128, 32], F32, s2); dma("sp", AB[:], a_log.partition_broadcast(128))
            dsB = sb("dsB", [128, 32], F32, s2); dma("sp", dsB[:], d_skip.partition_broadcast(128))
            act(AB[:], AB[:], AF.Exp)
            ts("dve", AB[:], AB[:], -1.0, None, ALU.mult)
            for h in range(32):
                ts("pool", dI[:, h, :], identF[:], dsB[:, h:h + 1], None, ALU.mult)
            xb_ = sb("xb_", [128, 32], F32, s2); ax = sb("ax_", [128, 32], F32, s2)
            ee = sb("ee_", [128, 32], F32, s2); tot = sb("tot_", [128, 32], F32, s2)
            for t in range(NT):
                ps = bank()
                for kc in range(8):
                    mm(ps[:, 0:32], hT[:, kc, t * 128:(t + 1) * 128], wdt[:, kc, :], start=(kc == 0), stop=(kc == 7))
                tt("dve", xb_[:], ps[:, 0:32], dtbB[:], ALU.add)
                act(ax[:], xb_[:], AF.Abs)
                act(ee[:], ax[:], AF.Exp, scale=-1.0)
                ts("dve", ee[:], ee[:], 1.0, None, ALU.add)
                act(ee[:], ee[:], AF.Ln)
                stt(dtall[:, t, :], xb_[:], 0.0, ee[:], ALU.max, ALU.add)
                tt("dve", dAall[:, t, :], dtall[:, t, :], AB[:], ALU.mult)
                p1 = bank(); p2 = bank()
                mm(p1[:, 0:32], triF[:], dAall[:, t, :])
                mm(p2[:, 0:32], onesF[:], dAall[:, t, :])
                act(eaall[:, t, :], p1[:, 0:32], AF.Exp)
                act(cdall[:, t, :], p2[:, 0:32], AF.Exp)
                cp("act", tot[:], p2[:, 0:32])
                tt("dve", tot[:], tot[:], p1[:, 0:32], ALU.subtract)
                act(decall[:, t, :], tot[:], AF.Exp)
            S.barrier()

        wxs = [sb("wxs%d" % i, [128, 8, 256], BF, st) for i in range(2)]
        wBs = [sb("wBs%d" % i, [128, 8, 128], BF, st) for i in range(2)]
        wCs = [sb("wCs%d" % i, [128, 8, 128], BF, st) for i in range(2)]
        wzs = [sb("wzs%d" % i, [128, 8, 256], BF, st) for i in range(2)]
        raw = sb("raw", [128, 4, 3 + L], BF, st)
        memset("pool", raw[:, :, 0:3], 0.0)
        cv = sb("cv", [128, 4, L], BF, st)
        cvf = [sb("cvf%d" % i, [128, L], F32, st) for i in range(2)]
        xstm = sb("xstm", [128, NT, 256], BF, st); Btm = sb("Btm", [128, NT, 128], BF, st)
        sztm = sb("sztm", [128, NT, 256], BF, st)
        prevs = sb("prevs", [128, 256], F32, st); prevb = sb("prevb", [128, 256], BF, st)
        cbm = [sb("cbm%d" % i, [128, 128], BF, st) for i in range(2)]
        L4 = [sb("L4%d" % i, [128, 4, 128], BF, st) for i in range(2)]
        triB = sb("triB", [128, 128], BF, st); cp("pool", triB[:], triF[:])
        E4 = [sb("E4%d" % i, [128, 4, 128], BF, st) for i in range(2)]
        M4 = [sb("M4%d" % i, [128, 4, 128], BF, st) for i in range(2)]
        X4 = [sb("X4%d" % i, [128, 4, 64], BF, st) for i in range(2)]
        Xd4 = [sb("Xd4%d" % i, [128, 4, 64], BF, st) for i in range(2)]
        yo = [sb("yo%d" % i, [128, 4, 64], F32, st) for i in range(2)]
        y1 = [sb("y1%d" % i, [128, 256], F32, st) for i in range(2)]
        yzt = [sb("yzt%d" % i, [128, 256], BF, st) for i in range(2)]
        yjunk = sb("yjunk", [128, 256], BF, st)
        ptmp = sb("ptmp", [128, 4, 64], F32, st)

        def load_group_w(g):
            b = g % 2
            load_w(wxs[b][:], w_in, C_XS + g * 256, C_XS + (g + 1) * 256)
            load_w(wBs[b][:], w_in, C_B + g * 128, C_B + (g + 1) * 128)
            load_w(wCs[b][:], w_in, C_C + g * 128, C_C + (g + 1) * 128)
            load_w(wzs[b][:], w_in, C_Z + g * 256, C_Z + (g + 1) * 256)

        load_group_w(0)
        for g in range(8):
            b = g % 2
            h0 = g * 4
            if g + 1 < 8:
                load_group_w(g + 1)
            chunks = (2 * g, 2 * g + 1, 16 + g, 24 + g)
            for r in range(4):
                wt = (wxs[b][:, :, 0:128], wxs[b][:, :, 128:256], wBs[b][:, :, :], wCs[b][:, :, :])[r]
                for tb in range(NB):
                    ps = bank()
                    for kc in range(8):
                        mm(ps[:, 0:TB], wt[:, kc, :], hT[:, kc, tb * TB:(tb + 1) * TB], start=(kc == 0), stop=(kc == 7))
                    cp(evq(), raw[:, r, 3 + tb * TB:3 + (tb + 1) * TB], ps[:, 0:TB])
            for r in range(4):
                ch = chunks[r]
                cf = cvf[r % 2]
                ts("dve", cf[:], raw[:, r, 3:3 + L], cwc[:, 3, ch:ch + 1], cbc[:, ch:ch + 1], ALU.mult, ALU.add)
                for k in range(3):
                    stt(cf[:], raw[:, r, k:k + L], cwc[:, k, ch:ch + 1], cf[:], ALU.mult, ALU.add)
                act(cv[:, r, :], cf[:], AF.Silu)
            for t in range(NT):
                ps = bank(); pb = bfv(ps)
                tsl = slice(t * 128, (t + 1) * 128)
                tr(pb[:, 0:128], cv[:, 0, tsl], identB); tr(pb[:, 128:256], cv[:, 1, tsl], identB)
                tr(pb[:, 256:384], cv[:, 2, tsl], identB)
                e_ = evq()
                cp(e_, xstm[:, t, :], pb[:, 0:256])
                cp(e_, Btm[:, t, :], pb[:, 256:384])
                pz = bank()
                for kc in range(8):
                    mm(pz[:, 0:256], hT[:, kc, tsl], wzs[b][:, kc, :], start=(kc == 0), stop=(kc == 7))
                act(sztm[:, t, :], pz[:, 0:256], AF.Silu)
            memset("pool", prevs[:], 0.0); memset("pool", prevb[:], 0.0)
            hs = slice(h0, h0 + 4)

            def stX(c):
                cc = c % 2
                csl = slice(c * 128, (c + 1) * 128)
                pcb = bank()
                mm(pcb[:, 0:128], cv[:, 2, csl], cv[:, 3, csl])
                tt("dve", cbm[cc][:], pcb[:, 0:128], triF[:], ALU.mult)
                tt("pool", L4[cc][:], bc(strF[:].unsqueeze(1), [128, 4, 128]), bc(dAall[:, c, hs].unsqueeze(2), [128, 4, 128]), ALU.mult)
                pseg = bank()
                for hh in range(4):
                    mm(pseg[:, hh * 128:(hh + 1) * 128], L4[cc][:, hh, :], triB[:])
                act(E4[cc][:].rearrange("p h l -> p (h l)"), pseg[:, :], AF.Exp)
                tt("pool", M4[cc][:], E4[cc][:], bc(cbm[cc][:].unsqueeze(1), [128, 4, 128]), ALU.mult)
                xv = xstm[:, c, :].rearrange("p (h d) -> p h d", h=4)
                tt("dve", X4[cc][:], xv, bc(dtall[:, c, hs].unsqueeze(2), [128, 4, 64]), ALU.mult)
                tt("pool", Xd4[cc][:], X4[cc][:], bc(decall[:, c, hs].unsqueeze(2), [128, 4, 64]), ALU.mult)

            def stY(c):
                cc = c % 2
                csl = slice(c * 128, (c + 1) * 128)
                py = bank()
                for hh in range(4):
                    mm(py[:, hh * 64:(hh + 1) * 64], M4[cc][:, hh, :], X4[cc][:, hh, :], start=True, stop=False)
                    mm(py[:, hh * 64:(hh + 1) * 64], dI[:, h0 + hh, :], xstm[:, c, hh * 64:(hh + 1) * 64], start=False, stop=True)
                pS = bank()
                mm(pS[:, 0:256], Btm[:, c, :], Xd4[cc][:].rearrange("p h d -> p (h d)"))
                pyo = bank()
                mm(pyo[:, 0:256], cv[:, 3, csl], prevb[:])
                tt("dve", yo[cc][:], pyo[:, 0:256].rearrange("p (h d) -> p h d", h=4), bc(eaall[:, c, hs].unsqueeze(2), [128, 4, 64]), ALU.mult)
                tt("dve", y1[cc][:], py[:, 0:256], yo[cc][:].rearrange("p h d -> p (h d)"), ALU.add)
                tt("pool", yzt[cc][:], y1[cc][:], sztm[:, c, :], ALU.mult)
                act(yjunk[:], yzt[cc][:], AF.Square, accum_out=ssq[:, c, g:g + 1])
                dma("sp", yz_d[csl, g * 256:(g + 1) * 256], yzt[cc][:])
                tt("pool", ptmp[:], prevs[:].rearrange("p (h d) -> p h d", h=4), bc(cdall[:, c, hs].unsqueeze(2), [128, 4, 64]), ALU.mult)
                tt("dve", prevs[:], ptmp[:].rearrange("p h d -> p (h d)"), pS[:, 0:256], ALU.add)
                cp("act", prevb[:], prevs[:])

            for c in range(NT + 1):
                if c < NT:
                    stX(c)
                if c >= 1:
                    stY(c - 1)
        I("dve", "tensor_reduce", out=rstds[:], in_=ssq[:], axis=AX.X, op=ALU.add)
        ts("dve", rstds[:], rstds[:], 1.0 / 2048, EPS, ALU.mult, ALU.add)
        act(rstds[:], rstds[:], AF.Sqrt)
        I("dve", "reciprocal", out=rstds[:], in_=rstds[:])
        S.barrier()
        if stop == 5:
            return nc

    h2T = sb("h2T", [128, 8, L], BF)
    with ExitStack() as st:
        wos = sb("wos", [128, 16, D], BF, st); load_w(wos[:, 0:8, :], w_o_ssd, 0, D); load_w(wos[:, 8:16, :], w_o_ssd, 0, D, r0=1024)
        wout = sb("wout", [128, 8, D], BF, st); load_w(wout[:], w_out, 0, D)
        wgb = sb("wgb", [128, 8, D], BF, st); load_w(wgb[:], w_in, C_GB, C_GB + D)
        sncol = sb("sncol", [128, 16], F32, st); load_cols(sncol[:], ssd_norm, 2048)
        yzl = [sb("yzl%d" % i, [128, 2048], BF, st) for i in range(2)]
        yT = [sb("yT0", [128, 16, 128], BF, st)] * 2
        sgb = [sb("sgb%d" % i, [128, 512], BF, st) for i in range(2)]
        mal = [sb("mal%d" % i, [128, D], BF, st) for i in range(2)]
        mpf = [sb("mpf0", [128, D], F32, st)] * 2
        mp = [sb("mp0", [128, D], BF, st)] * 2
        mpT = [sb("mpT%d" % i, [128, 8, 128], BF, st) for i in range(2)]
        xl = [sb("xl%d" % i, [128, D], F32, st) for i in range(2)]
        t1 = [sb("t10", [128, D], F32, st)] * 2
        x1t = xl
        jk = sb("jk", [128, 512], BF, st)
        so = [sb("so%d" % i, [128, 8], F32, st) for i in range(2)]
        xb2 = [sb("xb20", [128, D], BF, st)] * 2
        st2 = [sb("st2%d" % i, [128, 4], F32, st) for i in range(2)]
        def stageA(i):
            b = i % 2
            rs_ = slice(i * 128, (i + 1) * 128)
            dma("sp", yzl[b][:], yz_d[rs_, :])
            dma("sp", mal[b][:], mixa_d[rs_, :])
            for half in range(2):
                ps = bank(); pb = bfv(ps)
                for k8 in range(8):
                    kc = half * 8 + k8
                    tr(pb[:, k8 * 128:(k8 + 1) * 128], yzl[b][:, kc * 128:(kc + 1) * 128], identB)
                for k8 in range(8):
                    kc = half * 8 + k8
                    ts("dve", yT[b][:, kc, :], pb[:, k8 * 128:(k8 + 1) * 128], sncol[:, kc:kc + 1], None, ALU.mult)
            for cb in range(2):
                cs = slice(cb * 512, (cb + 1) * 512)
                pg = bank()
                for kc in range(8):
                    mm(pg[:, :], hT[:, kc, rs_], wgb[:, kc, cs], start=(kc == 0), stop=(kc == 7))
                act(sgb[cb][:], pg[:, :], AF.Sigmoid)
                py = bank()
                for kc in range(16):
                    mm(py[:, :], yT[b][:, kc, :], wos[:, kc, cs], start=(kc == 0), stop=(kc == 15))
                stt(mpf[b][:, cs], py[:, :], rstds[:, i:i + 1], sgb[cb][:], ALU.mult, ALU.mult)
            tt("dve", mp[b][:], mpf[b][:], mal[b][:], ALU.add)
            ps = bank(); pb = bfv(ps)
            for kc in range(8):
                tr(pb[:, kc * 128:(kc + 1) * 128], mp[b][:, kc * 128:(kc + 1) * 128], identB)
            cp("act", mpT[b][:].rearrange("p k t -> p (k t)"), pb[:, :])

        def stageB(i):
            b = i % 2
            rs_ = slice(i * 128, (i + 1) * 128)
            dma("sp", xl[b][:], x[rs_, :])
            pms = []
            for cb in range(2):
                cs = slice(cb * 512, (cb + 1) * 512)
                pm_ = bank(); pms.append(pm_)
                for kc in range(8):
                    mm(pm_[:, :], mpT[b][:, kc, :], wout[:, kc, cs], start=(kc == 0), stop=(kc == 7))
                act(jk[:], pm_[:, :], AF.Square, accum_out=so[b][:, cb:cb + 1])
            tt("dve", so[b][:, 2:3], so[b][:, 0:1], so[b][:, 1:2], ALU.add)
            ts("dve", so[b][:, 3:4], so[b][:, 2:3], 1.0 / D, EPS, ALU.mult, ALU.add)
            act(so[b][:, 4:5], so[b][:, 3:4], AF.Sqrt)
            I("dve", "reciprocal", out=so[b][:, 5:6], in_=so[b][:, 4:5])
            for cb in range(2):
                cs = slice(cb * 512, (cb + 1) * 512)
                stt(t1[b][:, cs], pms[cb][:, :], so[b][:, 5:6], gpmB[:, cs], ALU.mult, ALU.mult)
            tt("dve", x1t[b][:], t1[b][:], xl[b][:], ALU.add)
            dma("sp", x1_d[rs_, :], x1t[b][:])
            norm_to_T(x1t[b][:], h2T, i, gcol2, scol2, xb2[b], st2[b])

        for i in range(NT + 1):
            if i < NT:
                stageA(i)
            if i >= 1:
                stageB(i - 1)
        S.barrier()
        if stop == 6:
            return nc

    with ExitStack() as st:
        TBF = min(1024, L)
        wfo = sb("wfo", [128, NJ, D], BF, st)
        load_w(wfo[:, 0:11, :], w_ffn_out, 0, D); load_w(wfo[:, 11:22, :], w_ffn_out, 0, D, r0=11 * 128)
        wg = [sb("wg%d" % i, [128, 8, 128], BF, st) for i in range(2)]
        wu = [sb("wu%d" % i, [128, 8, 128], BF, st) for i in range(2)]
        fT = sb("fT", [128, NJ, TBF], BF, st)
        sgt = [sb("sgt%d" % i, [128, 512], BF, st) for i in range(2)]
        x1l = [sb("x1l%d" % i, [128, D], F32, st) for i in range(2)]
        t2 = [sb("t2%d" % i, [128, D], F32, st) for i in range(2)]
        ot = x1l
        jk2 = sb("jk2", [128, 512], BF, st)
        sf = [sb("sf%d" % i, [128, 8], F32, st) for i in range(2)]
        k = 0
        for blk in range(L // TBF):
            for j in range(NJ):
                b = k % 2; k += 1
                load_w(wg[b][:], w_ffn_in, j * 128, (j + 1) * 128)
                load_w(wu[b][:], w_ffn_in, DFF + j * 128, DFF + (j + 1) * 128)
                for sub in range(TBF // TB):
                    tsl = slice(blk * TBF + sub * TB, blk * TBF + (sub + 1) * TB)
                    pg = bank(); pu = bank()
                    for kc in range(8):
                        mm(pg[:, 0:TB], wg[b][:, kc, :], h2T[:, kc, tsl], start=(kc == 0), stop=(kc == 7))
                    for kc in range(8):
                        mm(pu[:, 0:TB], wu[b][:, kc, :], h2T[:, kc, tsl], start=(kc == 0), stop=(kc == 7))
                    act(sgt[sub % 2][:, 0:TB], pg[:, 0:TB], AF.Silu)
                    tt("dve", fT[:, j, sub * TB:(sub + 1) * TB], pu[:, 0:TB], sgt[sub % 2][:, 0:TB], ALU.mult)
            for tl in range(TBF // 128):
                i = blk * (TBF // 128) + tl
                b2 = i % 2
                rs_ = slice(i * 128, (i + 1) * 128)
                dma("sp", x1l[b2][:], x1_d[rs_, :])
                pfs = []
                for cb in range(2):
                    cs = slice(cb * 512, (cb + 1) * 512)
                    pf = bank(); pfs.append(pf)
                    for j in range(NJ):
                        mm(pf[:, :], fT[:, j, tl * 128:(tl + 1) * 128], wfo[:, j, cs], start=(j == 0), stop=(j == NJ - 1))
                    act(jk2[:], pf[:, :], AF.Square, accum_out=sf[b2][:, cb:cb + 1])
                tt("dve", sf[b2][:, 2:3], sf[b2][:, 0:1], sf[b2][:, 1:2], ALU.add)
                ts("dve", sf[b2][:, 3:4], sf[b2][:, 2:3], 1.0 / D, EPS, ALU.mult, ALU.add)
                act(sf[b2][:, 4:5], sf[b2][:, 3:4], AF.Sqrt)
                I("dve", "reciprocal", out=sf[b2][:, 5:6], in_=sf[b2][:, 4:5])
                for cb in range(2):
                    cs = slice(cb * 512, (cb + 1) * 512)
                    stt(t2[b2][:, cs], pfs[cb][:, :], sf[b2][:, 5:6], gpfB[:, cs], ALU.mult, ALU.mult)
                tt("pool", ot[b2][:], t2[b2][:], x1l[b2][:], ALU.add)
                dma("sp", out[rs_, :], ot[b2][:])
        S.barrier()
    return nc


def _consts():
    eye = np.eye(128, dtype=np.float32)
    k = np.arange(128)
    tri = (k[:, None] <= k[None, :]).astype(np.float32)
    strict = (k[:, None] > k[None, :]).astype(np.float32)
    dm = np.where((k[None, :] // 64) > (k[:, None] // 64), NEG, 0.0).astype(np.float32)
    rel = (127 - np.arange(383)).astype(np.int32)
    half = 16; max_exact = 8
    side = np.where(rel > 0, half, 0)
    n = np.abs(rel)
    ratio = np.maximum(n, max_exact).astype(np.float32) / np.float32(max_exact)
    large = max_exact + (np.log(ratio).astype(np.float32) / np.float32(math.log(128 / max_exact))
                         * np.float32(half - max_exact)).astype(np.int32)
    large = np.minimum(large, half - 1)
    bucket = side + np.where(n < max_exact, n, large)
    oh = np.zeros((32, 383), np.float32)
    oh[bucket, np.arange(383)] = 1.0
    return {"cI4": np.tile(eye, (1, 4)), "cJ": eye[::-1].copy(), "cTRI": tri, "cSTR": strict, "cDM": dm, "cOH": oh}


_WNAMES = ["ada_w", "ada_b", "pre_norm_mix", "post_norm_mix", "pre_norm_ffn", "post_norm_ffn", "w_in", "q_norm",
           "kv_norm", "w_uq", "w_uv", "rel_bias", "w_qidx", "kidx_norm", "conv_w", "conv_b", "dt_bias", "a_log",
           "d_skip", "ssd_norm", "w_o_attn", "w_o_ssd", "w_out", "w_ffn_in", "w_ffn_out"]


def make_in_maps(inputs):
    cst = _consts()
    B = inputs["x"].shape[0]
    shared = {k: np.ascontiguousarray(np.asarray(inputs[k], dtype=np.float32)) for k in _WNAMES}
    shared.update(cst)
    maps = []
    for b in range(B):
        m = dict(shared)
        m["x"] = np.ascontiguousarray(np.asarray(inputs["x"][b], dtype=np.float32))
        m["c"] = np.ascontiguousarray(np.asarray(inputs["c"][b], dtype=np.float32))
        maps.append(m)
    return maps


def kernel(**inputs):
    L = inputs["x"].shape[1]
    nc = build(L)
    maps = make_in_maps(inputs)
    res = run_bass_kernel_spmd(nc, maps, core_ids=list(range(len(maps))))
    return np.stack([np.asarray(r["out"], dtype=np.float32) for r in res.results], axis=0)
```

```python
import math
import numpy as np
import concourse.bass as bass
import concourse.mybir as mybir
from concourse.bass_utils import run_bass_kernel_spmd

F32 = mybir.dt.float32
BF = mybir.dt.bfloat16
AF = mybir.ActivationFunctionType
ALU = mybir.AluOpType
AX = mybir.AxisListType

D = 1024
EPS = 1e-6
NEG = -1.0e30
DSZ = {F32: 4, BF: 2}
C_Q, C_KV, C_KI, C_WI, C_Z, C_XS, C_B, C_C, C_DT, C_GA, C_GB = 0, 256, 384, 448, 464, 2512, 4560, 5584, 6608, 6640, 7664
DFF = 2816
NJ = DFF // 128


def _isap(v):
    return hasattr(v, "tensor") and hasattr(v, "ap") and hasattr(v, "offset")


def _box(ap):
    t = ap.tensor
    esz = DSZ.get(ap.dtype, 4)
    pat = [(int(s), int(c)) for s, c in ap.ap]
    off = int(ap.offset)
    if "DRAM" in str(ap.space).upper() or "HBM" in str(ap.space).upper():
        ext = sum((c - 1) * abs(s) for s, c in pat)
        return (t.name, 0, 1, off * esz, (off + ext + 1) * esz)
    row = 1
    for d in list(t.shape)[1:]:
        row *= int(d)
    p0 = off // row
    f0 = off % row
    pstride, pcnt = pat[0]
    if pstride == 0:
        pcnt = 1
    ext = sum((c - 1) * abs(s) for s, c in pat[1:])
    if "PSUM" in str(ap.space).upper():
        return (t.name, 0, 128, 0, 2048)
    return (t.name, p0, p0 + pcnt, f0 * esz, (f0 + ext + 1) * esz)


def _ovl(a, b):
    return a[1] < b[2] and b[1] < a[2] and a[3] < b[4] and b[3] < a[4]


def _contains(a, b):
    return a[1] <= b[1] and b[2] <= a[2] and a[3] <= b[3] and b[4] <= a[4]


class Sched:
    NDS = 40

    def __init__(self, nc):
        self.nc = nc
        self.engs = {}
        for name, e in (("pe", nc.tensor), ("act", nc.scalar), ("dve", nc.vector),
                        ("pool", nc.gpsimd), ("sp", nc.sync)):
            sem = nc.semaphore("sem_" + name).__enter__()
            self.engs[name] = dict(e=e, sem=sem, cnt=0, seen={}, key="E" + name)
        self.dsems = [nc.semaphore("dsem%d" % i).__enter__() for i in range(self.NDS)]
        self.dcnt = [0] * self.NDS
        self.dk = 0
        self.acc = {}
        self.nins = 0

    def _deps(self, eng, reads, writes):
        need = {}
        rb = [_box(a) for a in reads]
        wb = [_box(a) for a in writes]
        for b in rb:
            psum = b[0].startswith("ps") and b[0][2:].isdigit()
            for rec in self.acc.get(b[0], ()):
                if (rec[0] == "w" or (psum and rec[5] != eng)) and _ovl(rec[1], b):
                    if rec[5] == eng and eng == "pe":
                        continue
                    k = rec[2]
                    if need.get(k, (None, 0))[1] < rec[4]:
                        need[k] = (rec[3], rec[4])
        for b in wb:
            for rec in self.acc.get(b[0], ()):
                if _ovl(rec[1], b):
                    if rec[5] == eng and eng == "pe":
                        continue
                    k = rec[2]
                    if need.get(k, (None, 0))[1] < rec[4]:
                        need[k] = (rec[3], rec[4])
        return need, rb, wb

    def _record(self, eng, rb, wb, key, sem, val):
        for b in wb:
            lst = self.acc.setdefault(b[0], [])
            lst[:] = [r for r in lst if not _contains(b, r[1])]
            lst.append(("w", b, key, sem, val, eng))
        for b in rb:
            lst = self.acc.setdefault(b[0], [])
            lst[:] = [r for r in lst if not (r[0] == "r" and r[5] == eng and r[2] == key and _contains(b, r[1]))]
            lst.append(("r", b, key, sem, val, eng))

    def _wait(self, E, need):
        for k, (sem, val) in need.items():
            if E["seen"].get(k, 0) < val:
                E["e"].wait_ge(sem, val)
                E["seen"][k] = val

    def op(self, eng, fn, reads=(), writes=()):
        E = self.engs[eng]
        need, rb, wb = self._deps(eng, reads, writes)
        self._wait(E, need)
        ins = fn(E["e"])
        E["cnt"] += 1
        ins.then_inc(E["sem"], 1)
        self._record(eng, rb, wb, E["key"], E["sem"], E["cnt"])
        self.nins += 1
        return ins

    def dma(self, q, out, in_, **kw):
        E = self.engs[q]
        k = self.dk % self.NDS
        self.dk += 1
        sem = self.dsems[k]
        key = "D%d" % k
        need, rb, wb = self._deps("dma", [in_], [out])
        if self.dcnt[k] > 0:
            if need.get(key, (None, 0))[1] < self.dcnt[k]:
                need[key] = (sem, self.dcnt[k])
        self._wait(E, need)
        ins = E["e"].dma_start(out=out, in_=in_, **kw)
        self.dcnt[k] += 16
        ins.then_inc(sem, 16)
        self._record("dma", rb, wb, key, sem, self.dcnt[k])
        self.nins += 1
        return ins

    def barrier(self, only=None):
        for name, E in self.engs.items():
            if only and name not in only:
                continue
            need = {}
            for n2, E2 in self.engs.items():
                if n2 != name and E2["cnt"] > 0:
                    need[E2["key"]] = (E2["sem"], E2["cnt"])
            for k in range(self.NDS):
                if self.dcnt[k] > 0:
                    need["D%d" % k] = (self.dsems[k], self.dcnt[k])
            self._wait(E, need)
        if not only:
            self.acc = {}


def _interleave(*gens):
    gens = [g for g in gens if g is not None]
    while gens:
        for g in list(gens):
            try:
                next(g)
            except StopIteration:
                gens.remove(g)


def build(L, stop=99):
    from contextlib import ExitStack
    NT = L // 128
    KSEL = min(256, L // 4)
    TB = min(512, L)
    NB = L // TB
    nc = bass.Bass("TRN2", target_bir_lowering=False)
    S = Sched(nc)

    def dram(name, shape, dt=F32, kind="ExternalInput"):
        return nc.dram_tensor(name, list(shape), dt, kind=kind).ap()

    x = dram("x", [L, D]); cvec = dram("c", [D])
    ada_w = dram("ada_w", [D, 6 * D]); ada_b = dram("ada_b", [6 * D])
    pre_norm_mix = dram("pre_norm_mix", [D]); post_norm_mix = dram("post_norm_mix", [D])
    pre_norm_ffn = dram("pre_norm_ffn", [D]); post_norm_ffn = dram("post_norm_ffn", [D])
    w_in = dram("w_in", [D, 8688]); q_norm = dram("q_norm", [256]); kv_norm = dram("kv_norm", [128])
    w_uq = dram("w_uq", [256, 2048]); w_uv = dram("w_uv", [16, 128, 64]); rel_bias = dram("rel_bias", [32, 16])
    w_qidx = dram("w_qidx", [256, 1024]); kidx_norm = dram("kidx_norm", [64])
    conv_w = dram("conv_w", [4, 4096]); conv_b = dram("conv_b", [4096])
    dt_bias = dram("dt_bias", [32]); a_log = dram("a_log", [32]); d_skip = dram("d_skip", [32])
    ssd_norm = dram("ssd_norm", [2048]); w_o_attn = dram("w_o_attn", [D, D]); w_o_ssd = dram("w_o_ssd", [2048, D])
    w_out = dram("w_out", [D, D]); w_ffn_in = dram("w_ffn_in", [D, 2 * DFF]); w_ffn_out = dram("w_ffn_out", [DFF, D])
    cI4 = dram("cI4", [128, 512]); cJ = dram("cJ", [128, 128]); cTRI = dram("cTRI", [128, 128])
    cSTR = dram("cSTR", [128, 128]); cDM = dram("cDM", [128, 128]); cOH = dram("cOH", [32, 383])
    out = dram("out", [L, D], F32, kind="ExternalOutput")
    tv_d = dram("tv_d", [16, 383], F32, kind="Internal")
    mixa_d = dram("mixa_d", [L, D], BF, kind="Internal")
    yz_d = dram("yz_d", [L, 2048], BF, kind="Internal")
    x1_d = dram("x1_d", [L, D], F32, kind="Internal")

    glob = ExitStack()

    def sb(name, shape, dt=F32, st=None):
        return (st or glob).enter_context(nc.sbuf_tensor(name, list(shape), dt))

    banks = [glob.enter_context(nc.psum_tensor("ps%d" % i, [128, 512], F32)) for i in range(8)]
    bstate = {"k": 0, "free": list(range(8))}

    def bank():
        fr = bstate["free"]
        b = fr[bstate["k"] % len(fr)]
        bstate["k"] += 1
        return banks[b]

    def bfv(bk):
        return bk.bitcast(BF)

    def I(eng, meth, **kw):
        outs = [kw[k] for k in ("out", "accum_out", "ap") if k in kw and _isap(kw[k])]
        ins = [v for k, v in kw.items() if k not in ("out", "accum_out", "ap") and _isap(v)]
        return S.op(eng, lambda e: getattr(e, meth)(**kw), reads=ins, writes=outs)

    def mm(out_, lhsT, rhs, start=True, stop=True):
        return S.op("pe", lambda e: e.matmul(out_, lhsT, rhs, start=start, stop=stop), reads=[lhsT, rhs], writes=[out_])

    def tr(out_, in_, ident):
        return S.op("pe", lambda e: e.transpose(out_, in_, ident), reads=[in_, ident], writes=[out_])

    def act(out_, in_, func, **kw):
        return I("act", "activation", out=out_, in_=in_, func=func, **kw)

    def ts(eng, out_, in0, s1, s2, op0, op1=None):
        if op1 is None:
            return I(eng, "tensor_scalar", out=out_, in0=in0, scalar1=s1, scalar2=None, op0=op0)
        return I(eng, "tensor_scalar", out=out_, in0=in0, scalar1=s1, scalar2=s2, op0=op0, op1=op1)

    def tt(eng, out_, in0, in1, op):
        return I(eng, "tensor_tensor", out=out_, in0=in0, in1=in1, op=op)

    def stt(out_, in0, scalar, in1, op0, op1):
        return I("dve", "scalar_tensor_tensor", out=out_, in0=in0, scalar=scalar, in1=in1, op0=op0, op1=op1)

    def cp(eng, out_, in_):
        if eng == "act":
            return I("act", "copy", out=out_, in_=in_)
        return I(eng, "tensor_copy", out=out_, in_=in_)

    def memset(eng, ap, v):
        return I(eng, "memset", ap=ap, constant=v)

    def dma(q, out_, in_, **kw):
        return S.dma(q, out_, in_, **kw)

    def bc(ap, shape):
        return ap.broadcast_to(list(shape))

    _rr = {"k": 0}

    def evq():
        _rr["k"] += 1
        return "act" if _rr["k"] % 2 else "dve"

    identF = sb("identF", [128, 128]); dma("sp", identF[:], cI4[:, 0:128])
    i4b = sb("i4b", [128, 512], BF)
    identB = i4b[:, 0:128]
    jb = sb("jb", [128, 128], BF)
    triF = sb("triF", [128, 128]); dma("sp", triF[:], cTRI[:, :])
    strF = sb("strF", [128, 128]); dma("sp", strF[:], cSTR[:, :])
    dmF = sb("dmF", [128, 128]); dma("sp", dmF[:], cDM[:, :])
    onesF = sb("onesF", [128, 128]); memset("dve", onesF[:], 1.0)
    onesB = sb("onesB", [128, 128], BF); memset("dve", onesB[:], 1.0)

    lcstg = sb("lcstg", [48, 128], F32)

    def load_cols(dst, vec, n):
        kc = n // 128
        if kc <= 2:
            for k_ in range(kc):
                dma("sp", dst[:, k_:k_ + 1], vec[k_ * 128:(k_ + 1) * 128].rearrange("(p o) -> p o", o=1))
            return
        dma("sp", lcstg[0:kc, :], vec.rearrange("(k p) -> k p", p=128))
        ps = bank()
        tr(ps[:, 0:kc], lcstg[0:kc, :], identF[0:kc, 0:kc])
        cp("dve", dst, ps[:, 0:kc])

    wstg = [sb("wstg%d" % i, [128, 1024], F32) for i in range(3)]
    _ws = {"k": 0}

    def stage():
        _ws["k"] += 1
        return wstg[_ws["k"] % 3]

    def load_w(dst, W, c0, c1, r0=0):
        kcn = dst.shape[1]
        n = c1 - c0
        per = max(1, 1024 // n)
        for k0 in range(0, kcn, per):
            k1 = min(kcn, k0 + per)
            sv = stage()[:, 0:(k1 - k0) * n].rearrange("p (k n) -> p k n", n=n)
            dma("sp", sv, W[r0 + k0 * 128:r0 + k1 * 128, c0:c1].rearrange("(kc p) n -> p kc n", p=128))
            cp("pool", dst[:, k0:k1, :], sv)

    sv_ = stage(); dma("sp", sv_[:, 0:512], cI4[:, :]); cp("pool", i4b[:], sv_[:, 0:512])
    sv_ = stage(); dma("sp", sv_[:, 0:128], cJ[:, :]); cp("pool", jb[:], sv_[:, 0:128])

    modc = sb("modc", [128, 48])
    gcol = sb("gcol", [128, 8]); scol = sb("scol", [128, 8])
    gcol2 = sb("gcol2", [128, 8]); scol2 = sb("scol2", [128, 8])
    gpmB = sb("gpmB", [128, D]); gpfB = sb("gpfB", [128, D])
    with ExitStack() as st:
        ccol = sb("ccol", [128, 8], F32, st); scc = sb("scc", [128, 8], F32, st)
        abcol = sb("abcol", [128, 48], F32, st)
        tmpc = sb("tmpc", [128, 8], F32, st)
        load_cols(ccol[:], cvec, D)
        load_cols(abcol[:], ada_b, 6 * D)
        act(scc[:], ccol[:], AF.Silu)
        awt = [sb("awt%d" % i, [128, 8, 512], F32, st) for i in range(2)]
        awb = [sb("awb%d" % i, [128, 8, 512], BF, st) for i in range(2)]
        sccb = sb("sccb", [128, 8], BF, st)
        cp("dve", sccb[:], scc[:])
        pm = bank()
        for cb in range(12):
            a32 = awt[cb % 2]
            a = awb[cb % 2]
            dma("sp", a32[:], ada_w[:, cb * 512:(cb + 1) * 512].rearrange("(kc p) n -> p kc n", p=128))
            cp("act", a[:, 0:3, :], a32[:, 0:3, :])
            cp("dve", a[:, 3:6, :], a32[:, 3:6, :])
            cp("pool", a[:, 6:8, :], a32[:, 6:8, :])
            for jj in range(4):
                j = cb * 4 + jj
                for kc in range(8):
                    mm(pm[:, j:j + 1], a[:, kc, jj * 128:(jj + 1) * 128], sccb[:, kc:kc + 1], start=(kc == 0), stop=(kc == 7))
        tt("dve", modc[:], pm[:, 0:48], abcol[:], ALU.add)
        pn = sb("pn", [128, 8], F32, st)
        load_cols(pn[:], pre_norm_mix, D)
        ts("dve", tmpc[:], modc[:, 8:16], 1.0, None, ALU.add)
        tt("dve", gcol[:], tmpc[:], pn[:], ALU.mult)
        cp("dve", scol[:], modc[:, 0:8])
        pn2 = sb("pn2", [128, 8], F32, st)
        load_cols(pn2[:], pre_norm_ffn, D)
        ts("dve", tmpc[:], modc[:, 32:40], 1.0, None, ALU.add)
        tt("dve", gcol2[:], tmpc[:], pn2[:], ALU.mult)
        cp("dve", scol2[:], modc[:, 24:32])
        for (dst, gsl, pvec, nm) in ((gpmB, slice(16, 24), post_norm_mix, "a"), (gpfB, slice(40, 48), post_norm_ffn, "b")):
            pcol = sb("pcol" + nm, [128, 8], F32, st)
            gp = sb("gp" + nm, [128, 8], F32, st)
            gb = sb("gb" + nm, [128, 128], F32, st)
            load_cols(pcol[:], pvec, D)
            tt("dve", gp[:], modc[:, gsl], pcol[:], ALU.mult)
            for kc in range(8):
                ts("dve", gb[:], onesF[:], gp[:, kc:kc + 1], None, ALU.mult)
                ps = bank()
                mm(ps[:, 0:128], gb[:], identF[:])
                cp("act", dst[:, kc * 128:(kc + 1) * 128], ps[:, 0:128])
        S.barrier()
    if stop == 0:
        return nc

    import os
    DBG = int(os.environ.get("KDBG", "9"))

    def norm_to_T(xt, dstT, tile, gc, sc, tmp_bf, stat):
        junk = tmp_bf
        act(junk[:], xt, AF.Square, accum_out=stat[:, 0:1])
        if DBG < 2:
            return
        ts("dve", stat[:, 1:2], stat[:, 0:1], 1.0 / D, EPS, ALU.mult, ALU.add)
        act(stat[:, 2:3], stat[:, 1:2], AF.Sqrt)
        I("dve", "reciprocal", out=stat[:, 3:4], in_=stat[:, 2:3])
        if DBG < 3:
            return
        ts("dve", tmp_bf[:], xt, stat[:, 3:4], None, ALU.mult)
        if DBG < 4:
            return
        ps = bank()
        pb = bfv(ps)
        for kc in range(8):
            tr(pb[:, kc * 128:(kc + 1) * 128], tmp_bf[:, kc * 128:(kc + 1) * 128], identB)
        if DBG < 5:
            return
        for kc in range(8):
            if False:
                pass
            else:
                ts("dve", dstT[:, kc, tile * 128:(tile + 1) * 128], pb[:, kc * 128:(kc + 1) * 128],
                   gc[:, kc:kc + 1], sc[:, kc:kc + 1], ALU.mult, ALU.add)

    hT = sb("hT", [128, 8, L], BF)
    stA = ExitStack()
    st = stA
    dl = []
    wq = sb("wq", [128, 8, 256], BF, st); dl.append(lambda: load_w(wq[:], w_in, C_Q, C_Q + 256))
    wkv = sb("wkv", [128, 8, 128], BF, st); dl.append(lambda: load_w(wkv[:], w_in, C_KV, C_KV + 128))
    wkk = sb("wkk", [128, 8, 128], BF, st)
    dl.append(lambda: (load_w(wkk[:, :, 0:64], w_in, C_KI, C_KI + 64), load_w(wkk[:, :, 64:128], w_in, C_KI, C_KI + 64)))
    wwi = sb("wwi", [128, 8, 16], BF, st); dl.append(lambda: load_w(wwi[:], w_in, C_WI, C_WI + 16))
    wuq = sb("wuq", [128, 2, 2048], BF, st)
    for r in range(2):
        dl.append(lambda r=r: (load_w(wuq[:, r:r + 1, 0:1024], w_uq, 0, 1024, r0=r * 128), load_w(wuq[:, r:r + 1, 1024:2048], w_uq, 1024, 2048, r0=r * 128)))
    wqi = sb("wqi", [128, 2, 1024], BF, st)
    for r in range(2):
        dl.append(lambda r=r: load_w(wqi[:, r:r + 1, :], w_qidx, 0, 1024, r0=r * 128))
    wuv = sb("wuv", [128, 16, 128], BF, st)

    def _ld_wuv():
        memset("pool", wuv[:], 0.0)
        sv_ = stage()[:, 0:1024].rearrange("p (h d) -> p h d", d=64)
        dma("sp", sv_, w_uv.rearrange("h c d -> c h d"))
        for h in range(16):
            o = (h % 2) * 64
            cp("pool", wuv[:, h, o:o + 64], sv_[:, h, :])
    dl.append(_ld_wuv)
    woa = sb("woa", [128, 8, D], BF, st)
    for k0 in range(0, 8, 2):
        dl.append(lambda k0=k0: load_w(woa[:, k0:k0 + 2, :], w_o_attn, 0, D, r0=k0 * 128))
    wga = sb("wga", [128, 8, D], BF, st)
    for k0 in range(0, 8, 2):
        dl.append(lambda k0=k0: load_w(wga[:, k0:k0 + 2, :], w_in, C_GA, C_GA + D, r0=k0 * 128))
    qncol = sb("qncol", [128, 2], F32, st); dl.append(lambda: load_cols(qncol[:], q_norm, 256))
    kvcol = sb("kvcol", [128, 1], F32, st); dl.append(lambda: load_cols(kvcol[:], kv_norm, 128))
    kicol = sb("kicol", [128, 1], F32, st)
    dl.append(lambda: (dma("sp", kicol[0:64, :], kidx_norm.rearrange("(p o) -> p o", o=1)),
                       dma("sp", kicol[64:128, :], kidx_norm.rearrange("(p o) -> p o", o=1))))

    with ExitStack() as st:
        xts = [sb("xa%d" % i, [128, D], F32, st) for i in range(2)]
        xbs = [sb("xb%d" % i, [128, D], BF, st) for i in range(2)]
        sts = [sb("xs%d" % i, [128, 4], F32, st) for i in range(2)]
        for i in range(NT):
            xt = xts[i % 2]
            dma("sp", xt[:], x[i * 128:(i + 1) * 128, :])
            if i >= 1:
                for _ in range(3):
                    if dl:
                        dl.pop(0)()
            norm_to_T(xt[:], hT, i, gcol, scol, xbs[i % 2], sts[i % 2])
        while dl:
            dl.pop(0)()
        S.barrier()
    if stop == 1:
        return nc

    with stA as st:
        BFt = sb("BFt", [128, 2, 16, 128], BF, st)
        with ExitStack() as s2:
            rb = sb("rb", [32, 16], F32, s2); dma("sp", rb[:], rel_bias[:, :])
            rb15 = sb("rb15", [32, 16], F32, s2); dma("sp", rb15[:], rel_bias[15, :].partition_broadcast(32))
            tt("dve", rb[:], rb[:], rb15[:], ALU.subtract)
            oh = sb("oh", [32, 383], F32, s2); dma("sp", oh[:], cOH[:, :])
            ps = bank()
            mm(ps[0:16, 0:383], rb[:], oh[:])
            tv = sb("tv", [16, 383], F32, s2)
            cp("dve", tv[:], ps[0:16, 0:383])
            dma("sp", tv_d[:, :], tv[:])
            for dd in range(2):
                src = bass.AP(tensor=tv_d.tensor, offset=128 * dd, ap=[[1, 128], [383, 16], [1, 128]])
                for hh_ in range(2):
                    src = bass.AP(tensor=tv_d.tensor, offset=128 * dd + hh_ * 8 * 383, ap=[[1, 128], [383, 8], [1, 128]])
                    sv_ = stage()[:, 0:1024].rearrange("p (h q) -> p h q", q=128)
                    dma("sp", sv_, src)
                    cp("pool", BFt[:, dd, hh_ * 8:(hh_ + 1) * 8, :], sv_)
            S.barrier()
        if stop == 2:
            return nc

        qnT = sb("qnT", [128, 2, L], BF, st)
        knT = sb("knT", [128, L], BF, st)
        kiT = sb("kiT", [128, L], BF, st)
        with ExitStack() as s2:
            raw = sb("lraw", [128, 2, TB], F32, s2)
            sq = sb("lsq", [128, 2, TB], BF, s2)
            rs = sb("lrs", [128, TB], F32, s2)
            for tb in range(NB):
                tsl = slice(tb * TB, (tb + 1) * TB)
                for (wt, nch, ncol, dst) in ((wq, 2, qncol, qnT), (wkv, 1, kvcol, knT), (wkk, 1, kicol, kiT)):
                    for r in range(nch):
                        ps = bank()
                        for kc in range(8):
                            mm(ps[:, 0:TB], wt[:, kc, r * 128:(r + 1) * 128], hT[:, kc, tsl], start=(kc == 0), stop=(kc == 7))
                        cp("dve", raw[:, r, :], ps[:, 0:TB])
                        act(sq[:, r, :], raw[:, r, :], AF.Square)
                    p2 = bank()
                    for r in range(nch):
                        mm(p2[:, 0:TB], onesB[:], sq[:, r, :], start=(r == 0), stop=(r == nch - 1))
                    ts("dve", rs[:], p2[:, 0:TB], 1.0 / (128 * nch), EPS, ALU.mult, ALU.add)
                    act(rs[:], rs[:], AF.Sqrt)
                    I("dve", "reciprocal", out=rs[:], in_=rs[:])
                    for r in range(nch):
                        o_ = dst[:, r, tsl] if nch == 2 else dst[:, tsl]
                        stt(o_, raw[:, r, :], ncol[:, r:r + 1], rs[:], ALU.mult, ALU.mult)
            S.barrier()
        if stop == 3:
            return nc
        kvtm = sb("kvtm", [128, NT, 128], BF, st)
        witm = sb("witm", [128, NT, 16], F32, st)
        for t in range(NT):
            ps = bank(); pb = bfv(ps)
            tr(pb[:, 0:128], knT[:, t * 128:(t + 1) * 128], identB)
            cp("dve", kvtm[:, t, :], pb[:, 0:128])
            ps = bank()
            for kc in range(8):
                mm(ps[:, 0:16], hT[:, kc, t * 128:(t + 1) * 128], wwi[:, kc, :], start=(kc == 0), stop=(kc == 7))
            act(witm[:, t, :], ps[:, 0:16], AF.Copy, scale=1.0 / 32.0)

        qT = [sb("qT%d" % i, [128, 16, 128], BF, st) for i in range(2)]
        qiT = [sb("qiT%d" % i, [128, 8, 128], BF, st) for i in range(2)]
        accs = [sb("acc0", [128, L], F32, st)] * 2
        works = [sb("work0", [128, L], F32, st)] * 2
        madd = [sb("madd%d" % i, [128, L], BF, st) for i in range(2)]
        m8 = [sb("m8%d" % i, [128, 8], F32, st) for i in range(2)]
        thr = [sb("thr%d" % i, [128, 1], F32, st) for i in range(2)]
        rtmp = [sb("rtmp%d" % i, [128, 512], BF, st) for i in range(4)]
        dws = [sb("dw0", [128, 16, 128], BF, st)] * 2
        pTs = [sb("pT%d" % i, [128, 512], BF, st) for i in range(3)]
        oT = sb("oT", [128, 16, 128], BF, st)
        o2T = sb("o2T", [128, 8, 128], BF, st)
        lnd = sb("lnd", [128, 512], F32, st); rden = sb("rden", [128, 512], F32, st); oun = sb("oun", [128, 512], F32, st)
        sga = [sb("sga%d" % i, [128, 512], BF, st) for i in range(2)]
        mixt = [sb("mixt0", [128, D], BF, st)] * 2
        bstate["free"] = [0, 1, 2, 3, 7]
        bo = [banks[4], banks[4]]; bd = [banks[5], banks[5]]; pacc = banks[6]
        cnt = {"r": 0, "p": 0}
        QS = 128.0 ** -0.5

        def stage1(i):
            b = i % 2
            qs = slice(i * 128, (i + 1) * 128)
            for hq in range(4):
                ps = bank()
                for hh in range(4):
                    h = hq * 4 + hh
                    for r in range(2):
                        mm(ps[:, hh * 128:(hh + 1) * 128], wuq[:, r, h * 128:(h + 1) * 128], qnT[:, r, qs], start=(r == 0), stop=(r == 1))
                act(qT[b][:, hq * 4:(hq + 1) * 4, :].rearrange("p h q -> p (h q)"), ps[:, :], AF.Copy, scale=QS)
            yield
            for mq in range(2):
                ps = bank()
                for mm_ in range(4):
                    m = mq * 4 + mm_
                    for r in range(2):
                        mm(ps[:, mm_ * 128:(mm_ + 1) * 128], wqi[:, r, m * 128:(m + 1) * 128], qnT[:, r, qs], start=(r == 0), stop=(r == 1))
                cp("act", qiT[b][:, mq * 4:(mq + 1) * 4, :].rearrange("p h q -> p (h q)"), ps[:, :])
            yield
            nk = (i + 1) * 128
            acc = accs[b]
            dw = dws[b]
            for h in range(16):
                ts("pool", dw[:, h, :], identF[:], witm[:, i, h:h + 1], None, ALU.mult)
            for kb in range((nk + 511) // 512):
                n = min(512, nk - kb * 512)
                ks = slice(kb * 512, kb * 512 + n)
                pend = []

                def accmm(h, rt):
                    mm(pacc[:, 0:n], dw[:, h, :], rt[:, 0:n], start=(h == 0), stop=(h == 15))

                for h in range(16):
                    m, half = h // 2, h % 2
                    pr = slice(half * 64, half * 64 + 64)
                    ps = bank()
                    mm(ps[:, 0:n], qiT[b][pr, m, :], kiT[pr, ks])
                    rt = rtmp[cnt["r"] % 4]; cnt["r"] += 1
                    act(rt[:, 0:n], ps[:, 0:n], AF.Relu)
                    pend.append((h, rt))
                    if len(pend) > 2:
                        accmm(*pend.pop(0))
                    if h % 4 == 3:
                        yield
                while pend:
                    accmm(*pend.pop(0))
                cp("act", acc[:, ks], pacc[:, 0:n])
                yield
            tt("pool", acc[:, i * 128:(i + 1) * 128], acc[:, i * 128:(i + 1) * 128], dmF[:], ALU.add)
            if nk > KSEL:
                src = acc
                rounds = KSEL // 8
                for r in range(rounds):
                    I("dve", "max", out=m8[b][:], in_=src[:, 0:nk])
                    if r < rounds - 1:
                        I("dve", "match_replace", out=works[b][:, 0:nk], in_to_replace=m8[b][:], in_values=src[:, 0:nk], imm_value=NEG)
                        src = works[b]
                    if r % 2 == 1:
                        yield
                ts("dve", thr[b][:], m8[b][:, 7:8], -1.0e29, None, ALU.max)
            else:
                memset("dve", thr[b][:], -1.0e29)
            ts("dve", madd[b][:, 0:nk], acc[:, 0:nk], thr[b][:, 0:1], -30000.0, ALU.is_lt, ALU.mult)
            yield

        def stage2(i):
            b = i % 2
            qs = slice(i * 128, (i + 1) * 128)
            its = [(hg, j) for hg in range(4) for j in range(i + 1)]
            pend = []

            def logits(hg, j):
                dd = i - j
                pl = bank()
                mm(pl[:, :], knT[:, j * 128:(j + 1) * 128], qT[b][:, hg * 4:(hg + 1) * 4, :].rearrange("p h q -> p (h q)"), start=True, stop=False)
                mm(pl[:, :], madd[b][:, j * 128:(j + 1) * 128], i4b[:, :], start=False, stop=(dd >= 2))
                if dd < 2:
                    mm(pl[:, :], jb[:], BFt[:, dd, hg * 4:(hg + 1) * 4, :].rearrange("p h q -> p (h q)"), start=False, stop=True)
                pT = pTs[cnt["p"] % 3]; cnt["p"] += 1
                act(pT[:], pl[:, :], AF.Exp)
                return pT

            def pv(hg, j, pT):
                po = bo[0]; pd = bd[0]
                mm(po[:, :], kvtm[:, j, :], pT[:], start=(j == 0), stop=(j == i))
                mm(pd[:, :], onesB[:], pT[:], start=(j == 0), stop=(j == i))
                if j == i:
                    act(lnd[:], pd[:, :], AF.Ln)
                    act(rden[:], lnd[:], AF.Exp, scale=-1.0)
                    cp("act", oun[:], po[:, :])
                    tt("pool", oT[:, hg * 4:(hg + 1) * 4, :].rearrange("p h q -> p (h q)"), oun[:], rden[:], ALU.mult)

            for t_, (hg, j) in enumerate(its):
                pend.append((hg, j, logits(hg, j)))
                if len(pend) > 2:
                    pv(*pend.pop(0))
                if t_ % 4 == 3:
                    yield
            while pend:
                pv(*pend.pop(0))
            yield
            for mq in range(2):
                ps = bank()
                for mm_ in range(4):
                    m = mq * 4 + mm_
                    mm(ps[:, mm_ * 128:(mm_ + 1) * 128], wuv[:, 2 * m, :], oT[:, 2 * m, :], start=True, stop=False)
                    mm(ps[:, mm_ * 128:(mm_ + 1) * 128], wuv[:, 2 * m + 1, :], oT[:, 2 * m + 1, :], start=False, stop=True)
                cp("act", o2T[:, mq * 4:(mq + 1) * 4, :].rearrange("p h q -> p (h q)"), ps[:, :])
            yield
            mx = mixt[b]
            for cb in range(2):
                cs = slice(cb * 512, (cb + 1) * 512)
                pg = bank()
                for kc in range(8):
                    mm(pg[:, :], hT[:, kc, qs], wga[:, kc, cs], start=(kc == 0), stop=(kc == 7))
                act(sga[cb][:], pg[:, :], AF.Sigmoid)
                py = bank()
                for m in range(8):
                    mm(py[:, :], o2T[:, m, :], woa[:, m, cs], start=(m == 0), stop=(m == 7))
                tt("dve", mx[:, cs], py[:, :], sga[cb][:], ALU.mult)
                yield
            dma("sp", mixa_d[qs, :], mx[:])
            yield

        prev = None
        for i in range(NT + 1):
            g1 = stage1(i) if i < NT else None
            g2 = stage2(i - 1) if i >= 1 else None
            _interleave(g1, g2)
        bstate["free"] = list(range(8))
        S.barrier()
        if stop == 4:
            return nc

    ssq = sb("ssq", [128, NT, 8])
    rstds = sb("rstds", [128, NT])
    with ExitStack() as st:
        dtall = sb("dtall", [128, NT, 32], F32, st); dAall = sb("dAall", [128, NT, 32], F32, st)
        eaall = sb("eaall", [128, NT, 32], F32, st); decall = sb("decall", [128, NT, 32], F32, st)
        cdall = sb("cdall", [128, NT, 32], F32, st)
        dI = sb("dI", [128, 32, 128], BF, st)
        cwc = sb("cwc", [128, 4, 32], F32, st); cbc = sb("cbc", [128, 32], F32, st)
        for k in range(4):
            load_cols(cwc[:, k, :], conv_w[k, :], 4096)
        load_cols(cbc[:], conv_b, 4096)
        with ExitStack() as s2:
            wdt = sb("wdt", [128, 8, 32], BF, s2); load_w(wdt[:], w_in, C_DT, C_DT + 32)
            dtbB = sb("dtbB", [128, 32], F32, s2); dma("sp", dtbB[:], dt_bias.partition_broadcast(128))
            AB = sb("AB", [128, 32], F32, s2); dma("sp", AB[:], a_log.partition_broadcast(128))
            dsB = sb("dsB", [128, 32], F32, s2); dma("sp", dsB[:], d_skip.partition_broadcast(128))
            act(AB[:], AB[:], AF.Exp)
            ts("dve", AB[:], AB[:], -1.0, None, ALU.mult)
            for h in range(32):
                ts("pool", dI[:, h, :], identF[:], dsB[:, h:h + 1], None, ALU.mult)
            xb_ = sb("xb_", [128, 32], F32, s2); ax = sb("ax_", [128, 32], F32, s2)
            ee = sb("ee_", [128, 32], F32, s2); tot = sb("tot_", [128, 32], F32, s2)
            for t in range(NT):
                ps = bank()
                for kc in range(8):
                    mm(ps[:, 0:32], hT[:, kc, t * 128:(t + 1) * 128], wdt[:, kc, :], start=(kc == 0), stop=(kc == 7))
                tt("dve", xb_[:], ps[:, 0:32], dtbB[:], ALU.add)
                act(ax[:], xb_[:], AF.Abs)
                act(ee[:], ax[:], AF.Exp, scale=-1.0)
                ts("dve", ee[:], ee[:], 1.0, None, ALU.add)
                act(ee[:], ee[:], AF.Ln)
                stt(dtall[:, t, :], xb_[:], 0.0, ee[:], ALU.max, ALU.add)
                tt("dve", dAall[:, t, :], dtall[:, t, :], AB[:], ALU.mult)
                p1 = bank(); p2 = bank()
                mm(p1[:, 0:32], triF[:], dAall[:, t, :])
                mm(p2[:, 0:32], onesF[:], dAall[:, t, :])
                act(eaall[:, t, :], p1[:, 0:32], AF.Exp)
                act(cdall[:, t, :], p2[:, 0:32], AF.Exp)
                cp("act", tot[:], p2[:, 0:32])
                tt("dve", tot[:], tot[:], p1[:, 0:32], ALU.subtract)
                act(decall[:, t, :], tot[:], AF.Exp)
            S.barrier()

        wxs = [sb("wxs%d" % i, [128, 8, 256], BF, st) for i in range(2)]
        wBs = [sb("wBs%d" % i, [128, 8, 128], BF, st) for i in range(2)]
        wCs = [sb("wCs%d" % i, [128, 8, 128], BF, st) for i in range(2)]
        wzs = [sb("wzs%d" % i, [128, 8, 256], BF, st) for i in range(2)]
        raw = sb("raw", [128, 4, 3 + L], BF, st)
        memset("pool", raw[:, :, 0:3], 0.0)
        cv = sb("cv", [128, 4, L], BF, st)
        cvf = [sb("cvf%d" % i, [128, L], F32, st) for i in range(2)]
        xstm = sb("xstm", [128, NT, 256], BF, st); Btm = sb("Btm", [128, NT, 128], BF, st)
        sztm = sb("sztm", [128, NT, 256], BF, st)
        prevs = sb("prevs", [128, 256], F32, st); prevb = sb("prevb", [128, 256], BF, st)
        cbm = [sb("cbm%d" % i, [128, 128], BF, st) for i in range(2)]
        L4 = [sb("L4%d" % i, [128, 4, 128], BF, st) for i in range(2)]
        triB = sb("triB", [128, 128], BF, st); cp("pool", triB[:], triF[:])
        E4 = [sb("E4%d" % i, [128, 4, 128], BF, st) for i in range(2)]
        M4 = [sb("M4%d" % i, [128, 4, 128], BF, st) for i in range(2)]
        X4 = [sb("X4%d" % i, [128, 4, 64], BF, st) for i in range(2)]
        Xd4 = [sb("Xd4%d" % i, [128, 4, 64], BF, st) for i in range(2)]
        yo = [sb("yo%d" % i, [128, 4, 64], F32, st) for i in range(2)]
        y1 = [sb("y1%d" % i, [128, 256], F32, st) for i in range(2)]
        yzt = [sb("yzt%d" % i, [128, 256], BF, st) for i in range(2)]
        yjunk = sb("yjunk", [128, 256], BF, st)
        ptmp = sb("ptmp", [128, 4, 64], F32, st)

        def load_group_w(g):
            b = g % 2
            load_w(wxs[b][:], w_in, C_XS + g * 256, C_XS + (g + 1) * 256)
            load_w(wBs[b][:], w_in, C_B + g * 128, C_B + (g + 1) * 128)
            load_w(wCs[b][:], w_in, C_C + g * 128, C_C + (g + 1) * 128)
            load_w(wzs[b][:], w_in, C_Z + g * 256, C_Z + (g + 1) * 256)

        load_group_w(0)
        for g in range(8):
            b = g % 2
            h0 = g * 4
            if g + 1 < 8:
                load_group_w(g + 1)
            chunks = (2 * g, 2 * g + 1, 16 + g, 24 + g)
            for r in range(4):
                wt = (wxs[b][:, :, 0:128], wxs[b][:, :, 128:256], wBs[b][:, :, :], wCs[b][:, :, :])[r]
                for tb in range(NB):
                    ps = bank()
                    for kc in range(8):
                        mm(ps[:, 0:TB], wt[:, kc, :], hT[:, kc, tb * TB:(tb + 1) * TB], start=(kc == 0), stop=(kc == 7))
                    cp(evq(), raw[:, r, 3 + tb * TB:3 + (tb + 1) * TB], ps[:, 0:TB])
            for r in range(4):
                ch = chunks[r]
                cf = cvf[r % 2]
                ts("dve", cf[:], raw[:, r, 3:3 + L], cwc[:, 3, ch:ch + 1], cbc[:, ch:ch + 1], ALU.mult, ALU.add)
                for k in range(3):
                    stt(cf[:], raw[:, r, k:k + L], cwc[:, k, ch:ch + 1], cf[:], ALU.mult, ALU.add)
                act(cv[:, r, :], cf[:], AF.Silu)
            for t in range(NT):
                ps = bank(); pb = bfv(ps)
                tsl = slice(t * 128, (t + 1) * 128)
                tr(pb[:, 0:128], cv[:, 0, tsl], identB); tr(pb[:, 128:256], cv[:, 1, tsl], identB)
                tr(pb[:, 256:384], cv[:, 2, tsl], identB)
                e_ = evq()
                cp(e_, xstm[:, t, :], pb[:, 0:256])
                cp(e_, Btm[:, t, :], pb[:, 256:384])
                pz = bank()
                for kc in range(8):
                    mm(pz[:, 0:256], hT[:, kc, tsl], wzs[b][:, kc, :], start=(kc == 0), stop=(kc == 7))
                act(sztm[:, t, :], pz[:, 0:256], AF.Silu)
            memset("pool", prevs[:], 0.0); memset("pool", prevb[:], 0.0)
            hs = slice(h0, h0 + 4)

            def stX(c):
                cc = c % 2
                csl = slice(c * 128, (c + 1) * 128)
                pcb = bank()
                mm(pcb[:, 0:128], cv[:, 2, csl], cv[:, 3, csl])
                tt("dve", cbm[cc][:], pcb[:, 0:128], triF[:], ALU.mult)
                tt("pool", L4[cc][:], bc(strF[:].unsqueeze(1), [128, 4, 128]), bc(dAall[:, c, hs].unsqueeze(2), [128, 4, 128]), ALU.mult)
                pseg = bank()
                for hh in range(4):
                    mm(pseg[:, hh * 128:(hh + 1) * 128], L4[cc][:, hh, :], triB[:])
                act(E4[cc][:].rearrange("p h l -> p (h l)"), pseg[:, :], AF.Exp)
                tt("pool", M4[cc][:], E4[cc][:], bc(cbm[cc][:].unsqueeze(1), [128, 4, 128]), ALU.mult)
                xv = xstm[:, c, :].rearrange("p (h d) -> p h d", h=4)
                tt("dve", X4[cc][:], xv, bc(dtall[:, c, hs].unsqueeze(2), [128, 4, 64]), ALU.mult)
                tt("pool", Xd4[cc][:], X4[cc][:], bc(decall[:, c, hs].unsqueeze(2), [128, 4, 64]), ALU.mult)

            def stY(c):
                cc = c % 2
                csl = slice(c * 128, (c + 1) * 128)
                py = bank()
                for hh in range(4):
                    mm(py[:, hh * 64:(hh + 1) * 64], M4[cc][:, hh, :], X4[cc][:, hh, :], start=True, stop=False)
                    mm(py[:, hh * 64:(hh + 1) * 64], dI[:, h0 + hh, :], xstm[:, c, hh * 64:(hh + 1) * 64], start=False, stop=True)
                pS = bank()
                mm(pS[:, 0:256], Btm[:, c, :], Xd4[cc][:].rearrange("p h d -> p (h d)"))
                pyo = bank()
                mm(pyo[:, 0:256], cv[:, 3, csl], prevb[:])
                tt("dve", yo[cc][:], pyo[:, 0:256].rearrange("p (h d) -> p h d", h=4), bc(eaall[:, c, hs].unsqueeze(2), [128, 4, 64]), ALU.mult)
                tt("dve", y1[cc][:], py[:, 0:256], yo[cc][:].rearrange("p h d -> p (h d)"), ALU.add)
                tt("pool", yzt[cc][:], y1[cc][:], sztm[:, c, :], ALU.mult)
                act(yjunk[:], yzt[cc][:], AF.Square, accum_out=ssq[:, c, g:g + 1])
                dma("sp", yz_d[csl, g * 256:(g + 1) * 256], yzt[cc][:])
                tt("pool", ptmp[:], prevs[:].rearrange("p (h d) -> p h d", h=4), bc(cdall[:, c, hs].unsqueeze(2), [128, 4, 64]), ALU.mult)
                tt("dve", prevs[:], ptmp[:].rearrange("p h d -> p (h d)"), pS[:, 0:256], ALU.add)
                cp("act", prevb[:], prevs[:])

            for c in range(NT + 1):
                if c < NT:
                    stX(c)
                if c >= 1:
                    stY(c - 1)
        I("dve", "tensor_reduce", out=rstds[:], in_=ssq[:], axis=AX.X, op=ALU.add)
        ts("dve", rstds[:], rstds[:], 1.0 / 2048, EPS, ALU.mult, ALU.add)
        act(rstds[:], rstds[:], AF.Sqrt)
        I("dve", "reciprocal", out=rstds[:], in_=rstds[:])
        S.barrier()
        if stop == 5:
            return nc

    h2T = sb("h2T", [128, 8, L], BF)
    with ExitStack() as st:
        wos = sb("wos", [128, 16, D], BF, st); load_w(wos[:, 0:8, :], w_o_ssd, 0, D); load_w(wos[:, 8:16, :], w_o_ssd, 0, D, r0=1024)
        wout = sb("wout", [128, 8, D], BF, st); load_w(wout[:], w_out, 0, D)
        wgb = sb("wgb", [128, 8, D], BF, st); load_w(wgb[:], w_in, C_GB, C_GB + D)
        sncol = sb("sncol", [128, 16], F32, st); load_cols(sncol[:], ssd_norm, 2048)
        yzl = [sb("yzl%d" % i, [128, 2048], BF, st) for i in range(2)]
        yT = [sb("yT0", [128, 16, 128], BF, st)] * 2
        sgb = [sb("sgb%d" % i, [128, 512], BF, st) for i in range(2)]
        mal = [sb("mal%d" % i, [128, D], BF, st) for i in range(2)]
        mpf = [sb("mpf0", [128, D], F32, st)] * 2
        mp = [sb("mp0", [128, D], BF, st)] * 2
        mpT = [sb("mpT%d" % i, [128, 8, 128], BF, st) for i in range(2)]
        xl = [sb("xl%d" % i, [128, D], F32, st) for i in range(2)]
        t1 = [sb("t10", [128, D], F32, st)] * 2
        x1t = xl
        jk = sb("jk", [128, 512], BF, st)
        so = [sb("so%d" % i, [128, 8], F32, st) for i in range(2)]
        xb2 = [sb("xb20", [128, D], BF, st)] * 2
        st2 = [sb("st2%d" % i, [128, 4], F32, st) for i in range(2)]
        def stageA(i):
            b = i % 2
            rs_ = slice(i * 128, (i + 1) * 128)
            dma("sp", yzl[b][:], yz_d[rs_, :])
            dma("sp", mal[b][:], mixa_d[rs_, :])
            for half in range(2):
                ps = bank(); pb = bfv(ps)
                for k8 in range(8):
                    kc = half * 8 + k8
                    tr(pb[:, k8 * 128:(k8 + 1) * 128], yzl[b][:, kc * 128:(kc + 1) * 128], identB)
                for k8 in range(8):
                    kc = half * 8 + k8
                    ts("dve", yT[b][:, kc, :], pb[:, k8 * 128:(k8 + 1) * 128], sncol[:, kc:kc + 1], None, ALU.mult)
            for cb in range(2):
                cs = slice(cb * 512, (cb + 1) * 512)
                pg = bank()
                for kc in range(8):
                    mm(pg[:, :], hT[:, kc, rs_], wgb[:, kc, cs], start=(kc == 0), stop=(kc == 7))
                act(sgb[cb][:], pg[:, :], AF.Sigmoid)
                py = bank()
                for kc in range(16):
                    mm(py[:, :], yT[b][:, kc, :], wos[:, kc, cs], start=(kc == 0), stop=(kc == 15))
                stt(mpf[b][:, cs], py[:, :], rstds[:, i:i + 1], sgb[cb][:], ALU.mult, ALU.mult)
            tt("dve", mp[b][:], mpf[b][:], mal[b][:], ALU.add)
            ps = bank(); pb = bfv(ps)
            for kc in range(8):
                tr(pb[:, kc * 128:(kc + 1) * 128], mp[b][:, kc * 128:(kc + 1) * 128], identB)
            cp("act", mpT[b][:].rearrange("p k t -> p (k t)"), pb[:, :])

        def stageB(i):
            b = i % 2
            rs_ = slice(i * 128, (i + 1) * 128)
            dma("sp", xl[b][:], x[rs_, :])
            pms = []
            for cb in range(2):
                cs = slice(cb * 512, (cb + 1) * 512)
                pm_ = bank(); pms.append(pm_)
                for kc in range(8):
                    mm(pm_[:, :], mpT[b][:, kc, :], wout[:, kc, cs], start=(kc == 0), stop=(kc == 7))
                act(jk[:], pm_[:, :], AF.Square, accum_out=so[b][:, cb:cb + 1])
            tt("dve", so[b][:, 2:3], so[b][:, 0:1], so[b][:, 1:2], ALU.add)
            ts("dve", so[b][:, 3:4], so[b][:, 2:3], 1.0 / D, EPS, ALU.mult, ALU.add)
            act(so[b][:, 4:5], so[b][:, 3:4], AF.Sqrt)
            I("dve", "reciprocal", out=so[b][:, 5:6], in_=so[b][:, 4:5])
            for cb in range(2):
                cs = slice(cb * 512, (cb + 1) * 512)
                stt(t1[b][:, cs], pms[cb][:, :], so[b][:, 5:6], gpmB[:, cs], ALU.mult, ALU.mult)
            tt("dve", x1t[b][:], t1[b][:], xl[b][:], ALU.add)
            dma("sp", x1_d[rs_, :], x1t[b][:])
            norm_to_T(x1t[b][:], h2T, i, gcol2, scol2, xb2[b], st2[b])

        for i in range(NT + 1):
            if i < NT:
                stageA(i)
            if i >= 1:
                stageB(i - 1)
        S.barrier()
        if stop == 6:
            return nc

    with ExitStack() as st:
        TBF = min(1024, L)
        wfo = sb("wfo", [128, NJ, D], BF, st)
        load_w(wfo[:, 0:11, :], w_ffn_out, 0, D); load_w(wfo[:, 11:22, :], w_ffn_out, 0, D, r0=11 * 128)
        wg = [sb("wg%d" % i, [128, 8, 128], BF, st) for i in range(2)]
        wu = [sb("wu%d" % i, [128, 8, 128], BF, st) for i in range(2)]
        fT = sb("fT", [128, NJ, TBF], BF, st)
        sgt = [sb("sgt%d" % i, [128, 512], BF, st) for i in range(2)]
        x1l = [sb("x1l%d" % i, [128, D], F32, st) for i in range(2)]
        t2 = [sb("t2%d" % i, [128, D], F32, st) for i in range(2)]
        ot = x1l
        jk2 = sb("jk2", [128, 512], BF, st)
        sf = [sb("sf%d" % i, [128, 8], F32, st) for i in range(2)]
        k = 0
        for blk in range(L // TBF):
            for j in range(NJ):
                b = k % 2; k += 1
                load_w(wg[b][:], w_ffn_in, j * 128, (j + 1) * 128)
                load_w(wu[b][:], w_ffn_in, DFF + j * 128, DFF + (j + 1) * 128)
                for sub in range(TBF // TB):
                    tsl = slice(blk * TBF + sub * TB, blk * TBF + (sub + 1) * TB)
                    pg = bank(); pu = bank()
                    for kc in range(8):
                        mm(pg[:, 0:TB], wg[b][:, kc, :], h2T[:, kc, tsl], start=(kc == 0), stop=(kc == 7))
                    for kc in range(8):
                        mm(pu[:, 0:TB], wu[b][:, kc, :], h2T[:, kc, tsl], start=(kc == 0), stop=(kc == 7))
                    act(sgt[sub % 2][:, 0:TB], pg[:, 0:TB], AF.Silu)
                    tt("dve", fT[:, j, sub * TB:(sub + 1) * TB], pu[:, 0:TB], sgt[sub % 2][:, 0:TB], ALU.mult)
            for tl in range(TBF // 128):
                i = blk * (TBF // 128) + tl
                b2 = i % 2
                rs_ = slice(i * 128, (i + 1) * 128)
                dma("sp", x1l[b2][:], x1_d[rs_, :])
                pfs = []
                for cb in range(2):
                    cs = slice(cb * 512, (cb + 1) * 512)
                    pf = bank(); pfs.append(pf)
                    for j in range(NJ):
                        mm(pf[:, :], fT[:, j, tl * 128:(tl + 1) * 128], wfo[:, j, cs], start=(j == 0), stop=(j == NJ - 1))
                    act(jk2[:], pf[:, :], AF.Square, accum_out=sf[b2][:, cb:cb + 1])
                tt("dve", sf[b2][:, 2:3], sf[b2][:, 0:1], sf[b2][:, 1:2], ALU.add)
                ts("dve", sf[b2][:, 3:4], sf[b2][:, 2:3], 1.0 / D, EPS, ALU.mult, ALU.add)
                act(sf[b2][:, 4:5], sf[b2][:, 3:4], AF.Sqrt)
                I("dve", "reciprocal", out=sf[b2][:, 5:6], in_=sf[b2][:, 4:5])
                for cb in range(2):
                    cs = slice(cb * 512, (cb + 1) * 512)
                    stt(t2[b2][:, cs], pfs[cb][:, :], sf[b2][:, 5:6], gpfB[:, cs], ALU.mult, ALU.mult)
                tt("pool", ot[b2][:], t2[b2][:], x1l[b2][:], ALU.add)
                dma("sp", out[rs_, :], ot[b2][:])
        S.barrier()
    return nc


def _consts():
    eye = np.eye(128, dtype=np.float32)
    k = np.arange(128)
    tri = (k[:, None] <= k[None, :]).astype(np.float32)
    strict = (k[:, None] > k[None, :]).astype(np.float32)
    dm = np.where((k[None, :] // 64) > (k[:, None] // 64), NEG, 0.0).astype(np.float32)
    rel = (127 - np.arange(383)).astype(np.int32)
    half = 16; max_exact = 8
    side = np.where(rel > 0, half, 0)
    n = np.abs(rel)
    ratio = np.maximum(n, max_exact).astype(np.float32) / np.float32(max_exact)
    large = max_exact + (np.log(ratio).astype(np.float32) / np.float32(math.log(128 / max_exact))
                         * np.float32(half - max_exact)).astype(np.int32)
    large = np.minimum(large, half - 1)
    bucket = side + np.where(n < max_exact, n, large)
    oh = np.zeros((32, 383), np.float32)
    oh[bucket, np.arange(383)] = 1.0
    return {"cI4": np.tile(eye, (1, 4)), "cJ": eye[::-1].copy(), "cTRI": tri, "cSTR": strict, "cDM": dm, "cOH": oh}


_WNAMES = ["ada_w", "ada_b", "pre_norm_mix", "post_norm_mix", "pre_norm_ffn", "post_norm_ffn", "w_in", "q_norm",
           "kv_norm", "w_uq", "w_uv", "rel_bias", "w_qidx", "kidx_norm", "conv_w", "conv_b", "dt_bias", "a_log",
           "d_skip", "ssd_norm", "w_o_attn", "w_o_ssd", "w_out", "w_ffn_in", "w_ffn_out"]


def make_in_maps(inputs):
    cst = _consts()
    B = inputs["x"].shape[0]
    shared = {k: np.ascontiguousarray(np.asarray(inputs[k], dtype=np.float32)) for k in _WNAMES}
    shared.update(cst)
    maps = []
    for b in range(B):
        m = dict(shared)
        m["x"] = np.ascontiguousarray(np.asarray(inputs["x"][b], dtype=np.float32))
        m["c"] = np.ascontiguousarray(np.asarray(inputs["c"][b], dtype=np.float32))
        maps.append(m)
    return maps


def kernel(**inputs):
    L = inputs["x"].shape[1]
    nc = build(L)
    maps = make_in_maps(inputs)
    res = run_bass_kernel_spmd(nc, maps, core_ids=list(range(len(maps))))
    return np.stack([np.asarray(r["out"], dtype=np.float32) for r in res.results], axis=0)
```

```python
import math
import numpy as np
import concourse.bass as bass
import concourse.mybir as mybir
from concourse.bass_utils import run_bass_kernel_spmd

F32 = mybir.dt.float32
BF = mybir.dt.bfloat16
AF = mybir.ActivationFunctionType
ALU = mybir.AluOpType
AX = mybir.AxisListType

D = 1024
EPS = 1e-6
NEG = -1.0e30
DSZ = {F32: 4, BF: 2}
C_Q, C_KV, C_KI, C_WI, C_Z, C_XS, C_B, C_C, C_DT, C_GA, C_GB = 0, 256, 384, 448, 464, 2512, 4560, 5584, 6608, 6640, 7664
DFF = 2816
NJ = DFF // 128


def _isap(v):
    return hasattr(v, "tensor") and hasattr(v, "ap") and hasattr(v, "offset")


def _box(ap):
    t = ap.tensor
    esz = DSZ.get(ap.dtype, 4)
    pat = [(int(s), int(c)) for s, c in ap.ap]
    off = int(ap.offset)
    if "DRAM" in str(ap.space).upper() or "HBM" in str(ap.space).upper():
        ext = sum((c - 1) * abs(s) for s, c in pat)
        return (t.name, 0, 1, off * esz, (off + ext + 1) * esz)
    row = 1
    for d in list(t.shape)[1:]:
        row *= int(d)
    p0 = off // row
    f0 = off % row
    pstride, pcnt = pat[0]
    if pstride == 0:
        pcnt = 1
    ext = sum((c - 1) * abs(s) for s, c in pat[1:])
    if "PSUM" in str(ap.space).upper():
        return (t.name, 0, 128, 0, 2048)
    return (t.name, p0, p0 + pcnt, f0 * esz, (f0 + ext + 1) * esz)


def _ovl(a, b):
    return a[1] < b[2] and b[1] < a[2] and a[3] < b[4] and b[3] < a[4]


def _contains(a, b):
    return a[1] <= b[1] and b[2] <= a[2] and a[3] <= b[3] and b[4] <= a[4]


class Sched:
    NDS = 40

    def __init__(self, nc):
        self.nc = nc
        self.engs = {}
        for name, e in (("pe", nc.tensor), ("act", nc.scalar), ("dve", nc.vector),
                        ("pool", nc.gpsimd), ("sp", nc.sync)):
            sem = nc.semaphore("sem_" + name).__enter__()
            self.engs[name] = dict(e=e, sem=sem, cnt=0, seen={}, key="E" + name)
        self.dsems = [nc.semaphore("dsem%d" % i).__enter__() for i in range(self.NDS)]
        self.dcnt = [0] * self.NDS
        self.dk = 0
        self.acc = {}
        self.nins = 0

    def _deps(self, eng, reads, writes):
        need = {}
        rb = [_box(a) for a in reads]
        wb = [_box(a) for a in writes]
        for b in rb:
            psum = b[0].startswith("ps") and b[0][2:].isdigit()
            for rec in self.acc.get(b[0], ()):
                if (rec[0] == "w" or (psum and rec[5] != eng)) and _ovl(rec[1], b):
                    if rec[5] == eng and eng == "pe":
                        continue
                    k = rec[2]
                    if need.get(k, (None, 0))[1] < rec[4]:
                        need[k] = (rec[3], rec[4])
        for b in wb:
            for rec in self.acc.get(b[0], ()):
                if _ovl(rec[1], b):
                    if rec[5] == eng and eng == "pe":
                        continue
                    k = rec[2]
                    if need.get(k, (None, 0))[1] < rec[4]:
                        need[k] = (rec[3], rec[4])
        return need, rb, wb

    def _record(self, eng, rb, wb, key, sem, val):
        for b in wb:
            lst = self.acc.setdefault(b[0], [])
            lst[:] = [r for r in lst if not _contains(b, r[1])]
            lst.append(("w", b, key, sem, val, eng))
        for b in rb:
            lst = self.acc.setdefault(b[0], [])
            lst[:] = [r for r in lst if not (r[0] == "r" and r[5] == eng and r[2] == key and _contains(b, r[1]))]
            lst.append(("r", b, key, sem, val, eng))

    def _wait(self, E, need):
        for k, (sem, val) in need.items():
            if E["seen"].get(k, 0) < val:
                E["e"].wait_ge(sem, val)
                E["seen"][k] = val

    def op(self, eng, fn, reads=(), writes=()):
        E = self.engs[eng]
        need, rb, wb = self._deps(eng, reads, writes)
        self._wait(E, need)
        ins = fn(E["e"])
        E["cnt"] += 1
        ins.then_inc(E["sem"], 1)
        self._record(eng, rb, wb, E["key"], E["sem"], E["cnt"])
        self.nins += 1
        return ins

    def dma(self, q, out, in_, **kw):
        E = self.engs[q]
        k = self.dk % self.NDS
        self.dk += 1
        sem = self.dsems[k]
        key = "D%d" % k
        need, rb, wb = self._deps("dma", [in_], [out])
        if self.dcnt[k] > 0:
            if need.get(key, (None, 0))[1] < self.dcnt[k]:
                need[key] = (sem, self.dcnt[k])
        self._wait(E, need)
        ins = E["e"].dma_start(out=out, in_=in_, **kw)
        self.dcnt[k] += 16
        ins.then_inc(sem, 16)
        self._record("dma", rb, wb, key, sem, self.dcnt[k])
        self.nins += 1
        return ins

    def barrier(self, only=None):
        for name, E in self.engs.items():
            if only and name not in only:
                continue
            need = {}
            for n2, E2 in self.engs.items():
                if n2 != name and E2["cnt"] > 0:
                    need[E2["key"]] = (E2["sem"], E2["cnt"])
            for k in range(self.NDS):
                if self.dcnt[k] > 0:
                    need["D%d" % k] = (self.dsems[k], self.dcnt[k])
            self._wait(E, need)
        if not only:
            self.acc = {}


def _interleave(*gens):
    gens = [g for g in gens if g is not None]
    while gens:
        for g in list(gens):
            try:
                next(g)
            except StopIteration:
                gens.remove(g)


def _interleave_w(g1, n1, g2, n2):
    d1 = d2 = 0
    a1 = g1 is not None
    a2 = g2 is not None
    while a1 or a2:
        if a1 and (not a2 or d1 * n2 <= d2 * n1):
            try:
                next(g1); d1 += 1
            except StopIteration:
                a1 = False
        else:
            try:
                next(g2); d2 += 1
            except StopIteration:
                a2 = False


def build(L, stop=99):
    from contextlib import ExitStack
    NT = L // 128
    KSEL = min(256, L // 4)
    TB = min(512, L)
    NB = L // TB
    nc = bass.Bass("TRN2", target_bir_lowering=False)
    S = Sched(nc)

    def dram(name, shape, dt=F32, kind="ExternalInput"):
        return nc.dram_tensor(name, list(shape), dt, kind=kind).ap()

    x = dram("x", [L, D]); cvec = dram("c", [D])
    ada_w = dram("ada_w", [D, 6 * D]); ada_b = dram("ada_b", [6 * D])
    pre_norm_mix = dram("pre_norm_mix", [D]); post_norm_mix = dram("post_norm_mix", [D])
    pre_norm_ffn = dram("pre_norm_ffn", [D]); post_norm_ffn = dram("post_norm_ffn", [D])
    w_in = dram("w_in", [D, 8688]); q_norm = dram("q_norm", [256]); kv_norm = dram("kv_norm", [128])
    w_uq = dram("w_uq", [256, 2048]); w_uv = dram("w_uv", [16, 128, 64]); rel_bias = dram("rel_bias", [32, 16])
    w_qidx = dram("w_qidx", [256, 1024]); kidx_norm = dram("kidx_norm", [64])
    conv_w = dram("conv_w", [4, 4096]); conv_b = dram("conv_b", [4096])
    dt_bias = dram("dt_bias", [32]); a_log = dram("a_log", [32]); d_skip = dram("d_skip", [32])
    ssd_norm = dram("ssd_norm", [2048]); w_o_attn = dram("w_o_attn", [D, D]); w_o_ssd = dram("w_o_ssd", [2048, D])
    w_out = dram("w_out", [D, D]); w_ffn_in = dram("w_ffn_in", [D, 2 * DFF]); w_ffn_out = dram("w_ffn_out", [DFF, D])
    cI4 = dram("cI4", [128, 512]); cJ = dram("cJ", [128, 128]); cTRI = dram("cTRI", [128, 128])
    cSTR = dram("cSTR", [128, 128]); cDM = dram("cDM", [128, 128]); cOH = dram("cOH", [32, 383])
    out = dram("out", [L, D], F32, kind="ExternalOutput")
    tv_d = dram("tv_d", [16, 383], F32, kind="Internal")
    mixa_d = dram("mixa_d", [L, D], BF, kind="Internal")
    yz_d = dram("yz_d", [L, 2048], BF, kind="Internal")
    x1_d = dram("x1_d", [L, D], F32, kind="Internal")

    glob = ExitStack()

    def sb(name, shape, dt=F32, st=None):
        return (st or glob).enter_context(nc.sbuf_tensor(name, list(shape), dt))

    banks = [glob.enter_context(nc.psum_tensor("ps%d" % i, [128, 512], F32)) for i in range(8)]
    bstate = {"k": 0, "free": list(range(8))}

    def bank():
        fr = bstate["free"]
        b = fr[bstate["k"] % len(fr)]
        bstate["k"] += 1
        return banks[b]

    def bfv(bk):
        return bk.bitcast(BF)

    def I(eng, meth, **kw):
        outs = [kw[k] for k in ("out", "accum_out", "ap") if k in kw and _isap(kw[k])]
        ins = [v for k, v in kw.items() if k not in ("out", "accum_out", "ap") and _isap(v)]
        return S.op(eng, lambda e: getattr(e, meth)(**kw), reads=ins, writes=outs)

    def mm(out_, lhsT, rhs, start=True, stop=True):
        return S.op("pe", lambda e: e.matmul(out_, lhsT, rhs, start=start, stop=stop), reads=[lhsT, rhs], writes=[out_])

    def tr(out_, in_, ident):
        return S.op("pe", lambda e: e.transpose(out_, in_, ident), reads=[in_, ident], writes=[out_])

    def act(out_, in_, func, **kw):
        return I("act", "activation", out=out_, in_=in_, func=func, **kw)

    def ts(eng, out_, in0, s1, s2, op0, op1=None):
        if op1 is None:
            return I(eng, "tensor_scalar", out=out_, in0=in0, scalar1=s1, scalar2=None, op0=op0)
        return I(eng, "tensor_scalar", out=out_, in0=in0, scalar1=s1, scalar2=s2, op0=op0, op1=op1)

    def tt(eng, out_, in0, in1, op):
        return I(eng, "tensor_tensor", out=out_, in0=in0, in1=in1, op=op)

    def stt(out_, in0, scalar, in1, op0, op1):
        return I("dve", "scalar_tensor_tensor", out=out_, in0=in0, scalar=scalar, in1=in1, op0=op0, op1=op1)

    def cp(eng, out_, in_):
        if eng == "act":
            return I("act", "copy", out=out_, in_=in_)
        return I(eng, "tensor_copy", out=out_, in_=in_)

    def memset(eng, ap, v):
        return I(eng, "memset", ap=ap, constant=v)

    def dma(q, out_, in_, **kw):
        return S.dma(q, out_, in_, **kw)

    def bc(ap, shape):
        return ap.broadcast_to(list(shape))

    _rr = {"k": 0}

    def evq():
        _rr["k"] += 1
        return "act" if _rr["k"] % 2 else "dve"

    identF = sb("identF", [128, 128]); dma("sp", identF[:], cI4[:, 0:128])
    i4b = sb("i4b", [128, 512], BF)
    identB = i4b[:, 0:128]
    jb = sb("jb", [128, 128], BF)
    triF = sb("triF", [128, 128]); dma("sp", triF[:], cTRI[:, :])
    strF = sb("strF", [128, 128]); dma("sp", strF[:], cSTR[:, :])
    dmF = sb("dmF", [128, 128]); dma("sp", dmF[:], cDM[:, :])
    onesF = sb("onesF", [128, 128]); memset("dve", onesF[:], 1.0)
    onesB = sb("onesB", [128, 128], BF); memset("dve", onesB[:], 1.0)

    lcstg = sb("lcstg", [48, 128], F32)

    def load_cols(dst, vec, n):
        kc = n // 128
        if kc <= 2:
            for k_ in range(kc):
                dma("sp", dst[:, k_:k_ + 1], vec[k_ * 128:(k_ + 1) * 128].rearrange("(p o) -> p o", o=1))
            return
        dma("sp", lcstg[0:kc, :], vec.rearrange("(k p) -> k p", p=128))
        ps = bank()
        tr(ps[:, 0:kc], lcstg[0:kc, :], identF[0:kc, 0:kc])
        cp("dve", dst, ps[:, 0:kc])

    wstg = [sb("wstg%d" % i, [128, 1024], F32) for i in range(3)]
    _ws = {"k": 0}

    def stage():
        _ws["k"] += 1
        return wstg[_ws["k"] % 3]

    def load_w(dst, W, c0, c1, r0=0):
        kcn = dst.shape[1]
        n = c1 - c0
        per = max(1, 1024 // n)
        for k0 in range(0, kcn, per):
            k1 = min(kcn, k0 + per)
            sv = stage()[:, 0:(k1 - k0) * n].rearrange("p (k n) -> p k n", n=n)
            dma("sp", sv, W[r0 + k0 * 128:r0 + k1 * 128, c0:c1].rearrange("(kc p) n -> p kc n", p=128))
            cp("pool", dst[:, k0:k1, :], sv)

    sv_ = stage(); dma("sp", sv_[:, 0:512], cI4[:, :]); cp("pool", i4b[:], sv_[:, 0:512])
    sv_ = stage(); dma("sp", sv_[:, 0:128], cJ[:, :]); cp("pool", jb[:], sv_[:, 0:128])

    modc = sb("modc", [128, 48])
    gcol = sb("gcol", [128, 8]); scol = sb("scol", [128, 8])
    gcol2 = sb("gcol2", [128, 8]); scol2 = sb("scol2", [128, 8])
    gpmB = sb("gpmB", [128, D]); gpfB = sb("gpfB", [128, D])
    with ExitStack() as st:
        ccol = sb("ccol", [128, 8], F32, st); scc = sb("scc", [128, 8], F32, st)
        abcol = sb("abcol", [128, 48], F32, st)
        tmpc = sb("tmpc", [128, 8], F32, st)
        load_cols(ccol[:], cvec, D)
        load_cols(abcol[:], ada_b, 6 * D)
        act(scc[:], ccol[:], AF.Silu)
        awt = [sb("awt%d" % i, [128, 8, 512], F32, st) for i in range(2)]
        awb = [sb("awb%d" % i, [128, 8, 512], BF, st) for i in range(2)]
        sccb = sb("sccb", [128, 8], BF, st)
        cp("dve", sccb[:], scc[:])
        pm = bank()
        for cb in range(12):
            a32 = awt[cb % 2]
            a = awb[cb % 2]
            dma("sp", a32[:], ada_w[:, cb * 512:(cb + 1) * 512].rearrange("(kc p) n -> p kc n", p=128))
            cp("act", a[:, 0:3, :], a32[:, 0:3, :])
            cp("dve", a[:, 3:6, :], a32[:, 3:6, :])
            cp("pool", a[:, 6:8, :], a32[:, 6:8, :])
            for jj in range(4):
                j = cb * 4 + jj
                for kc in range(8):
                    mm(pm[:, j:j + 1], a[:, kc, jj * 128:(jj + 1) * 128], sccb[:, kc:kc + 1], start=(kc == 0), stop=(kc == 7))
        tt("dve", modc[:], pm[:, 0:48], abcol[:], ALU.add)
        pn = sb("pn", [128, 8], F32, st)
        load_cols(pn[:], pre_norm_mix, D)
        ts("dve", tmpc[:], modc[:, 8:16], 1.0, None, ALU.add)
        tt("dve", gcol[:], tmpc[:], pn[:], ALU.mult)
        cp("dve", scol[:], modc[:, 0:8])
        pn2 = sb("pn2", [128, 8], F32, st)
        load_cols(pn2[:], pre_norm_ffn, D)
        ts("dve", tmpc[:], modc[:, 32:40], 1.0, None, ALU.add)
        tt("dve", gcol2[:], tmpc[:], pn2[:], ALU.mult)
        cp("dve", scol2[:], modc[:, 24:32])
        for (dst, gsl, pvec, nm) in ((gpmB, slice(16, 24), post_norm_mix, "a"), (gpfB, slice(40, 48), post_norm_ffn, "b")):
            pcol = sb("pcol" + nm, [128, 8], F32, st)
            gp = sb("gp" + nm, [128, 8], F32, st)
            gb = sb("gb" + nm, [128, 128], F32, st)
            load_cols(pcol[:], pvec, D)
            tt("dve", gp[:], modc[:, gsl], pcol[:], ALU.mult)
            for kc in range(8):
                ts("dve", gb[:], onesF[:], gp[:, kc:kc + 1], None, ALU.mult)
                ps = bank()
                mm(ps[:, 0:128], gb[:], identF[:])
                cp("act", dst[:, kc * 128:(kc + 1) * 128], ps[:, 0:128])
        S.barrier()
    if stop == 0:
        return nc

    import os
    DBG = int(os.environ.get("KDBG", "9"))

    def norm_to_T(xt, dstT, tile, gc, sc, tmp_bf, stat):
        junk = tmp_bf
        act(junk[:], xt, AF.Square, accum_out=stat[:, 0:1])
        if DBG < 2:
            return
        ts("dve", stat[:, 1:2], stat[:, 0:1], 1.0 / D, EPS, ALU.mult, ALU.add)
        act(stat[:, 2:3], stat[:, 1:2], AF.Sqrt)
        I("dve", "reciprocal", out=stat[:, 3:4], in_=stat[:, 2:3])
        if DBG < 3:
            return
        ts("dve", tmp_bf[:], xt, stat[:, 3:4], None, ALU.mult)
        if DBG < 4:
            return
        ps = bank()
        pb = bfv(ps)
        for kc in range(8):
            tr(pb[:, kc * 128:(kc + 1) * 128], tmp_bf[:, kc * 128:(kc + 1) * 128], identB)
        if DBG < 5:
            return
        for kc in range(8):
            if False:
                pass
            else:
                ts("dve", dstT[:, kc, tile * 128:(tile + 1) * 128], pb[:, kc * 128:(kc + 1) * 128],
                   gc[:, kc:kc + 1], sc[:, kc:kc + 1], ALU.mult, ALU.add)

    hT = sb("hT", [128, 8, L], BF)
    stA = ExitStack()
    st = stA
    dl = []
    wq = sb("wq", [128, 8, 256], BF, st); dl.append(lambda: load_w(wq[:], w_in, C_Q, C_Q + 256))
    wkv = sb("wkv", [128, 8, 128], BF, st); dl.append(lambda: load_w(wkv[:], w_in, C_KV, C_KV + 128))
    wkk = sb("wkk", [128, 8, 128], BF, st)
    dl.append(lambda: (load_w(wkk[:, :, 0:64], w_in, C_KI, C_KI + 64), load_w(wkk[:, :, 64:128], w_in, C_KI, C_KI + 64)))
    wwi = sb("wwi", [128, 8, 16], BF, st); dl.append(lambda: load_w(wwi[:], w_in, C_WI, C_WI + 16))
    wuq = sb("wuq", [128, 2, 2048], BF, st)
    for r in range(2):
        dl.append(lambda r=r: (load_w(wuq[:, r:r + 1, 0:1024], w_uq, 0, 1024, r0=r * 128), load_w(wuq[:, r:r + 1, 1024:2048], w_uq, 1024, 2048, r0=r * 128)))
    wqi = sb("wqi", [128, 2, 1024], BF, st)
    for r in range(2):
        dl.append(lambda r=r: load_w(wqi[:, r:r + 1, :], w_qidx, 0, 1024, r0=r * 128))
    wuv = sb("wuv", [128, 16, 128], BF, st)

    def _ld_wuv():
        memset("pool", wuv[:], 0.0)
        sv_ = stage()[:, 0:1024].rearrange("p (h d) -> p h d", d=64)
        dma("sp", sv_, w_uv.rearrange("h c d -> c h d"))
        for h in range(16):
            o = (h % 2) * 64
            cp("pool", wuv[:, h, o:o + 64], sv_[:, h, :])
    dl.append(_ld_wuv)
    woa = sb("woa", [128, 8, D], BF, st)
    for k0 in range(0, 8, 2):
        dl.append(lambda k0=k0: load_w(woa[:, k0:k0 + 2, :], w_o_attn, 0, D, r0=k0 * 128))
    wga = sb("wga", [128, 8, D], BF, st)
    for k0 in range(0, 8, 2):
        dl.append(lambda k0=k0: load_w(wga[:, k0:k0 + 2, :], w_in, C_GA, C_GA + D, r0=k0 * 128))
    qncol = sb("qncol", [128, 2], F32, st); dl.append(lambda: load_cols(qncol[:], q_norm, 256))
    kvcol = sb("kvcol", [128, 1], F32, st); dl.append(lambda: load_cols(kvcol[:], kv_norm, 128))
    kicol = sb("kicol", [128, 1], F32, st)
    dl.append(lambda: (dma("sp", kicol[0:64, :], kidx_norm.rearrange("(p o) -> p o", o=1)),
                       dma("sp", kicol[64:128, :], kidx_norm.rearrange("(p o) -> p o", o=1))))

    with ExitStack() as st:
        xts = [sb("xa%d" % i, [128, D], F32, st) for i in range(2)]
        xbs = [sb("xb%d" % i, [128, D], BF, st) for i in range(2)]
        sts = [sb("xs%d" % i, [128, 4], F32, st) for i in range(2)]
        for i in range(NT):
            xt = xts[i % 2]
            dma("sp", xt[:], x[i * 128:(i + 1) * 128, :])
            if i >= 1:
                for _ in range(3):
                    if dl:
                        dl.pop(0)()
            norm_to_T(xt[:], hT, i, gcol, scol, xbs[i % 2], sts[i % 2])
        while dl:
            dl.pop(0)()
        S.barrier()
    if stop == 1:
        return nc

    with stA as st:
        BFt = sb("BFt", [128, 2, 16, 128], BF, st)
        with ExitStack() as s2:
            rb = sb("rb", [32, 16], F32, s2); dma("sp", rb[:], rel_bias[:, :])
            rb15 = sb("rb15", [32, 16], F32, s2); dma("sp", rb15[:], rel_bias[15, :].partition_broadcast(32))
            tt("dve", rb[:], rb[:], rb15[:], ALU.subtract)
            oh = sb("oh", [32, 383], F32, s2); dma("sp", oh[:], cOH[:, :])
            ps = bank()
            mm(ps[0:16, 0:383], rb[:], oh[:])
            tv = sb("tv", [16, 383], F32, s2)
            cp("dve", tv[:], ps[0:16, 0:383])
            dma("sp", tv_d[:, :], tv[:])
            for dd in range(2):
                src = bass.AP(tensor=tv_d.tensor, offset=128 * dd, ap=[[1, 128], [383, 16], [1, 128]])
                for hh_ in range(2):
                    src = bass.AP(tensor=tv_d.tensor, offset=128 * dd + hh_ * 8 * 383, ap=[[1, 128], [383, 8], [1, 128]])
                    sv_ = stage()[:, 0:1024].rearrange("p (h q) -> p h q", q=128)
                    dma("sp", sv_, src)
                    cp("pool", BFt[:, dd, hh_ * 8:(hh_ + 1) * 8, :], sv_)
            S.barrier()
        if stop == 2:
            return nc

        qnT = sb("qnT", [128, 2, L], BF, st)
        knT = sb("knT", [128, L], BF, st)
        kiT = sb("kiT", [128, L], BF, st)
        with ExitStack() as s2:
            raw = sb("lraw", [128, 2, TB], F32, s2)
            sq = sb("lsq", [128, 2, TB], BF, s2)
            rs = sb("lrs", [128, TB], F32, s2)
            for tb in range(NB):
                tsl = slice(tb * TB, (tb + 1) * TB)
                for (wt, nch, ncol, dst) in ((wq, 2, qncol, qnT), (wkv, 1, kvcol, knT), (wkk, 1, kicol, kiT)):
                    for r in range(nch):
                        ps = bank()
                        for kc in range(8):
                            mm(ps[:, 0:TB], wt[:, kc, r * 128:(r + 1) * 128], hT[:, kc, tsl], start=(kc == 0), stop=(kc == 7))
                        cp("dve", raw[:, r, :], ps[:, 0:TB])
                        act(sq[:, r, :], raw[:, r, :], AF.Square)
                    p2 = bank()
                    for r in range(nch):
                        mm(p2[:, 0:TB], onesB[:], sq[:, r, :], start=(r == 0), stop=(r == nch - 1))
                    ts("dve", rs[:], p2[:, 0:TB], 1.0 / (128 * nch), EPS, ALU.mult, ALU.add)
                    act(rs[:], rs[:], AF.Sqrt)
                    I("dve", "reciprocal", out=rs[:], in_=rs[:])
                    for r in range(nch):
                        o_ = dst[:, r, tsl] if nch == 2 else dst[:, tsl]
                        stt(o_, raw[:, r, :], ncol[:, r:r + 1], rs[:], ALU.mult, ALU.mult)
            S.barrier()
        if stop == 3:
            return nc
        kvtm = sb("kvtm", [128, NT, 128], BF, st)
        witm = sb("witm", [128, NT, 16], F32, st)
        for t in range(NT):
            ps = bank(); pb = bfv(ps)
            tr(pb[:, 0:128], knT[:, t * 128:(t + 1) * 128], identB)
            cp("dve", kvtm[:, t, :], pb[:, 0:128])
            ps = bank()
            for kc in range(8):
                mm(ps[:, 0:16], hT[:, kc, t * 128:(t + 1) * 128], wwi[:, kc, :], start=(kc == 0), stop=(kc == 7))
            act(witm[:, t, :], ps[:, 0:16], AF.Copy, scale=1.0 / 32.0)

        qT = [sb("qT%d" % i, [128, 16, 128], BF, st) for i in range(2)]
        qiT = [sb("qiT%d" % i, [128, 8, 128], BF, st) for i in range(2)]
        accs = [sb("acc0", [128, L], F32, st)] * 2
        works = [sb("work0", [128, L], F32, st)] * 2
        madd = [sb("madd%d" % i, [128, L], BF, st) for i in range(2)]
        m8 = [sb("m8%d" % i, [128, 8], F32, st) for i in range(2)]
        thr = [sb("thr%d" % i, [128, 1], F32, st) for i in range(2)]
        rtmp = [sb("rtmp%d" % i, [128, 512], BF, st) for i in range(4)]
        dws = [sb("dw0", [128, 16, 128], BF, st)] * 2
        pTs = [sb("pT%d" % i, [128, 512], BF, st) for i in range(3)]
        oT = sb("oT", [128, 16, 128], BF, st)
        o2T = sb("o2T", [128, 8, 128], BF, st)
        lnd = sb("lnd", [128, 512], F32, st); rden = sb("rden", [128, 512], F32, st); oun = sb("oun", [128, 512], F32, st)
        sga = [sb("sga%d" % i, [128, 512], BF, st) for i in range(2)]
        mixt = [sb("mixt0", [128, D], BF, st)] * 2
        bstate["free"] = [0, 1, 2, 3, 7]
        bo = [banks[4], banks[4]]; bd = [banks[5], banks[5]]; pacc = banks[6]
        cnt = {"r": 0, "p": 0}
        QS = 128.0 ** -0.5

        def stage1(i):
            b = i % 2
            qs = slice(i * 128, (i + 1) * 128)
            for hq in range(4):
                ps = bank()
                for hh in range(4):
                    h = hq * 4 + hh
                    for r in range(2):
                        mm(ps[:, hh * 128:(hh + 1) * 128], wuq[:, r, h * 128:(h + 1) * 128], qnT[:, r, qs], start=(r == 0), stop=(r == 1))
                act(qT[b][:, hq * 4:(hq + 1) * 4, :].rearrange("p h q -> p (h q)"), ps[:, :], AF.Copy, scale=QS)
            yield
            for mq in range(2):
                ps = bank()
                for mm_ in range(4):
                    m = mq * 4 + mm_
                    for r in range(2):
                        mm(ps[:, mm_ * 128:(mm_ + 1) * 128], wqi[:, r, m * 128:(m + 1) * 128], qnT[:, r, qs], start=(r == 0), stop=(r == 1))
                cp("act", qiT[b][:, mq * 4:(mq + 1) * 4, :].rearrange("p h q -> p (h q)"), ps[:, :])
            yield
            nk = (i + 1) * 128
            acc = accs[b]
            dw = dws[b]
            for h in range(16):
                ts("pool", dw[:, h, :], identF[:], witm[:, i, h:h + 1], None, ALU.mult)
            for kb in range((nk + 511) // 512):
                n = min(512, nk - kb * 512)
                ks = slice(kb * 512, kb * 512 + n)
                pend = []

                def accmm(h, rt):
                    mm(pacc[:, 0:n], dw[:, h, :], rt[:, 0:n], start=(h == 0), stop=(h == 15))

                for h in range(16):
                    m, half = h // 2, h % 2
                    pr = slice(half * 64, half * 64 + 64)
                    ps = bank()
                    mm(ps[:, 0:n], qiT[b][pr, m, :], kiT[pr, ks])
                    rt = rtmp[cnt["r"] % 4]; cnt["r"] += 1
                    act(rt[:, 0:n], ps[:, 0:n], AF.Relu)
                    pend.append((h, rt))
                    if len(pend) > 2:
                        accmm(*pend.pop(0))
                    if h % 4 == 3:
                        yield
                while pend:
                    accmm(*pend.pop(0))
                cp("act", acc[:, ks], pacc[:, 0:n])
                yield
            tt("pool", acc[:, i * 128:(i + 1) * 128], acc[:, i * 128:(i + 1) * 128], dmF[:], ALU.add)
            if nk > KSEL:
                src = acc
                rounds = KSEL // 8
                for r in range(rounds):
                    I("dve", "max", out=m8[b][:], in_=src[:, 0:nk])
                    if r < rounds - 1:
                        I("dve", "match_replace", out=works[b][:, 0:nk], in_to_replace=m8[b][:], in_values=src[:, 0:nk], imm_value=NEG)
                        src = works[b]
                    yield
                ts("dve", thr[b][:], m8[b][:, 7:8], -1.0e29, None, ALU.max)
            else:
                memset("dve", thr[b][:], -1.0e29)
            ts("dve", madd[b][:, 0:nk], acc[:, 0:nk], thr[b][:, 0:1], -30000.0, ALU.is_lt, ALU.mult)
            yield

        def stage2(i):
            b = i % 2
            qs = slice(i * 128, (i + 1) * 128)
            its = [(hg, j) for hg in range(4) for j in range(i + 1)]
            pend = []

            def logits(hg, j):
                dd = i - j
                pl = bank()
                mm(pl[:, :], knT[:, j * 128:(j + 1) * 128], qT[b][:, hg * 4:(hg + 1) * 4, :].rearrange("p h q -> p (h q)"), start=True, stop=False)
                mm(pl[:, :], madd[b][:, j * 128:(j + 1) * 128], i4b[:, :], start=False, stop=(dd >= 2))
                if dd < 2:
                    mm(pl[:, :], jb[:], BFt[:, dd, hg * 4:(hg + 1) * 4, :].rearrange("p h q -> p (h q)"), start=False, stop=True)
                pT = pTs[cnt["p"] % 3]; cnt["p"] += 1
                act(pT[:], pl[:, :], AF.Exp)
                return pT

            def pv(hg, j, pT):
                po = bo[0]; pd = bd[0]
                mm(po[:, :], kvtm[:, j, :], pT[:], start=(j == 0), stop=(j == i))
                mm(pd[:, :], onesB[:], pT[:], start=(j == 0), stop=(j == i))
                if j == i:
                    act(lnd[:], pd[:, :], AF.Ln)
                    act(rden[:], lnd[:], AF.Exp, scale=-1.0)
                    cp("act", oun[:], po[:, :])
                    tt("pool", oT[:, hg * 4:(hg + 1) * 4, :].rearrange("p h q -> p (h q)"), oun[:], rden[:], ALU.mult)

            for t_, (hg, j) in enumerate(its):
                pend.append((hg, j, logits(hg, j)))
                if len(pend) > 2:
                    pv(*pend.pop(0))
                yield
            while pend:
                pv(*pend.pop(0))
            yield
            for mq in range(2):
                ps = bank()
                for mm_ in range(4):
                    m = mq * 4 + mm_
                    mm(ps[:, mm_ * 128:(mm_ + 1) * 128], wuv[:, 2 * m, :], oT[:, 2 * m, :], start=True, stop=False)
                    mm(ps[:, mm_ * 128:(mm_ + 1) * 128], wuv[:, 2 * m + 1, :], oT[:, 2 * m + 1, :], start=False, stop=True)
                cp("act", o2T[:, mq * 4:(mq + 1) * 4, :].rearrange("p h q -> p (h q)"), ps[:, :])
            yield
            mx = mixt[b]
            for cb in range(2):
                cs = slice(cb * 512, (cb + 1) * 512)
                pg = bank()
                for kc in range(8):
                    mm(pg[:, :], hT[:, kc, qs], wga[:, kc, cs], start=(kc == 0), stop=(kc == 7))
                act(sga[cb][:], pg[:, :], AF.Sigmoid)
                py = bank()
                for m in range(8):
                    mm(py[:, :], o2T[:, m, :], woa[:, m, cs], start=(m == 0), stop=(m == 7))
                tt("dve", mx[:, cs], py[:, :], sga[cb][:], ALU.mult)
                yield
            dma("sp", mixa_d[qs, :], mx[:])
            yield

        prev = None
        for i in range(NT + 1):
            g1 = stage1(i) if i < NT else None
            g2 = stage2(i - 1) if i >= 1 else None
            nkb_ = (i + 1 + 3) // 4
            n1 = 3 + 5 * nkb_ + ((KSEL // 8) if (i + 1) * 128 > KSEL else 0)
            n2 = 4 * i + 6
            _interleave_w(g1, n1, g2, max(1, n2))
        bstate["free"] = list(range(8))
        S.barrier()
        if stop == 4:
            return nc

    ssq = sb("ssq", [128, NT, 8])
    rstds = sb("rstds", [128, NT])
    with ExitStack() as st:
        dtall = sb("dtall", [128, NT, 32], F32, st); dAall = sb("dAall", [128, NT, 32], F32, st)
        eaall = sb("eaall", [128, NT, 32], F32, st); decall = sb("decall", [128, NT, 32], F32, st)
        cdall = sb("cdall", [128, NT, 32], F32, st)
        dI = sb("dI", [128, 32, 128], BF, st)
        cwc = sb("cwc", [128, 4, 32], F32, st); cbc = sb("cbc", [128, 32], F32, st)
        for k in range(4):
            load_cols(cwc[:, k, :], conv_w[k, :], 4096)
        load_cols(cbc[:], conv_b, 4096)
        with ExitStack() as s2:
            wdt = sb("wdt", [128, 8, 32], BF, s2); load_w(wdt[:], w_in, C_DT, C_DT + 32)
            dtbB = sb("dtbB", [128, 32], F32, s2); dma("sp", dtbB[:], dt_bias.partition_broadcast(128))
            AB = sb("AB", [128, 32], F32, s2); dma("sp", AB[:], a_log.partition_broadcast(128))
            dsB = sb("dsB", [128, 32], F32, s2); dma("sp", dsB[:], d_skip.partition_broadcast(128))
            act(AB[:], AB[:], AF.Exp)
            ts("dve", AB[:], AB[:], -1.0, None, ALU.mult)
            for h in range(32):
                ts("pool", dI[:, h, :], identF[:], dsB[:, h:h + 1], None, ALU.mult)
            xb_ = sb("xb_", [128, 32], F32, s2); ax = sb("ax_", [128, 32], F32, s2)
            ee = sb("ee_", [128, 32], F32, s2); tot = sb("tot_", [128, 32], F32, s2)
            for t in range(NT):
                ps = bank()
                for kc in range(8):
                    mm(ps[:, 0:32], hT[:, kc, t * 128:(t + 1) * 128], wdt[:, kc, :], start=(kc == 0), stop=(kc == 7))
                tt("dve", xb_[:], ps[:, 0:32], dtbB[:], ALU.add)
                act(ax[:], xb_[:], AF.Abs)
                act(ee[:], ax[:], AF.Exp, scale=-1.0)
                ts("dve", ee[:], ee[:], 1.0, None, ALU.add)
                act(ee[:], ee[:], AF.Ln)
                stt(dtall[:, t, :], xb_[:], 0.0, ee[:], ALU.max, ALU.add)
                tt("dve", dAall[:, t, :], dtall[:, t, :], AB[:], ALU.mult)
                p1 = bank(); p2 = bank()
                mm(p1[:, 0:32], triF[:], dAall[:, t, :])
                mm(p2[:, 0:32], onesF[:], dAall[:, t, :])
                act(eaall[:, t, :], p1[:, 0:32], AF.Exp)
                act(cdall[:, t, :], p2[:, 0:32], AF.Exp)
                cp("act", tot[:], p2[:, 0:32])
                tt("dve", tot[:], tot[:], p1[:, 0:32], ALU.subtract)
                act(decall[:, t, :], tot[:], AF.Exp)
            S.barrier()

        wxs = [sb("wxs%d" % i, [128, 8, 256], BF, st) for i in range(2)]
        wBs = [sb("wBs%d" % i, [128, 8, 128], BF, st) for i in range(2)]
        wCs = [sb("wCs%d" % i, [128, 8, 128], BF, st) for i in range(2)]
        wzs = [sb("wzs%d" % i, [128, 8, 256], BF, st) for i in range(2)]
        raw = sb("raw", [128, 4, 3 + L], BF, st)
        memset("pool", raw[:, :, 0:3], 0.0)
        cv = sb("cv", [128, 4, L], BF, st)
        cvf = [sb("cvf%d" % i, [128, L], F32, st) for i in range(2)]
        xstm = sb("xstm", [128, NT, 256], BF, st); Btm = sb("Btm", [128, NT, 128], BF, st)
        sztm = sb("sztm", [128, NT, 256], BF, st)
        prevs = sb("prevs", [128, 256], F32, st); prevb = sb("prevb", [128, 256], BF, st)
        cbm = [sb("cbm%d" % i, [128, 128], BF, st) for i in range(2)]
        L4 = [sb("L4%d" % i, [128, 4, 128], BF, st) for i in range(2)]
        triB = sb("triB", [128, 128], BF, st); cp("pool", triB[:], triF[:])
        E4 = [sb("E4%d" % i, [128, 4, 128], BF, st) for i in range(2)]
        M4 = [sb("M4%d" % i, [128, 4, 128], BF, st) for i in range(2)]
        X4 = [sb("X4%d" % i, [128, 4, 64], BF, st) for i in range(2)]
        Xd4 = [sb("Xd4%d" % i, [128, 4, 64], BF, st) for i in range(2)]
        yo = [sb("yo%d" % i, [128, 4, 64], F32, st) for i in range(2)]
        y1 = [sb("y1%d" % i, [128, 256], F32, st) for i in range(2)]
        yzt = [sb("yzt%d" % i, [128, 256], BF, st) for i in range(2)]
        yjunk = sb("yjunk", [128, 256], BF, st)
        ptmp = sb("ptmp", [128, 4, 64], F32, st)

        def load_group_w(g):
            b = g % 2
            load_w(wxs[b][:], w_in, C_XS + g * 256, C_XS + (g + 1) * 256)
            load_w(wBs[b][:], w_in, C_B + g * 128, C_B + (g + 1) * 128)
            load_w(wCs[b][:], w_in, C_C + g * 128, C_C + (g + 1) * 128)
            load_w(wzs[b][:], w_in, C_Z + g * 256, C_Z + (g + 1) * 256)

        load_group_w(0)
        for g in range(8):
            b = g % 2
            h0 = g * 4
            if g + 1 < 8:
                load_group_w(g + 1)
            chunks = (2 * g, 2 * g + 1, 16 + g, 24 + g)
            for r in range(4):
                wt = (wxs[b][:, :, 0:128], wxs[b][:, :, 128:256], wBs[b][:, :, :], wCs[b][:, :, :])[r]
                for tb in range(NB):
                    ps = bank()
                    for kc in range(8):
                        mm(ps[:, 0:TB], wt[:, kc, :], hT[:, kc, tb * TB:(tb + 1) * TB], start=(kc == 0), stop=(kc == 7))
                    cp(evq(), raw[:, r, 3 + tb * TB:3 + (tb + 1) * TB], ps[:, 0:TB])
            for r in range(4):
                ch = chunks[r]
                cf = cvf[r % 2]
                ts("dve", cf[:], raw[:, r, 3:3 + L], cwc[:, 3, ch:ch + 1], cbc[:, ch:ch + 1], ALU.mult, ALU.add)
                for k in range(3):
                    stt(cf[:], raw[:, r, k:k + L], cwc[:, k, ch:ch + 1], cf[:], ALU.mult, ALU.add)
                act(cv[:, r, :], cf[:], AF.Silu)
            for t in range(NT):
                ps = bank(); pb = bfv(ps)
                tsl = slice(t * 128, (t + 1) * 128)
                tr(pb[:, 0:128], cv[:, 0, tsl], identB); tr(pb[:, 128:256], cv[:, 1, tsl], identB)
                tr(pb[:, 256:384], cv[:, 2, tsl], identB)
                e_ = evq()
                cp(e_, xstm[:, t, :], pb[:, 0:256])
                cp(e_, Btm[:, t, :], pb[:, 256:384])
                pz = bank()
                for kc in range(8):
                    mm(pz[:, 0:256], hT[:, kc, tsl], wzs[b][:, kc, :], start=(kc == 0), stop=(kc == 7))
                act(sztm[:, t, :], pz[:, 0:256], AF.Silu)
            memset("pool", prevs[:], 0.0); memset("pool", prevb[:], 0.0)
            hs = slice(h0, h0 + 4)

            def stX(c):
                cc = c % 2
                csl = slice(c * 128, (c + 1) * 128)
                pcb = bank()
                mm(pcb[:, 0:128], cv[:, 2, csl], cv[:, 3, csl])
                tt("dve", cbm[cc][:], pcb[:, 0:128], triF[:], ALU.mult)
                tt("pool", L4[cc][:], bc(strF[:].unsqueeze(1), [128, 4, 128]), bc(dAall[:, c, hs].unsqueeze(2), [128, 4, 128]), ALU.mult)
                pseg = bank()
                for hh in range(4):
                    mm(pseg[:, hh * 128:(hh + 1) * 128], L4[cc][:, hh, :], triB[:])
                act(E4[cc][:].rearrange("p h l -> p (h l)"), pseg[:, :], AF.Exp)
                tt("pool", M4[cc][:], E4[cc][:], bc(cbm[cc][:].unsqueeze(1), [128, 4, 128]), ALU.mult)
                xv = xstm[:, c, :].rearrange("p (h d) -> p h d", h=4)
                tt("dve", X4[cc][:], xv, bc(dtall[:, c, hs].unsqueeze(2), [128, 4, 64]), ALU.mult)
                tt("pool", Xd4[cc][:], X4[cc][:], bc(decall[:, c, hs].unsqueeze(2), [128, 4, 64]), ALU.mult)

            def stY(c):
                cc = c % 2
                csl = slice(c * 128, (c + 1) * 128)
                py = bank()
                for hh in range(4):
                    mm(py[:, hh * 64:(hh + 1) * 64], M4[cc][:, hh, :], X4[cc][:, hh, :], start=True, stop=False)
                    mm(py[:, hh * 64:(hh + 1) * 64], dI[:, h0 + hh, :], xstm[:, c, hh * 64:(hh + 1) * 64], start=False, stop=True)
                pS = bank()
                mm(pS[:, 0:256], Btm[:, c, :], Xd4[cc][:].rearrange("p h d -> p (h d)"))
                pyo = bank()
                mm(pyo[:, 0:256], cv[:, 3, csl], prevb[:])
                tt("dve", yo[cc][:], pyo[:, 0:256].rearrange("p (h d) -> p h d", h=4), bc(eaall[:, c, hs].unsqueeze(2), [128, 4, 64]), ALU.mult)
                tt("dve", y1[cc][:], py[:, 0:256], yo[cc][:].rearrange("p h d -> p (h d)"), ALU.add)
                tt("pool", yzt[cc][:], y1[cc][:], sztm[:, c, :], ALU.mult)
                act(yjunk[:], yzt[cc][:], AF.Square, accum_out=ssq[:, c, g:g + 1])
                dma("sp", yz_d[csl, g * 256:(g + 1) * 256], yzt[cc][:])
                tt("pool", ptmp[:], prevs[:].rearrange("p (h d) -> p h d", h=4), bc(cdall[:, c, hs].unsqueeze(2), [128, 4, 64]), ALU.mult)
                tt("dve", prevs[:], ptmp[:].rearrange("p h d -> p (h d)"), pS[:, 0:256], ALU.add)
                cp("act", prevb[:], prevs[:])

            for c in range(NT + 1):
                if c < NT:
                    stX(c)
                if c >= 1:
                    stY(c - 1)
        I("dve", "tensor_reduce", out=rstds[:], in_=ssq[:], axis=AX.X, op=ALU.add)
        ts("dve", rstds[:], rstds[:], 1.0 / 2048, EPS, ALU.mult, ALU.add)
        act(rstds[:], rstds[:], AF.Sqrt)
        I("dve", "reciprocal", out=rstds[:], in_=rstds[:])
        S.barrier()
        if stop == 5:
            return nc

    h2T = sb("h2T", [128, 8, L], BF)
    with ExitStack() as st:
        wos = sb("wos", [128, 16, D], BF, st); load_w(wos[:, 0:8, :], w_o_ssd, 0, D); load_w(wos[:, 8:16, :], w_o_ssd, 0, D, r0=1024)
        wout = sb("wout", [128, 8, D], BF, st); load_w(wout[:], w_out, 0, D)
        wgb = sb("wgb", [128, 8, D], BF, st); load_w(wgb[:], w_in, C_GB, C_GB + D)
        sncol = sb("sncol", [128, 16], F32, st); load_cols(sncol[:], ssd_norm, 2048)
        yzl = [sb("yzl%d" % i, [128, 2048], BF, st) for i in range(2)]
        yT = [sb("yT0", [128, 16, 128], BF, st)] * 2
        sgb = [sb("sgb%d" % i, [128, 512], BF, st) for i in range(2)]
        mal = [sb("mal%d" % i, [128, D], BF, st) for i in range(2)]
        mpf = [sb("mpf0", [128, D], F32, st)] * 2
        mp = [sb("mp0", [128, D], BF, st)] * 2
        mpT = [sb("mpT%d" % i, [128, 8, 128], BF, st) for i in range(2)]
        xl = [sb("xl%d" % i, [128, D], F32, st) for i in range(2)]
        t1 = [sb("t10", [128, D], F32, st)] * 2
        x1t = xl
        jk = sb("jk", [128, 512], BF, st)
        so = [sb("so%d" % i, [128, 8], F32, st) for i in range(2)]
        xb2 = [sb("xb20", [128, D], BF, st)] * 2
        st2 = [sb("st2%d" % i, [128, 4], F32, st) for i in range(2)]
        def stageA(i):
            b = i % 2
            rs_ = slice(i * 128, (i + 1) * 128)
            dma("sp", yzl[b][:], yz_d[rs_, :])
            dma("sp", mal[b][:], mixa_d[rs_, :])
            for half in range(2):
                ps = bank(); pb = bfv(ps)
                for k8 in range(8):
                    kc = half * 8 + k8
                    tr(pb[:, k8 * 128:(k8 + 1) * 128], yzl[b][:, kc * 128:(kc + 1) * 128], identB)
                for k8 in range(8):
                    kc = half * 8 + k8
                    ts("dve", yT[b][:, kc, :], pb[:, k8 * 128:(k8 + 1) * 128], sncol[:, kc:kc + 1], None, ALU.mult)
            for cb in range(2):
                cs = slice(cb * 512, (cb + 1) * 512)
                pg = bank()
                for kc in range(8):
                    mm(pg[:, :], hT[:, kc, rs_], wgb[:, kc, cs], start=(kc == 0), stop=(kc == 7))
                act(sgb[cb][:], pg[:, :], AF.Sigmoid)
                py = bank()
                for kc in range(16):
                    mm(py[:, :], yT[b][:, kc, :], wos[:, kc, cs], start=(kc == 0), stop=(kc == 15))
                stt(mpf[b][:, cs], py[:, :], rstds[:, i:i + 1], sgb[cb][:], ALU.mult, ALU.mult)
            tt("dve", mp[b][:], mpf[b][:], mal[b][:], ALU.add)
            ps = bank(); pb = bfv(ps)
            for kc in range(8):
                tr(pb[:, kc * 128:(kc + 1) * 128], mp[b][:, kc * 128:(kc + 1) * 128], identB)
            cp("act", mpT[b][:].rearrange("p k t -> p (k t)"), pb[:, :])

        def stageB(i):
            b = i % 2
            rs_ = slice(i * 128, (i + 1) * 128)
            dma("sp", xl[b][:], x[rs_, :])
            pms = []
            for cb in range(2):
                cs = slice(cb * 512, (cb + 1) * 512)
                pm_ = bank(); pms.append(pm_)
                for kc in range(8):
                    mm(pm_[:, :], mpT[b][:, kc, :], wout[:, kc, cs], start=(kc == 0), stop=(kc == 7))
                act(jk[:], pm_[:, :], AF.Square, accum_out=so[b][:, cb:cb + 1])
            tt("dve", so[b][:, 2:3], so[b][:, 0:1], so[b][:, 1:2], ALU.add)
            ts("dve", so[b][:, 3:4], so[b][:, 2:3], 1.0 / D, EPS, ALU.mult, ALU.add)
            act(so[b][:, 4:5], so[b][:, 3:4], AF.Sqrt)
            I("dve", "reciprocal", out=so[b][:, 5:6], in_=so[b][:, 4:5])
            for cb in range(2):
                cs = slice(cb * 512, (cb + 1) * 512)
                stt(t1[b][:, cs], pms[cb][:, :], so[b][:, 5:6], gpmB[:, cs], ALU.mult, ALU.mult)
            tt("dve", x1t[b][:], t1[b][:], xl[b][:], ALU.add)
            dma("sp", x1_d[rs_, :], x1t[b][:])
            norm_to_T(x1t[b][:], h2T, i, gcol2, scol2, xb2[b], st2[b])

        for i in range(NT + 1):
            if i < NT:
                stageA(i)
            if i >= 1:
                stageB(i - 1)
        S.barrier()
        if stop == 6:
            return nc

    with ExitStack() as st:
        TBF = min(1024, L)
        wfo = sb("wfo", [128, NJ, D], BF, st)
        load_w(wfo[:, 0:11, :], w_ffn_out, 0, D); load_w(wfo[:, 11:22, :], w_ffn_out, 0, D, r0=11 * 128)
        wg = [sb("wg%d" % i, [128, 8, 128], BF, st) for i in range(2)]
        wu = [sb("wu%d" % i, [128, 8, 128], BF, st) for i in range(2)]
        fT = sb("fT", [128, NJ, TBF], BF, st)
        sgt = [sb("sgt%d" % i, [128, 512], BF, st) for i in range(2)]
        x1l = [sb("x1l%d" % i, [128, D], F32, st) for i in range(2)]
        t2 = [sb("t2%d" % i, [128, D], F32, st) for i in range(2)]
        ot = x1l
        jk2 = sb("jk2", [128, 512], BF, st)
        sf = [sb("sf%d" % i, [128, 8], F32, st) for i in range(2)]
        k = 0
        for blk in range(L // TBF):
            for j in range(NJ):
                b = k % 2; k += 1
                load_w(wg[b][:], w_ffn_in, j * 128, (j + 1) * 128)
                load_w(wu[b][:], w_ffn_in, DFF + j * 128, DFF + (j + 1) * 128)
                for sub in range(TBF // TB):
                    tsl = slice(blk * TBF + sub * TB, blk * TBF + (sub + 1) * TB)
                    pg = bank(); pu = bank()
                    for kc in range(8):
                        mm(pg[:, 0:TB], wg[b][:, kc, :], h2T[:, kc, tsl], start=(kc == 0), stop=(kc == 7))
                    for kc in range(8):
                        mm(pu[:, 0:TB], wu[b][:, kc, :], h2T[:, kc, tsl], start=(kc == 0), stop=(kc == 7))
                    act(sgt[sub % 2][:, 0:TB], pg[:, 0:TB], AF.Silu)
                    tt("dve", fT[:, j, sub * TB:(sub + 1) * TB], pu[:, 0:TB], sgt[sub % 2][:, 0:TB], ALU.mult)
            for tl in range(TBF // 128):
                i = blk * (TBF // 128) + tl
                b2 = i % 2
                rs_ = slice(i * 128, (i + 1) * 128)
                dma("sp", x1l[b2][:], x1_d[rs_, :])
                pfs = []
                for cb in range(2):
                    cs = slice(cb * 512, (cb + 1) * 512)
                    pf = bank(); pfs.append(pf)
                    for j in range(NJ):
                        mm(pf[:, :], fT[:, j, tl * 128:(tl + 1) * 128], wfo[:, j, cs], start=(j == 0), stop=(j == NJ - 1))
                    act(jk2[:], pf[:, :], AF.Square, accum_out=sf[b2][:, cb:cb + 1])
                tt("dve", sf[b2][:, 2:3], sf[b2][:, 0:1], sf[b2][:, 1:2], ALU.add)
                ts("dve", sf[b2][:, 3:4], sf[b2][:, 2:3], 1.0 / D, EPS, ALU.mult, ALU.add)
                act(sf[b2][:, 4:5], sf[b2][:, 3:4], AF.Sqrt)
                I("dve", "reciprocal", out=sf[b2][:, 5:6], in_=sf[b2][:, 4:5])
                for cb in range(2):
                    cs = slice(cb * 512, (cb + 1) * 512)
                    stt(t2[b2][:, cs], pfs[cb][:, :], sf[b2][:, 5:6], gpfB[:, cs], ALU.mult, ALU.mult)
                tt("pool", ot[b2][:], t2[b2][:], x1l[b2][:], ALU.add)
                dma("sp", out[rs_, :], ot[b2][:])
        S.barrier()
    return nc


def _consts():
    eye = np.eye(128, dtype=np.float32)
    k = np.arange(128)
    tri = (k[:, None] <= k[None, :]).astype(np.float32)
    strict = (k[:, None] > k[None, :]).astype(np.float32)
    dm = np.where((k[None, :] // 64) > (k[:, None] // 64), NEG, 0.0).astype(np.float32)
    rel = (127 - np.arange(383)).astype(np.int32)
    half = 16; max_exact = 8
    side = np.where(rel > 0, half, 0)
    n = np.abs(rel)
    ratio = np.maximum(n, max_exact).astype(np.float32) / np.float32(max_exact)
    large = max_exact + (np.log(ratio).astype(np.float32) / np.float32(math.log(128 / max_exact))
                         * np.float32(half - max_exact)).astype(np.int32)
    large = np.minimum(large, half - 1)
    bucket = side + np.where(n < max_exact, n, large)
    oh = np.zeros((32, 383), np.float32)
    oh[bucket, np.arange(383)] = 1.0
    return {"cI4": np.tile(eye, (1, 4)), "cJ": eye[::-1].copy(), "cTRI": tri, "cSTR": strict, "cDM": dm, "cOH": oh}


_WNAMES = ["ada_w", "ada_b", "pre_norm_mix", "post_norm_mix", "pre_norm_ffn", "post_norm_ffn", "w_in", "q_norm",
           "kv_norm", "w_uq", "w_uv", "rel_bias", "w_qidx", "kidx_norm", "conv_w", "conv_b", "dt_bias", "a_log",
           "d_skip", "ssd_norm", "w_o_attn", "w_o_ssd", "w_out", "w_ffn_in", "w_ffn_out"]


def make_in_maps(inputs):
    cst = _consts()
    B = inputs["x"].shape[0]
    shared = {k: np.ascontiguousarray(np.asarray(inputs[k], dtype=np.float32)) for k in _WNAMES}
    shared.update(cst)
    maps = []
    for b in range(B):
        m = dict(shared)
        m["x"] = np.ascontiguousarray(np.asarray(inputs["x"][b], dtype=np.float32))
        m["c"] = np.ascontiguousarray(np.asarray(inputs["c"][b], dtype=np.float32))
        maps.append(m)
    return maps


def kernel(**inputs):
    L = inputs["x"].shape[1]
    nc = build(L)
    maps = make_in_maps(inputs)
    res = run_bass_kernel_spmd(nc, maps, core_ids=list(range(len(maps))))
    return np.stack([np.asarray(r["out"], dtype=np.float32) for r in res.results], axis=0)
```

```python
import math
import numpy as np
import concourse.bass as bass
import concourse.mybir as mybir
from concourse.bass_utils import run_bass_kernel_spmd

F32 = mybir.dt.float32
BF = mybir.dt.bfloat16
AF = mybir.ActivationFunctionType
ALU = mybir.AluOpType
AX = mybir.AxisListType

D = 1024
EPS = 1e-6
NEG = -1.0e30
DSZ = {F32: 4, BF: 2}
C_Q, C_KV, C_KI, C_WI, C_Z, C_XS, C_B, C_C, C_DT, C_GA, C_GB = 0, 256, 384, 448, 464, 2512, 4560, 5584, 6608, 6640, 7664
DFF = 2816
NJ = DFF // 128


def _isap(v):
    return hasattr(v, "tensor") and hasattr(v, "ap") and hasattr(v, "offset")


def _box(ap):
    t = ap.tensor
    esz = DSZ.get(ap.dtype, 4)
    pat = [(int(s), int(c)) for s, c in ap.ap]
    off = int(ap.offset)
    if "DRAM" in str(ap.space).upper() or "HBM" in str(ap.space).upper():
        ext = sum((c - 1) * abs(s) for s, c in pat)
        return (t.name, 0, 1, off * esz, (off + ext + 1) * esz)
    row = 1
    for d in list(t.shape)[1:]:
        row *= int(d)
    p0 = off // row
    f0 = off % row
    pstride, pcnt = pat[0]
    if pstride == 0:
        pcnt = 1
    ext = sum((c - 1) * abs(s) for s, c in pat[1:])
    if "PSUM" in str(ap.space).upper():
        return (t.name, 0, 128, 0, 2048)
    return (t.name, p0, p0 + pcnt, f0 * esz, (f0 + ext + 1) * esz)


def _ovl(a, b):
    return a[1] < b[2] and b[1] < a[2] and a[3] < b[4] and b[3] < a[4]


def _contains(a, b):
    return a[1] <= b[1] and b[2] <= a[2] and a[3] <= b[3] and b[4] <= a[4]


class Sched:
    NDS = 40

    def __init__(self, nc):
        self.nc = nc
        self.engs = {}
        for name, e in (("pe", nc.tensor), ("act", nc.scalar), ("dve", nc.vector),
                        ("pool", nc.gpsimd), ("sp", nc.sync)):
            sem = nc.semaphore("sem_" + name).__enter__()
            self.engs[name] = dict(e=e, sem=sem, cnt=0, seen={}, key="E" + name)
        self.dsems = [nc.semaphore("dsem%d" % i).__enter__() for i in range(self.NDS)]
        self.dcnt = [0] * self.NDS
        self.dk = 0
        self.acc = {}
        self.nins = 0

    def _deps(self, eng, reads, writes):
        need = {}
        rb = [_box(a) for a in reads]
        wb = [_box(a) for a in writes]
        for b in rb:
            psum = b[0].startswith("ps") and b[0][2:].isdigit()
            for rec in self.acc.get(b[0], ()):
                if (rec[0] == "w" or (psum and rec[5] != eng)) and _ovl(rec[1], b):
                    if rec[5] == eng and eng == "pe":
                        continue
                    k = rec[2]
                    if need.get(k, (None, 0))[1] < rec[4]:
                        need[k] = (rec[3], rec[4])
        for b in wb:
            for rec in self.acc.get(b[0], ()):
                if _ovl(rec[1], b):
                    if rec[5] == eng and eng == "pe":
                        continue
                    k = rec[2]
                    if need.get(k, (None, 0))[1] < rec[4]:
                        need[k] = (rec[3], rec[4])
        return need, rb, wb

    def _record(self, eng, rb, wb, key, sem, val):
        for b in wb:
            lst = self.acc.setdefault(b[0], [])
            lst[:] = [r for r in lst if not _contains(b, r[1])]
            lst.append(("w", b, key, sem, val, eng))
        for b in rb:
            lst = self.acc.setdefault(b[0], [])
            lst[:] = [r for r in lst if not (r[0] == "r" and r[5] == eng and r[2] == key and _contains(b, r[1]))]
            lst.append(("r", b, key, sem, val, eng))

    def _wait(self, E, need):
        for k, (sem, val) in need.items():
            if E["seen"].get(k, 0) < val:
                E["e"].wait_ge(sem, val)
                E["seen"][k] = val

    def op(self, eng, fn, reads=(), writes=()):
        E = self.engs[eng]
        need, rb, wb = self._deps(eng, reads, writes)
        self._wait(E, need)
        ins = fn(E["e"])
        E["cnt"] += 1
        ins.then_inc(E["sem"], 1)
        self._record(eng, rb, wb, E["key"], E["sem"], E["cnt"])
        self.nins += 1
        return ins

    def dma(self, q, out, in_, **kw):
        E = self.engs[q]
        k = self.dk % self.NDS
        self.dk += 1
        sem = self.dsems[k]
        key = "D%d" % k
        need, rb, wb = self._deps("dma", [in_], [out])
        if self.dcnt[k] > 0:
            if need.get(key, (None, 0))[1] < self.dcnt[k]:
                need[key] = (sem, self.dcnt[k])
        self._wait(E, need)
        ins = E["e"].dma_start(out=out, in_=in_, **kw)
        self.dcnt[k] += 16
        ins.then_inc(sem, 16)
        self._record("dma", rb, wb, key, sem, self.dcnt[k])
        self.nins += 1
        return ins

    def barrier(self, only=None):
        for name, E in self.engs.items():
            if only and name not in only:
                continue
            need = {}
            for n2, E2 in self.engs.items():
                if n2 != name and E2["cnt"] > 0:
                    need[E2["key"]] = (E2["sem"], E2["cnt"])
            for k in range(self.NDS):
                if self.dcnt[k] > 0:
                    need["D%d" % k] = (self.dsems[k], self.dcnt[k])
            self._wait(E, need)
        if not only:
            self.acc = {}


def _interleave(*gens):
    gens = [g for g in gens if g is not None]
    while gens:
        for g in list(gens):
            try:
                next(g)
            except StopIteration:
                gens.remove(g)


def _interleave_w(g1, n1, g2, n2):
    d1 = d2 = 0
    a1 = g1 is not None
    a2 = g2 is not None
    while a1 or a2:
        if a1 and (not a2 or d1 * n2 <= d2 * n1):
            try:
                next(g1); d1 += 1
            except StopIteration:
                a1 = False
        else:
            try:
                next(g2); d2 += 1
            except StopIteration:
                a2 = False


def build(L, stop=99):
    from contextlib import ExitStack
    NT = L // 128
    KSEL = min(256, L // 4)
    TB = min(512, L)
    NB = L // TB
    nc = bass.Bass("TRN2", target_bir_lowering=False)
    S = Sched(nc)

    def dram(name, shape, dt=F32, kind="ExternalInput"):
        return nc.dram_tensor(name, list(shape), dt, kind=kind).ap()

    x = dram("x", [L, D]); cvec = dram("c", [D])
    ada_w = dram("ada_w", [D, 6 * D]); ada_b = dram("ada_b", [6 * D])
    pre_norm_mix = dram("pre_norm_mix", [D]); post_norm_mix = dram("post_norm_mix", [D])
    pre_norm_ffn = dram("pre_norm_ffn", [D]); post_norm_ffn = dram("post_norm_ffn", [D])
    w_in = dram("w_in", [D, 8688]); q_norm = dram("q_norm", [256]); kv_norm = dram("kv_norm", [128])
    w_uq = dram("w_uq", [256, 2048]); w_uv = dram("w_uv", [16, 128, 64]); rel_bias = dram("rel_bias", [32, 16])
    w_qidx = dram("w_qidx", [256, 1024]); kidx_norm = dram("kidx_norm", [64])
    conv_w = dram("conv_w", [4, 4096]); conv_b = dram("conv_b", [4096])
    dt_bias = dram("dt_bias", [32]); a_log = dram("a_log", [32]); d_skip = dram("d_skip", [32])
    ssd_norm = dram("ssd_norm", [2048]); w_o_attn = dram("w_o_attn", [D, D]); w_o_ssd = dram("w_o_ssd", [2048, D])
    w_out = dram("w_out", [D, D]); w_ffn_in = dram("w_ffn_in", [D, 2 * DFF]); w_ffn_out = dram("w_ffn_out", [DFF, D])
    cI4 = dram("cI4", [128, 512]); cJ = dram("cJ", [128, 128]); cTRI = dram("cTRI", [128, 128])
    cSTR = dram("cSTR", [128, 128]); cDM = dram("cDM", [128, 128]); cOH = dram("cOH", [32, 383])
    out = dram("out", [L, D], F32, kind="ExternalOutput")
    tv_d = dram("tv_d", [16, 383], F32, kind="Internal")
    mixa_d = dram("mixa_d", [L, D], BF, kind="Internal")
    yz_d = dram("yz_d", [L, 2048], BF, kind="Internal")
    x1_d = dram("x1_d", [L, D], F32, kind="Internal")

    glob = ExitStack()

    def sb(name, shape, dt=F32, st=None):
        return (st or glob).enter_context(nc.sbuf_tensor(name, list(shape), dt))

    banks = [glob.enter_context(nc.psum_tensor("ps%d" % i, [128, 512], F32)) for i in range(8)]
    bstate = {"k": 0, "free": list(range(8))}

    def bank():
        fr = bstate["free"]
        b = fr[bstate["k"] % len(fr)]
        bstate["k"] += 1
        return banks[b]

    def bfv(bk):
        return bk.bitcast(BF)

    def I(eng, meth, **kw):
        outs = [kw[k] for k in ("out", "accum_out", "ap") if k in kw and _isap(kw[k])]
        ins = [v for k, v in kw.items() if k not in ("out", "accum_out", "ap") and _isap(v)]
        return S.op(eng, lambda e: getattr(e, meth)(**kw), reads=ins, writes=outs)

    def mm(out_, lhsT, rhs, start=True, stop=True):
        return S.op("pe", lambda e: e.matmul(out_, lhsT, rhs, start=start, stop=stop), reads=[lhsT, rhs], writes=[out_])

    def tr(out_, in_, ident):
        return S.op("pe", lambda e: e.transpose(out_, in_, ident), reads=[in_, ident], writes=[out_])

    def act(out_, in_, func, **kw):
        return I("act", "activation", out=out_, in_=in_, func=func, **kw)

    def ts(eng, out_, in0, s1, s2, op0, op1=None):
        if op1 is None:
            return I(eng, "tensor_scalar", out=out_, in0=in0, scalar1=s1, scalar2=None, op0=op0)
        return I(eng, "tensor_scalar", out=out_, in0=in0, scalar1=s1, scalar2=s2, op0=op0, op1=op1)

    def tt(eng, out_, in0, in1, op):
        return I(eng, "tensor_tensor", out=out_, in0=in0, in1=in1, op=op)

    def stt(out_, in0, scalar, in1, op0, op1):
        return I("dve", "scalar_tensor_tensor", out=out_, in0=in0, scalar=scalar, in1=in1, op0=op0, op1=op1)

    def cp(eng, out_, in_):
        if eng == "act":
            return I("act", "copy", out=out_, in_=in_)
        return I(eng, "tensor_copy", out=out_, in_=in_)

    def memset(eng, ap, v):
        return I(eng, "memset", ap=ap, constant=v)

    def dma(q, out_, in_, **kw):
        return S.dma(q, out_, in_, **kw)

    def bc(ap, shape):
        return ap.broadcast_to(list(shape))

    _rr = {"k": 0}

    def evq():
        _rr["k"] += 1
        return "act" if _rr["k"] % 2 else "dve"

    identF = sb("identF", [128, 128]); dma("sp", identF[:], cI4[:, 0:128])
    i4b = sb("i4b", [128, 512], BF)
    identB = i4b[:, 0:128]
    jb = sb("jb", [128, 128], BF)
    triF = sb("triF", [128, 128]); dma("sp", triF[:], cTRI[:, :])
    strF = sb("strF", [128, 128]); dma("sp", strF[:], cSTR[:, :])
    dmF = sb("dmF", [128, 128]); dma("sp", dmF[:], cDM[:, :])
    onesF = sb("onesF", [128, 128]); memset("dve", onesF[:], 1.0)
    onesB = sb("onesB", [128, 128], BF); memset("dve", onesB[:], 1.0)

    lcstg = sb("lcstg", [48, 128], F32)

    def load_cols(dst, vec, n):
        kc = n // 128
        if kc <= 2:
            for k_ in range(kc):
                dma("sp", dst[:, k_:k_ + 1], vec[k_ * 128:(k_ + 1) * 128].rearrange("(p o) -> p o", o=1))
            return
        dma("sp", lcstg[0:kc, :], vec.rearrange("(k p) -> k p", p=128))
        ps = bank()
        tr(ps[:, 0:kc], lcstg[0:kc, :], identF[0:kc, 0:kc])
        cp("dve", dst, ps[:, 0:kc])

    wstg = [sb("wstg%d" % i, [128, 1024], F32) for i in range(3)]
    _ws = {"k": 0}

    def stage():
        _ws["k"] += 1
        return wstg[_ws["k"] % 3]

    def load_w(dst, W, c0, c1, r0=0):
        kcn = dst.shape[1]
        n = c1 - c0
        per = max(1, 1024 // n)
        for k0 in range(0, kcn, per):
            k1 = min(kcn, k0 + per)
            sv = stage()[:, 0:(k1 - k0) * n].rearrange("p (k n) -> p k n", n=n)
            dma("sp", sv, W[r0 + k0 * 128:r0 + k1 * 128, c0:c1].rearrange("(kc p) n -> p kc n", p=128))
            cp("pool", dst[:, k0:k1, :], sv)

    sv_ = stage(); dma("sp", sv_[:, 0:512], cI4[:, :]); cp("pool", i4b[:], sv_[:, 0:512])
    sv_ = stage(); dma("sp", sv_[:, 0:128], cJ[:, :]); cp("pool", jb[:], sv_[:, 0:128])

    modc = sb("modc", [128, 48])
    gcol = sb("gcol", [128, 8]); scol = sb("scol", [128, 8])
    gcol2 = sb("gcol2", [128, 8]); scol2 = sb("scol2", [128, 8])
    gpmB = sb("gpmB", [128, D]); gpfB = sb("gpfB", [128, D])
    with ExitStack() as st:
        ccol = sb("ccol", [128, 8], F32, st); scc = sb("scc", [128, 8], F32, st)
        abcol = sb("abcol", [128, 48], F32, st)
        tmpc = sb("tmpc", [128, 8], F32, st)
        load_cols(ccol[:], cvec, D)
        load_cols(abcol[:], ada_b, 6 * D)
        act(scc[:], ccol[:], AF.Silu)
        awt = [sb("awt%d" % i, [128, 8, 512], F32, st) for i in range(2)]
        awb = [sb("awb%d" % i, [128, 8, 512], BF, st) for i in range(2)]
        sccb = sb("sccb", [128, 8], BF, st)
        cp("dve", sccb[:], scc[:])
        pm = bank()
        for cb in range(12):
            a32 = awt[cb % 2]
            a = awb[cb % 2]
            dma("sp", a32[:], ada_w[:, cb * 512:(cb + 1) * 512].rearrange("(kc p) n -> p kc n", p=128))
            cp("act", a[:, 0:3, :], a32[:, 0:3, :])
            cp("dve", a[:, 3:6, :], a32[:, 3:6, :])
            cp("pool", a[:, 6:8, :], a32[:, 6:8, :])
            for jj in range(4):
                j = cb * 4 + jj
                for kc in range(8):
                    mm(pm[:, j:j + 1], a[:, kc, jj * 128:(jj + 1) * 128], sccb[:, kc:kc + 1], start=(kc == 0), stop=(kc == 7))
        tt("dve", modc[:], pm[:, 0:48], abcol[:], ALU.add)
        pn = sb("pn", [128, 8], F32, st)
        load_cols(pn[:], pre_norm_mix, D)
        ts("dve", tmpc[:], modc[:, 8:16], 1.0, None, ALU.add)
        tt("dve", gcol[:], tmpc[:], pn[:], ALU.mult)
        cp("dve", scol[:], modc[:, 0:8])
        pn2 = sb("pn2", [128, 8], F32, st)
        load_cols(pn2[:], pre_norm_ffn, D)
        ts("dve", tmpc[:], modc[:, 32:40], 1.0, None, ALU.add)
        tt("dve", gcol2[:], tmpc[:], pn2[:], ALU.mult)
        cp("dve", scol2[:], modc[:, 24:32])
        for (dst, gsl, pvec, nm) in ((gpmB, slice(16, 24), post_norm_mix, "a"), (gpfB, slice(40, 48), post_norm_ffn, "b")):
            pcol = sb("pcol" + nm, [128, 8], F32, st)
            gp = sb("gp" + nm, [128, 8], F32, st)
            gb = sb("gb" + nm, [128, 128], F32, st)
            load_cols(pcol[:], pvec, D)
            tt("dve", gp[:], modc[:, gsl], pcol[:], ALU.mult)
            for kc in range(8):
                ts("dve", gb[:], onesF[:], gp[:, kc:kc + 1], None, ALU.mult)
                ps = bank()
                mm(ps[:, 0:128], gb[:], identF[:])
                cp("act", dst[:, kc * 128:(kc + 1) * 128], ps[:, 0:128])
        S.barrier()
    if stop == 0:
        return nc

    import os
    DBG = int(os.environ.get("KDBG", "9"))

    def norm_to_T(xt, dstT, tile, gc, sc, tmp_bf, stat):
        junk = tmp_bf
        act(junk[:], xt, AF.Square, accum_out=stat[:, 0:1])
        if DBG < 2:
            return
        ts("dve", stat[:, 1:2], stat[:, 0:1], 1.0 / D, EPS, ALU.mult, ALU.add)
        act(stat[:, 2:3], stat[:, 1:2], AF.Sqrt)
        I("dve", "reciprocal", out=stat[:, 3:4], in_=stat[:, 2:3])
        if DBG < 3:
            return
        ts("dve", tmp_bf[:], xt, stat[:, 3:4], None, ALU.mult)
        if DBG < 4:
            return
        ps = bank()
        pb = bfv(ps)
        for kc in range(8):
            tr(pb[:, kc * 128:(kc + 1) * 128], tmp_bf[:, kc * 128:(kc + 1) * 128], identB)
        if DBG < 5:
            return
        for kc in range(8):
            if False:
                pass
            else:
                ts("dve", dstT[:, kc, tile * 128:(tile + 1) * 128], pb[:, kc * 128:(kc + 1) * 128],
                   gc[:, kc:kc + 1], sc[:, kc:kc + 1], ALU.mult, ALU.add)

    hT = sb("hT", [128, 8, L], BF)
    stA = ExitStack()
    st = stA
    dl = []
    wq = sb("wq", [128, 8, 256], BF, st); dl.append(lambda: load_w(wq[:], w_in, C_Q, C_Q + 256))
    wkv = sb("wkv", [128, 8, 128], BF, st); dl.append(lambda: load_w(wkv[:], w_in, C_KV, C_KV + 128))
    wkk = sb("wkk", [128, 8, 128], BF, st)
    dl.append(lambda: (load_w(wkk[:, :, 0:64], w_in, C_KI, C_KI + 64), load_w(wkk[:, :, 64:128], w_in, C_KI, C_KI + 64)))
    wwi = sb("wwi", [128, 8, 16], BF, st); dl.append(lambda: load_w(wwi[:], w_in, C_WI, C_WI + 16))
    wuq = sb("wuq", [128, 2, 2048], BF, st)
    for r in range(2):
        dl.append(lambda r=r: (load_w(wuq[:, r:r + 1, 0:1024], w_uq, 0, 1024, r0=r * 128), load_w(wuq[:, r:r + 1, 1024:2048], w_uq, 1024, 2048, r0=r * 128)))
    wqi = sb("wqi", [128, 2, 1024], BF, st)
    for r in range(2):
        dl.append(lambda r=r: load_w(wqi[:, r:r + 1, :], w_qidx, 0, 1024, r0=r * 128))
    wuv = sb("wuv", [128, 16, 128], BF, st)

    def _ld_wuv():
        memset("pool", wuv[:], 0.0)
        sv_ = stage()[:, 0:1024].rearrange("p (h d) -> p h d", d=64)
        dma("sp", sv_, w_uv.rearrange("h c d -> c h d"))
        for h in range(16):
            o = (h % 2) * 64
            cp("pool", wuv[:, h, o:o + 64], sv_[:, h, :])
    dl.append(_ld_wuv)
    woa = sb("woa", [128, 8, D], BF, st)
    for k0 in range(0, 8, 2):
        dl.append(lambda k0=k0: load_w(woa[:, k0:k0 + 2, :], w_o_attn, 0, D, r0=k0 * 128))
    wga = sb("wga", [128, 8, D], BF, st)
    for k0 in range(0, 8, 2):
        dl.append(lambda k0=k0: load_w(wga[:, k0:k0 + 2, :], w_in, C_GA, C_GA + D, r0=k0 * 128))
    qncol = sb("qncol", [128, 2], F32, st); dl.append(lambda: load_cols(qncol[:], q_norm, 256))
    kvcol = sb("kvcol", [128, 1], F32, st); dl.append(lambda: load_cols(kvcol[:], kv_norm, 128))
    kicol = sb("kicol", [128, 1], F32, st)
    dl.append(lambda: (dma("sp", kicol[0:64, :], kidx_norm.rearrange("(p o) -> p o", o=1)),
                       dma("sp", kicol[64:128, :], kidx_norm.rearrange("(p o) -> p o", o=1))))

    with ExitStack() as st:
        xts = [sb("xa%d" % i, [128, D], F32, st) for i in range(2)]
        xbs = [sb("xb%d" % i, [128, D], BF, st) for i in range(2)]
        sts = [sb("xs%d" % i, [128, 4], F32, st) for i in range(2)]
        for i in range(NT):
            xt = xts[i % 2]
            dma("sp", xt[:], x[i * 128:(i + 1) * 128, :])
            if i >= 1:
                for _ in range(3):
                    if dl:
                        dl.pop(0)()
            norm_to_T(xt[:], hT, i, gcol, scol, xbs[i % 2], sts[i % 2])
        while dl:
            dl.pop(0)()
        S.barrier()
    if stop == 1:
        return nc

    with stA as st:
        BFt = sb("BFt", [128, 2, 16, 128], BF, st)
        with ExitStack() as s2:
            rb = sb("rb", [32, 16], F32, s2); dma("sp", rb[:], rel_bias[:, :])
            rb15 = sb("rb15", [32, 16], F32, s2); dma("sp", rb15[:], rel_bias[15, :].partition_broadcast(32))
            tt("dve", rb[:], rb[:], rb15[:], ALU.subtract)
            oh = sb("oh", [32, 383], F32, s2); dma("sp", oh[:], cOH[:, :])
            ps = bank()
            mm(ps[0:16, 0:383], rb[:], oh[:])
            tv = sb("tv", [16, 383], F32, s2)
            cp("dve", tv[:], ps[0:16, 0:383])
            dma("sp", tv_d[:, :], tv[:])
            for dd in range(2):
                src = bass.AP(tensor=tv_d.tensor, offset=128 * dd, ap=[[1, 128], [383, 16], [1, 128]])
                for hh_ in range(2):
                    src = bass.AP(tensor=tv_d.tensor, offset=128 * dd + hh_ * 8 * 383, ap=[[1, 128], [383, 8], [1, 128]])
                    sv_ = stage()[:, 0:1024].rearrange("p (h q) -> p h q", q=128)
                    dma("sp", sv_, src)
                    cp("pool", BFt[:, dd, hh_ * 8:(hh_ + 1) * 8, :], sv_)
            S.barrier()
        if stop == 2:
            return nc

        qnT = sb("qnT", [128, 2, L], BF, st)
        knT = sb("knT", [128, L], BF, st)
        kiT = sb("kiT", [128, L], BF, st)
        with ExitStack() as s2:
            raw = sb("lraw", [128, 2, TB], F32, s2)
            sq = sb("lsq", [128, 2, TB], BF, s2)
            rs = sb("lrs", [128, TB], F32, s2)
            for tb in range(NB):
                tsl = slice(tb * TB, (tb + 1) * TB)
                for (wt, nch, ncol, dst) in ((wq, 2, qncol, qnT), (wkv, 1, kvcol, knT), (wkk, 1, kicol, kiT)):
                    for r in range(nch):
                        ps = bank()
                        for kc in range(8):
                            mm(ps[:, 0:TB], wt[:, kc, r * 128:(r + 1) * 128], hT[:, kc, tsl], start=(kc == 0), stop=(kc == 7))
                        cp("dve", raw[:, r, :], ps[:, 0:TB])
                        act(sq[:, r, :], raw[:, r, :], AF.Square)
                    p2 = bank()
                    for r in range(nch):
                        mm(p2[:, 0:TB], onesB[:], sq[:, r, :], start=(r == 0), stop=(r == nch - 1))
                    ts("dve", rs[:], p2[:, 0:TB], 1.0 / (128 * nch), EPS, ALU.mult, ALU.add)
                    act(rs[:], rs[:], AF.Sqrt)
                    I("dve", "reciprocal", out=rs[:], in_=rs[:])
                    for r in range(nch):
                        o_ = dst[:, r, tsl] if nch == 2 else dst[:, tsl]
                        stt(o_, raw[:, r, :], ncol[:, r:r + 1], rs[:], ALU.mult, ALU.mult)
            S.barrier()
        if stop == 3:
            return nc
        kvtm = sb("kvtm", [128, NT, 128], BF, st)
        witm = sb("witm", [128, NT, 16], F32, st)
        for t in range(NT):
            ps = bank(); pb = bfv(ps)
            tr(pb[:, 0:128], knT[:, t * 128:(t + 1) * 128], identB)
            cp("dve", kvtm[:, t, :], pb[:, 0:128])
            ps = bank()
            for kc in range(8):
                mm(ps[:, 0:16], hT[:, kc, t * 128:(t + 1) * 128], wwi[:, kc, :], start=(kc == 0), stop=(kc == 7))
            act(witm[:, t, :], ps[:, 0:16], AF.Copy, scale=1.0 / 32.0)

        qT = [sb("qT%d" % i, [128, 16, 128], BF, st) for i in range(2)]
        qiT = [sb("qiT%d" % i, [128, 8, 128], BF, st) for i in range(2)]
        accs = [sb("acc0", [128, L], F32, st)] * 2
        works = [sb("work0", [128, L], F32, st)] * 2
        madd = [sb("madd%d" % i, [128, L], BF, st) for i in range(2)]
        m8 = [sb("m8%d" % i, [128, 8], F32, st) for i in range(2)]
        thr = [sb("thr%d" % i, [128, 1], F32, st) for i in range(2)]
        rtmp = [sb("rtmp%d" % i, [128, 512], BF, st) for i in range(4)]
        dws = [sb("dw0", [128, 16, 128], BF, st)] * 2
        pTs = [sb("pT%d" % i, [128, 512], BF, st) for i in range(3)]
        oT = sb("oT", [128, 16, 128], BF, st)
        o2T = sb("o2T", [128, 8, 128], BF, st)
        lnd = sb("lnd", [128, 512], F32, st); rden = sb("rden", [128, 512], F32, st); oun = sb("oun", [128, 512], F32, st)
        sga = [sb("sga%d" % i, [128, 512], BF, st) for i in range(2)]
        mixt = [sb("mixt0", [128, D], BF, st)] * 2
        bstate["free"] = [0, 1, 2, 3, 7]
        bo = [banks[4], banks[4]]; bd = [banks[5], banks[5]]; pacc = banks[6]
        cnt = {"r": 0, "p": 0}
        QS = 128.0 ** -0.5

        def stage1(i):
            b = i % 2
            qs = slice(i * 128, (i + 1) * 128)
            for hq in range(4):
                ps = bank()
                for hh in range(4):
                    h = hq * 4 + hh
                    for r in range(2):
                        mm(ps[:, hh * 128:(hh + 1) * 128], wuq[:, r, h * 128:(h + 1) * 128], qnT[:, r, qs], start=(r == 0), stop=(r == 1))
                act(qT[b][:, hq * 4:(hq + 1) * 4, :].rearrange("p h q -> p (h q)"), ps[:, :], AF.Copy, scale=QS)
            yield
            for mq in range(2):
                ps = bank()
                for mm_ in range(4):
                    m = mq * 4 + mm_
                    for r in range(2):
                        mm(ps[:, mm_ * 128:(mm_ + 1) * 128], wqi[:, r, m * 128:(m + 1) * 128], qnT[:, r, qs], start=(r == 0), stop=(r == 1))
                cp("act", qiT[b][:, mq * 4:(mq + 1) * 4, :].rearrange("p h q -> p (h q)"), ps[:, :])
            yield
            nk = (i + 1) * 128
            acc = accs[b]
            dw = dws[b]
            NPE = 10
            for h in range(NPE):
                ts("pool", dw[:, h, :], identF[:], witm[:, i, h:h + 1], None, ALU.mult)
            for kb in range((nk + 511) // 512):
                n = min(512, nk - kb * 512)
                ks = slice(kb * 512, kb * 512 + n)
                pend = []

                def accmm(h, rt):
                    mm(pacc[:, 0:n], dw[:, h, :], rt[:, 0:n], start=(h == 0), stop=(h == NPE - 1))

                for h in range(16):
                    m, half = h // 2, h % 2
                    pr = slice(half * 64, half * 64 + 64)
                    ps = bank()
                    mm(ps[:, 0:n], qiT[b][pr, m, :], kiT[pr, ks])
                    rt = rtmp[cnt["r"] % 4]; cnt["r"] += 1
                    act(rt[:, 0:n], ps[:, 0:n], AF.Relu)
                    if h < NPE:
                        pend.append((h, rt))
                        if len(pend) > 2:
                            accmm(*pend.pop(0))
                    else:
                        while pend:
                            accmm(*pend.pop(0))
                        if h == NPE:
                            ts("dve", acc[:, ks], rt[:, 0:n], witm[:, i, h:h + 1], None, ALU.mult)
                        else:
                            stt(acc[:, ks], rt[:, 0:n], witm[:, i, h:h + 1], acc[:, ks], ALU.mult, ALU.add)
                    if h % 4 == 3:
                        yield
                tt("dve", acc[:, ks], pacc[:, 0:n], acc[:, ks], ALU.add)
                yield
            tt("pool", acc[:, i * 128:(i + 1) * 128], acc[:, i * 128:(i + 1) * 128], dmF[:], ALU.add)
            if nk > KSEL:
                src = acc
                rounds = KSEL // 8
                for r in range(rounds):
                    I("dve", "max", out=m8[b][:], in_=src[:, 0:nk])
                    if r < rounds - 1:
                        I("dve", "match_replace", out=works[b][:, 0:nk], in_to_replace=m8[b][:], in_values=src[:, 0:nk], imm_value=NEG)
                        src = works[b]
                    yield
                ts("dve", thr[b][:], m8[b][:, 7:8], -1.0e29, None, ALU.max)
            else:
                memset("dve", thr[b][:], -1.0e29)
            ts("dve", madd[b][:, 0:nk], acc[:, 0:nk], thr[b][:, 0:1], -30000.0, ALU.is_lt, ALU.mult)
            yield

        def stage2(i):
            b = i % 2
            qs = slice(i * 128, (i + 1) * 128)
            its = [(hg, j) for hg in range(4) for j in range(i + 1)]
            pend = []

            def logits(hg, j):
                dd = i - j
                pl = bank()
                mm(pl[:, :], knT[:, j * 128:(j + 1) * 128], qT[b][:, hg * 4:(hg + 1) * 4, :].rearrange("p h q -> p (h q)"), start=True, stop=False)
                mm(pl[:, :], madd[b][:, j * 128:(j + 1) * 128], i4b[:, :], start=False, stop=(dd >= 2))
                if dd < 2:
                    mm(pl[:, :], jb[:], BFt[:, dd, hg * 4:(hg + 1) * 4, :].rearrange("p h q -> p (h q)"), start=False, stop=True)
                pT = pTs[cnt["p"] % 3]; cnt["p"] += 1
                act(pT[:], pl[:, :], AF.Exp)
                return pT

            def pv(hg, j, pT):
                po = bo[0]; pd = bd[0]
                mm(po[:, :], kvtm[:, j, :], pT[:], start=(j == 0), stop=(j == i))
                mm(pd[:, :], onesB[:], pT[:], start=(j == 0), stop=(j == i))
                if j == i:
                    act(lnd[:], pd[:, :], AF.Ln)
                    act(rden[:], lnd[:], AF.Exp, scale=-1.0)
                    cp("act", oun[:], po[:, :])
                    tt("pool", oT[:, hg * 4:(hg + 1) * 4, :].rearrange("p h q -> p (h q)"), oun[:], rden[:], ALU.mult)

            for t_, (hg, j) in enumerate(its):
                pend.append((hg, j, logits(hg, j)))
                if len(pend) > 2:
                    pv(*pend.pop(0))
                yield
            while pend:
                pv(*pend.pop(0))
            yield
            for mq in range(2):
                ps = bank()
                for mm_ in range(4):
                    m = mq * 4 + mm_
                    mm(ps[:, mm_ * 128:(mm_ + 1) * 128], wuv[:, 2 * m, :], oT[:, 2 * m, :], start=True, stop=False)
                    mm(ps[:, mm_ * 128:(mm_ + 1) * 128], wuv[:, 2 * m + 1, :], oT[:, 2 * m + 1, :], start=False, stop=True)
                cp("act", o2T[:, mq * 4:(mq + 1) * 4, :].rearrange("p h q -> p (h q)"), ps[:, :])
            yield
            mx = mixt[b]
            for cb in range(2):
                cs = slice(cb * 512, (cb + 1) * 512)
                pg = bank()
                for kc in range(8):
                    mm(pg[:, :], hT[:, kc, qs], wga[:, kc, cs], start=(kc == 0), stop=(kc == 7))
                act(sga[cb][:], pg[:, :], AF.Sigmoid)
                py = bank()
                for m in range(8):
                    mm(py[:, :], o2T[:, m, :], woa[:, m, cs], start=(m == 0), stop=(m == 7))
                tt("dve", mx[:, cs], py[:, :], sga[cb][:], ALU.mult)
                yield
            dma("sp", mixa_d[qs, :], mx[:])
            yield

        prev = None
        for i in range(NT + 1):
            g1 = stage1(i) if i < NT else None
            g2 = stage2(i - 1) if i >= 1 else None
            nkb_ = (i + 1 + 3) // 4
            n1 = 3 + 5 * nkb_ + ((KSEL // 8) if (i + 1) * 128 > KSEL else 0)
            n2 = 4 * i + 6
            _interleave_w(g1, n1, g2, max(1, n2))
        bstate["free"] = list(range(8))
        S.barrier()
        if stop == 4:
            return nc

    ssq = sb("ssq", [128, NT, 8])
    rstds = sb("rstds", [128, NT])
    with ExitStack() as st:
        dtall = sb("dtall", [128, NT, 32], F32, st); dAall = sb("dAall", [128, NT, 32], F32, st)
        eaall = sb("eaall", [128, NT, 32], F32, st); decall = sb("decall", [128, NT, 32], F32, st)
        cdall = sb("cdall", [128, NT, 32], F32, st)
        dI = sb("dI", [128, 32, 128], BF, st)
        cwc = sb("cwc", [128, 4, 32], F32, st); cbc = sb("cbc", [128, 32], F32, st)
        for k in range(4):
            load_cols(cwc[:, k, :], conv_w[k, :], 4096)
        load_cols(cbc[:], conv_b, 4096)
        with ExitStack() as s2:
            wdt = sb("wdt", [128, 8, 32], BF, s2); load_w(wdt[:], w_in, C_DT, C_DT + 32)
            dtbB = sb("dtbB", [128, 32], F32, s2); dma("sp", dtbB[:], dt_bias.partition_broadcast(128))
            AB = sb("AB", [128, 32], F32, s2); dma("sp", AB[:], a_log.partition_broadcast(128))
            dsB = sb("dsB", [128, 32], F32, s2); dma("sp", dsB[:], d_skip.partition_broadcast(128))
            act(AB[:], AB[:], AF.Exp)
            ts("dve", AB[:], AB[:], -1.0, None, ALU.mult)
            for h in range(32):
                ts("pool", dI[:, h, :], identF[:], dsB[:, h:h + 1], None, ALU.mult)
            xb_ = sb("xb_", [128, 32], F32, s2); ax = sb("ax_", [128, 32], F32, s2)
            ee = sb("ee_", [128, 32], F32, s2); tot = sb("tot_", [128, 32], F32, s2)
            for t in range(NT):
                ps = bank()
                for kc in range(8):
                    mm(ps[:, 0:32], hT[:, kc, t * 128:(t + 1) * 128], wdt[:, kc, :], start=(kc == 0), stop=(kc == 7))
                tt("dve", xb_[:], ps[:, 0:32], dtbB[:], ALU.add)
                act(ax[:], xb_[:], AF.Abs)
                act(ee[:], ax[:], AF.Exp, scale=-1.0)
                ts("dve", ee[:], ee[:], 1.0, None, ALU.add)
                act(ee[:], ee[:], AF.Ln)
                stt(dtall[:, t, :], xb_[:], 0.0, ee[:], ALU.max, ALU.add)
                tt("dve", dAall[:, t, :], dtall[:, t, :], AB[:], ALU.mult)
                p1 = bank(); p2 = bank()
                mm(p1[:, 0:32], triF[:], dAall[:, t, :])
                mm(p2[:, 0:32], onesF[:], dAall[:, t, :])
                act(eaall[:, t, :], p1[:, 0:32], AF.Exp)
                act(cdall[:, t, :], p2[:, 0:32], AF.Exp)
                cp("act", tot[:], p2[:, 0:32])
                tt("dve", tot[:], tot[:], p1[:, 0:32], ALU.subtract)
                act(decall[:, t, :], tot[:], AF.Exp)
            S.barrier()

        wxs = [sb("wxs%d" % i, [128, 8, 256], BF, st) for i in range(2)]
        wBs = [sb("wBs%d" % i, [128, 8, 128], BF, st) for i in range(2)]
        wCs = [sb("wCs%d" % i, [128, 8, 128], BF, st) for i in range(2)]
        wzs = [sb("wzs%d" % i, [128, 8, 256], BF, st) for i in range(2)]
        raw = sb("raw", [128, 4, 3 + L], BF, st)
        memset("pool", raw[:, :, 0:3], 0.0)
        cv = sb("cv", [128, 4, L], BF, st)
        cvf = [sb("cvf%d" % i, [128, L], F32, st) for i in range(2)]
        xstm = sb("xstm", [128, NT, 256], BF, st); Btm = sb("Btm", [128, NT, 128], BF, st)
        sztm = sb("sztm", [128, NT, 256], BF, st)
        prevs = sb("prevs", [128, 256], F32, st); prevb = sb("prevb", [128, 256], BF, st)
        cbm = [sb("cbm%d" % i, [128, 128], BF, st) for i in range(2)]
        L4 = [sb("L4%d" % i, [128, 4, 128], BF, st) for i in range(2)]
        triB = sb("triB", [128, 128], BF, st); cp("pool", triB[:], triF[:])
        E4 = [sb("E4%d" % i, [128, 4, 128], BF, st) for i in range(2)]
        M4 = [sb("M4%d" % i, [128, 4, 128], BF, st) for i in range(2)]
        X4 = [sb("X4%d" % i, [128, 4, 64], BF, st) for i in range(2)]
        Xd4 = [sb("Xd4%d" % i, [128, 4, 64], BF, st) for i in range(2)]
        yo = [sb("yo%d" % i, [128, 4, 64], F32, st) for i in range(2)]
        y1 = [sb("y1%d" % i, [128, 256], F32, st) for i in range(2)]
        yzt = [sb("yzt%d" % i, [128, 256], BF, st) for i in range(2)]
        yjunk = sb("yjunk", [128, 256], BF, st)
        ptmp = sb("ptmp", [128, 4, 64], F32, st)

        def load_group_w(g):
            b = g % 2
            load_w(wxs[b][:], w_in, C_XS + g * 256, C_XS + (g + 1) * 256)
            load_w(wBs[b][:], w_in, C_B + g * 128, C_B + (g + 1) * 128)
            load_w(wCs[b][:], w_in, C_C + g * 128, C_C + (g + 1) * 128)
            load_w(wzs[b][:], w_in, C_Z + g * 256, C_Z + (g + 1) * 256)

        load_group_w(0)
        for g in range(8):
            b = g % 2
            h0 = g * 4
            if g + 1 < 8:
                load_group_w(g + 1)
            chunks = (2 * g, 2 * g + 1, 16 + g, 24 + g)
            for r in range(4):
                wt = (wxs[b][:, :, 0:128], wxs[b][:, :, 128:256], wBs[b][:, :, :], wCs[b][:, :, :])[r]
                for tb in range(NB):
                    ps = bank()
                    for kc in range(8):
                        mm(ps[:, 0:TB], wt[:, kc, :], hT[:, kc, tb * TB:(tb + 1) * TB], start=(kc == 0), stop=(kc == 7))
                    cp(evq(), raw[:, r, 3 + tb * TB:3 + (tb + 1) * TB], ps[:, 0:TB])
            for r in range(4):
                ch = chunks[r]
                cf = cvf[r % 2]
                ts("dve", cf[:], raw[:, r, 3:3 + L], cwc[:, 3, ch:ch + 1], cbc[:, ch:ch + 1], ALU.mult, ALU.add)
                for k in range(3):
                    stt(cf[:], raw[:, r, k:k + L], cwc[:, k, ch:ch + 1], cf[:], ALU.mult, ALU.add)
                act(cv[:, r, :], cf[:], AF.Silu)
            for t in range(NT):
                ps = bank(); pb = bfv(ps)
                tsl = slice(t * 128, (t + 1) * 128)
                tr(pb[:, 0:128], cv[:, 0, tsl], identB); tr(pb[:, 128:256], cv[:, 1, tsl], identB)
                tr(pb[:, 256:384], cv[:, 2, tsl], identB)
                e_ = evq()
                cp(e_, xstm[:, t, :], pb[:, 0:256])
                cp(e_, Btm[:, t, :], pb[:, 256:384])
                pz = bank()
                for kc in range(8):
                    mm(pz[:, 0:256], hT[:, kc, tsl], wzs[b][:, kc, :], start=(kc == 0), stop=(kc == 7))
                act(sztm[:, t, :], pz[:, 0:256], AF.Silu)
            memset("pool", prevs[:], 0.0); memset("pool", prevb[:], 0.0)
            hs = slice(h0, h0 + 4)

            def stX(c):
                cc = c % 2
                csl = slice(c * 128, (c + 1) * 128)
                pcb = bank()
                mm(pcb[:, 0:128], cv[:, 2, csl], cv[:, 3, csl])
                tt("dve", cbm[cc][:], pcb[:, 0:128], triF[:], ALU.mult)
                tt("pool", L4[cc][:], bc(strF[:].unsqueeze(1), [128, 4, 128]), bc(dAall[:, c, hs].unsqueeze(2), [128, 4, 128]), ALU.mult)
                pseg = bank()
                for hh in range(4):
                    mm(pseg[:, hh * 128:(hh + 1) * 128], L4[cc][:, hh, :], triB[:])
                act(E4[cc][:].rearrange("p h l -> p (h l)"), pseg[:, :], AF.Exp)
                tt("pool", M4[cc][:], E4[cc][:], bc(cbm[cc][:].unsqueeze(1), [128, 4, 128]), ALU.mult)
                xv = xstm[:, c, :].rearrange("p (h d) -> p h d", h=4)
                tt("dve", X4[cc][:], xv, bc(dtall[:, c, hs].unsqueeze(2), [128, 4, 64]), ALU.mult)
                tt("pool", Xd4[cc][:], X4[cc][:], bc(decall[:, c, hs].unsqueeze(2), [128, 4, 64]), ALU.mult)

            def stY(c):
                cc = c % 2
                csl = slice(c * 128, (c + 1) * 128)
                py = bank()
                for hh in range(4):
                    mm(py[:, hh * 64:(hh + 1) * 64], M4[cc][:, hh, :], X4[cc][:, hh, :], start=True, stop=False)
                    mm(py[:, hh * 64:(hh + 1) * 64], dI[:, h0 + hh, :], xstm[:, c, hh * 64:(hh + 1) * 64], start=False, stop=True)
                pS = bank()
                mm(pS[:, 0:256], Btm[:, c, :], Xd4[cc][:].rearrange("p h d -> p (h d)"))
                pyo = bank()
                mm(pyo[:, 0:256], cv[:, 3, csl], prevb[:])
                tt("dve", yo[cc][:], pyo[:, 0:256].rearrange("p (h d) -> p h d", h=4), bc(eaall[:, c, hs].unsqueeze(2), [128, 4, 64]), ALU.mult)
                tt("dve", y1[cc][:], py[:, 0:256], yo[cc][:].rearrange("p h d -> p (h d)"), ALU.add)
                tt("pool", yzt[cc][:], y1[cc][:], sztm[:, c, :], ALU.mult)
                act(yjunk[:], yzt[cc][:], AF.Square, accum_out=ssq[:, c, g:g + 1])
                dma("sp", yz_d[csl, g * 256:(g + 1) * 256], yzt[cc][:])
                tt("pool", ptmp[:], prevs[:].rearrange("p (h d) -> p h d", h=4), bc(cdall[:, c, hs].unsqueeze(2), [128, 4, 64]), ALU.mult)
                tt("dve", prevs[:], ptmp[:].rearrange("p h d -> p (h d)"), pS[:, 0:256], ALU.add)
                cp("act", prevb[:], prevs[:])

            for c in range(NT + 1):
                if c < NT:
                    stX(c)
                if c >= 1:
                    stY(c - 1)
        I("dve", "tensor_reduce", out=rstds[:], in_=ssq[:], axis=AX.X, op=ALU.add)
        ts("dve", rstds[:], rstds[:], 1.0 / 2048, EPS, ALU.mult, ALU.add)
        act(rstds[:], rstds[:], AF.Sqrt)
        I("dve", "reciprocal", out=rstds[:], in_=rstds[:])
        S.barrier()
        if stop == 5:
            return nc

    h2T = sb("h2T", [128, 8, L], BF)
    with ExitStack() as st:
        wos = sb("wos", [128, 16, D], BF, st); load_w(wos[:, 0:8, :], w_o_ssd, 0, D); load_w(wos[:, 8:16, :], w_o_ssd, 0, D, r0=1024)
        wout = sb("wout", [128, 8, D], BF, st); load_w(wout[:], w_out, 0, D)
        wgb = sb("wgb", [128, 8, D], BF, st); load_w(wgb[:], w_in, C_GB, C_GB + D)
        sncol = sb("sncol", [128, 16], F32, st); load_cols(sncol[:], ssd_norm, 2048)
        yzl = [sb("yzl%d" % i, [128, 2048], BF, st) for i in range(2)]
        yT = [sb("yT0", [128, 16, 128], BF, st)] * 2
        sgb = [sb("sgb%d" % i, [128, 512], BF, st) for i in range(2)]
        mal = [sb("mal%d" % i, [128, D], BF, st) for i in range(2)]
        mpf = [sb("mpf0", [128, D], F32, st)] * 2
        mp = [sb("mp0", [128, D], BF, st)] * 2
        mpT = [sb("mpT%d" % i, [128, 8, 128], BF, st) for i in range(2)]
        xl = [sb("xl%d" % i, [128, D], F32, st) for i in range(2)]
        t1 = [sb("t10", [128, D], F32, st)] * 2
        x1t = xl
        jk = sb("jk", [128, 512], BF, st)
        so = [sb("so%d" % i, [128, 8], F32, st) for i in range(2)]
        xb2 = [sb("xb20", [128, D], BF, st)] * 2
        st2 = [sb("st2%d" % i, [128, 4], F32, st) for i in range(2)]
        def stageA(i):
            b = i % 2
            rs_ = slice(i * 128, (i + 1) * 128)
            dma("sp", yzl[b][:], yz_d[rs_, :])
            dma("sp", mal[b][:], mixa_d[rs_, :])
            for half in range(2):
                ps = bank(); pb = bfv(ps)
                for k8 in range(8):
                    kc = half * 8 + k8
                    tr(pb[:, k8 * 128:(k8 + 1) * 128], yzl[b][:, kc * 128:(kc + 1) * 128], identB)
                for k8 in range(8):
                    kc = half * 8 + k8
                    ts("dve", yT[b][:, kc, :], pb[:, k8 * 128:(k8 + 1) * 128], sncol[:, kc:kc + 1], None, ALU.mult)
            for cb in range(2):
                cs = slice(cb * 512, (cb + 1) * 512)
                pg = bank()
                for kc in range(8):
                    mm(pg[:, :], hT[:, kc, rs_], wgb[:, kc, cs], start=(kc == 0), stop=(kc == 7))
                act(sgb[cb][:], pg[:, :], AF.Sigmoid)
                py = bank()
                for kc in range(16):
                    mm(py[:, :], yT[b][:, kc, :], wos[:, kc, cs], start=(kc == 0), stop=(kc == 15))
                stt(mpf[b][:, cs], py[:, :], rstds[:, i:i + 1], sgb[cb][:], ALU.mult, ALU.mult)
            tt("dve", mp[b][:], mpf[b][:], mal[b][:], ALU.add)
            ps = bank(); pb = bfv(ps)
            for kc in range(8):
                tr(pb[:, kc * 128:(kc + 1) * 128], mp[b][:, kc * 128:(kc + 1) * 128], identB)
            cp("act", mpT[b][:].rearrange("p k t -> p (k t)"), pb[:, :])

        def stageB(i):
            b = i % 2
            rs_ = slice(i * 128, (i + 1) * 128)
            dma("sp", xl[b][:], x[rs_, :])
            pms = []
            for cb in range(2):
                cs = slice(cb * 512, (cb + 1) * 512)
                pm_ = bank(); pms.append(pm_)
                for kc in range(8):
                    mm(pm_[:, :], mpT[b][:, kc, :], wout[:, kc, cs], start=(kc == 0), stop=(kc == 7))
                act(jk[:], pm_[:, :], AF.Square, accum_out=so[b][:, cb:cb + 1])
            tt("dve", so[b][:, 2:3], so[b][:, 0:1], so[b][:, 1:2], ALU.add)
            ts("dve", so[b][:, 3:4], so[b][:, 2:3], 1.0 / D, EPS, ALU.mult, ALU.add)
            act(so[b][:, 4:5], so[b][:, 3:4], AF.Sqrt)
            I("dve", "reciprocal", out=so[b][:, 5:6], in_=so[b][:, 4:5])
            for cb in range(2):
                cs = slice(cb * 512, (cb + 1) * 512)
                stt(t1[b][:, cs], pms[cb][:, :], so[b][:, 5:6], gpmB[:, cs], ALU.mult, ALU.mult)
            tt("dve", x1t[b][:], t1[b][:], xl[b][:], ALU.add)
            dma("sp", x1_d[rs_, :], x1t[b][:])
            norm_to_T(x1t[b][:], h2T, i, gcol2, scol2, xb2[b], st2[b])

        for i in range(NT + 1):
            if i < NT:
                stageA(i)
            if i >= 1:
                stageB(i - 1)
        S.barrier()
        if stop == 6:
            return nc

    with ExitStack() as st:
        TBF = min(1024, L)
        wfo = sb("wfo", [128, NJ, D], BF, st)
        load_w(wfo[:, 0:11, :], w_ffn_out, 0, D); load_w(wfo[:, 11:22, :], w_ffn_out, 0, D, r0=11 * 128)
        wg = [sb("wg%d" % i, [128, 8, 128], BF, st) for i in range(2)]
        wu = [sb("wu%d" % i, [128, 8, 128], BF, st) for i in range(2)]
        fT = sb("fT", [128, NJ, TBF], BF, st)
        sgt = [sb("sgt%d" % i, [128, 512], BF, st) for i in range(2)]
        x1l = [sb("x1l%d" % i, [128, D], F32, st) for i in range(2)]
        t2 = [sb("t2%d" % i, [128, D], F32, st) for i in range(2)]
        ot = x1l
        jk2 = sb("jk2", [128, 512], BF, st)
        sf = [sb("sf%d" % i, [128, 8], F32, st) for i in range(2)]
        k = 0
        for blk in range(L // TBF):
            for j in range(NJ):
                b = k % 2; k += 1
                load_w(wg[b][:], w_ffn_in, j * 128, (j + 1) * 128)
                load_w(wu[b][:], w_ffn_in, DFF + j * 128, DFF + (j + 1) * 128)
                for sub in range(TBF // TB):
                    tsl = slice(blk * TBF + sub * TB, blk * TBF + (sub + 1) * TB)
                    pg = bank(); pu = bank()
                    for kc in range(8):
                        mm(pg[:, 0:TB], wg[b][:, kc, :], h2T[:, kc, tsl], start=(kc == 0), stop=(kc == 7))
                    for kc in range(8):
                        mm(pu[:, 0:TB], wu[b][:, kc, :], h2T[:, kc, tsl], start=(kc == 0), stop=(kc == 7))
                    act(sgt[sub % 2][:, 0:TB], pg[:, 0:TB], AF.Silu)
                    tt("dve", fT[:, j, sub * TB:(sub + 1) * TB], pu[:, 0:TB], sgt[sub % 2][:, 0:TB], ALU.mult)
            for tl in range(TBF // 128):
                i = blk * (TBF // 128) + tl
                b2 = i % 2
                rs_ = slice(i * 128, (i + 1) * 128)
                dma("sp", x1l[b2][:], x1_d[rs_, :])
                pfs = []
                for cb in range(2):
                    cs = slice(cb * 512, (cb + 1) * 512)
                    pf = bank(); pfs.append(pf)
                    for j in range(NJ):
                        mm(pf[:, :], fT[:, j, tl * 128:(tl + 1) * 128], wfo[:, j, cs], start=(j == 0), stop=(j == NJ - 1))
                    act(jk2[:], pf[:, :], AF.Square, accum_out=sf[b2][:, cb:cb + 1])
                tt("dve", sf[b2][:, 2:3], sf[b2][:, 0:1], sf[b2][:, 1:2], ALU.add)
                ts("dve", sf[b2][:, 3:4], sf[b2][:, 2:3], 1.0 / D, EPS, ALU.mult, ALU.add)
                act(sf[b2][:, 4:5], sf[b2][:, 3:4], AF.Sqrt)
                I("dve", "reciprocal", out=sf[b2][:, 5:6], in_=sf[b2][:, 4:5])
                for cb in range(2):
                    cs = slice(cb * 512, (cb + 1) * 512)
                    stt(t2[b2][:, cs], pfs[cb][:, :], sf[b2][:, 5:6], gpfB[:, cs], ALU.mult, ALU.mult)
                tt("pool", ot[b2][:], t2[b2][:], x1l[b2][:], ALU.add)
                dma("sp", out[rs_, :], ot[b2][:])
        S.barrier()
    return nc


def _consts():
    eye = np.eye(128, dtype=np.float32)
    k = np.arange(128)
    tri = (k[:, None] <= k[None, :]).astype(np.float32)
    strict = (k[:, None] > k[None, :]).astype(np.float32)
    dm = np.where((k[None, :] // 64) > (k[:, None] // 64), NEG, 0.0).astype(np.float32)
    rel = (127 - np.arange(383)).astype(np.int32)
    half = 16; max_exact = 8
    side = np.where(rel > 0, half, 0)
    n = np.abs(rel)
    ratio = np.maximum(n, max_exact).astype(np.float32) / np.float32(max_exact)
    large = max_exact + (np.log(ratio).astype(np.float32) / np.float32(math.log(128 / max_exact))
                         * np.float32(half - max_exact)).astype(np.int32)
    large = np.minimum(large, half - 1)
    bucket = side + np.where(n < max_exact, n, large)
    oh = np.zeros((32, 383), np.float32)
    oh[bucket, np.arange(383)] = 1.0
    return {"cI4": np.tile(eye, (1, 4)), "cJ": eye[::-1].copy(), "cTRI": tri, "cSTR": strict, "cDM": dm, "cOH": oh}


_WNAMES = ["ada_w", "ada_b", "pre_norm_mix", "post_norm_mix", "pre_norm_ffn", "post_norm_ffn", "w_in", "q_norm",
           "kv_norm", "w_uq", "w_uv", "rel_bias", "w_qidx", "kidx_norm", "conv_w", "conv_b", "dt_bias", "a_log",
           "d_skip", "ssd_norm", "w_o_attn", "w_o_ssd", "w_out", "w_ffn_in", "w_ffn_out"]


def make_in_maps(inputs):
    cst = _consts()
    B = inputs["x"].shape[0]
    shared = {k: np.ascontiguousarray(np.asarray(inputs[k], dtype=np.float32)) for k in _WNAMES}
    shared.update(cst)
    maps = []
    for b in range(B):
        m = dict(shared)
        m["x"] = np.ascontiguousarray(np.asarray(inputs["x"][b], dtype=np.float32))
        m["c"] = np.ascontiguousarray(np.asarray(inputs["c"][b], dtype=np.float32))
        maps.append(m)
    return maps


def kernel(**inputs):
    L = inputs["x"].shape[1]
    nc = build(L)
    maps = make_in_maps(inputs)
    res = run_bass_kernel_spmd(nc, maps, core_ids=list(range(len(maps))))
    return np.stack([np.asarray(r["out"], dtype=np.float32) for r in res.results], axis=0)
```
